# Optimizing a Trainium2 kernel written in Bass

```python
import jax, jax.numpy as jnp
from jax import lax
import numpy as np

D_MODEL = 2048
BATCH = 2
SEQ = 8192
DEPTH = 2

CHUNK = 64
Q_BLOCK = 128
N_MEM = 256
D_FF = 5632
CONV_CH = D_MODEL // 2
CONV_WIDTH = 31
MLA_HEADS = 8
QK_NOPE = 128
QK_ROPE = 64
QK_HEAD = QK_NOPE + QK_ROPE
V_HEAD = 128
Q_LORA = 768
KV_LORA = 256
ROPE_THETA = 10000.0
X_HEADS = 4
X_HEAD_DIM = D_MODEL // X_HEADS
IN_COLS = 2 * CONV_CH + Q_LORA + KV_LORA + QK_ROPE
MIX_OUT = CONV_CH + MLA_HEADS * V_HEAD
EPS = 1e-6
NEG_INF = -1e30

kernel_name = "hybrid_conv_mla_macaron_encoder"


def rms_norm(x, g):
    xf = x.astype(jnp.float32)
    y = xf * lax.rsqrt(jnp.mean(xf * xf, axis=-1, keepdims=True) + EPS)
    return (y * g.astype(jnp.float32)).astype(x.dtype)


def layer_norm(x, g, b):
    xf = x.astype(jnp.float32)
    mu = jnp.mean(xf, axis=-1, keepdims=True)
    var = jnp.mean(jnp.square(xf - mu), axis=-1, keepdims=True)
    y = (xf - mu) * lax.rsqrt(var + EPS)
    return (y * g.astype(jnp.float32) + b.astype(jnp.float32)).astype(x.dtype)


def swiglu_ffn(x, norm_g, w_gate, w_up, w_down):
    h = rms_norm(x, norm_g)
    return (jax.nn.silu(h @ w_gate) * (h @ w_up)) @ w_down


def rope_tables(positions):
    inv_freq = 1.0 / (ROPE_THETA ** (jnp.arange(0, QK_ROPE, 2, dtype=jnp.float32) / QK_ROPE))
    ang = positions.astype(jnp.float32)[..., None] * inv_freq
    return jnp.cos(ang)[:, :, None, :], jnp.sin(ang)[:, :, None, :]


def apply_rope(x, cos, sin):
    c = cos.astype(x.dtype)
    s = sin.astype(x.dtype)
    x1, x2 = x[..., : QK_ROPE // 2], x[..., QK_ROPE // 2:]
    return jnp.concatenate([x1 * c - x2 * s, x2 * c + x1 * s], axis=-1)


def conv_module(u, conv_w, conv_b, ln_g, ln_b):
    a = u[..., :CONV_CH] * jax.nn.sigmoid(u[..., CONV_CH:])
    y = lax.conv_general_dilated(
        a, conv_w[:, None, :], window_strides=(1,),
        padding=[(CONV_WIDTH - 1, 0)],
        dimension_numbers=("NWC", "WIO", "NWC"),
        feature_group_count=CONV_CH) + conv_b
    return jax.nn.silu(layer_norm(y, ln_g, ln_b))


def chunk_causal_attention(q, k, v):
    B, S, H, Dq = q.shape
    Dv = v.shape[-1]
    nb = S // Q_BLOCK
    scale = Dq ** -0.5
    key_chunk = jnp.arange(S) // CHUNK
    qb = q.reshape(B, nb, Q_BLOCK, H, Dq).transpose(1, 0, 2, 3, 4)

    def one_block(args):
        qi, i = args
        s = jnp.einsum("bqhd,bkhd->bhqk", qi, k).astype(jnp.float32) * scale
        q_chunk = (i * Q_BLOCK + jnp.arange(Q_BLOCK)) // CHUNK
        mask = key_chunk[None, :] <= q_chunk[:, None]
        s = jnp.where(mask[None, None], s, NEG_INF)
        p = jax.nn.softmax(s, axis=-1).astype(v.dtype)
        return jnp.einsum("bhqk,bkhd->bqhd", p, v)

    out = lax.map(one_block, (qb, jnp.arange(nb)))
    return out.transpose(1, 0, 2, 3, 4).reshape(B, S, H, Dv)


def mla_group(c_q, c_kv, k_pe, cos, sin, q_a_norm, w_q_b, kv_a_norm, w_kv_b, q_norm, k_norm):
    B, S, _ = c_q.shape
    q = (rms_norm(c_q, q_a_norm) @ w_q_b).reshape(B, S, MLA_HEADS, QK_HEAD)
    kv = (rms_norm(c_kv, kv_a_norm) @ w_kv_b).reshape(B, S, MLA_HEADS, QK_NOPE + V_HEAD)
    k_nope, v = kv[..., :QK_NOPE], kv[..., QK_NOPE:]
    k = jnp.concatenate(
        [k_nope, jnp.broadcast_to(k_pe[:, :, None, :], (B, S, MLA_HEADS, QK_ROPE))], axis=-1)
    q = rms_norm(q, q_norm)
    k = rms_norm(k, k_norm)
    q = jnp.concatenate([q[..., :QK_NOPE], apply_rope(q[..., QK_NOPE:], cos, sin)], axis=-1)
    k = jnp.concatenate([k[..., :QK_NOPE], apply_rope(k[..., QK_NOPE:], cos, sin)], axis=-1)
    o = chunk_causal_attention(q, k, v)
    return o.reshape(B, S, MLA_HEADS * V_HEAD)


def memory_cross_attention(x, mem, cross_norm, mem_norm, w_cq, w_ck, w_cv, cq_norm, ck_norm, w_co):
    B, S, D = x.shape
    M = mem.shape[1]
    h = rms_norm(x, cross_norm)
    m = rms_norm(mem, mem_norm)
    q = rms_norm((h @ w_cq).reshape(B, S, X_HEADS, X_HEAD_DIM), cq_norm)
    k = rms_norm((m @ w_ck).reshape(B, M, X_HEADS, X_HEAD_DIM), ck_norm)
    v = (m @ w_cv).reshape(B, M, X_HEADS, X_HEAD_DIM)
    s = jnp.einsum("bqhd,bmhd->bhqm", q, k).astype(jnp.float32) * (X_HEAD_DIM ** -0.5)
    p = jax.nn.softmax(s, axis=-1).astype(v.dtype)
    o = jnp.einsum("bhqm,bmhd->bqhd", p, v).reshape(B, S, D)
    return o @ w_co


def setup_inputs(seed: int = 0) -> dict:
    key = jax.random.key(seed)
    ks = iter(jax.random.split(key, 64))
    L = DEPTH

    def nrm(shape):
        return jax.random.normal(next(ks), shape, jnp.float32)

    def w(shape, fan_in):
        return nrm(shape) * (fan_in ** -0.5)

    def gain(shape):
        return 1.0 + 0.02 * nrm(shape)

    def bias(shape):
        return 0.02 * nrm(shape)

    x = nrm((BATCH, SEQ, D_MODEL))
    mem = nrm((BATCH, N_MEM, D_MODEL))
    offs = jax.random.randint(next(ks), (BATCH,), 0, 1024) * CHUNK
    positions = (offs[:, None] + jnp.arange(SEQ)[None, :]).astype(jnp.int32)
    return {
        "x": x,
        "mem": mem,
        "positions": positions,
        "ffn1_norm": gain((L, D_MODEL)),
        "ffn1_w_gate": w((L, D_MODEL, D_FF), D_MODEL),
        "ffn1_w_up": w((L, D_MODEL, D_FF), D_MODEL),
        "ffn1_w_down": w((L, D_FF, D_MODEL), D_FF),
        "mix_norm": gain((L, D_MODEL)),
        "w_in": w((L, D_MODEL, IN_COLS), D_MODEL),
        "conv_w": w((L, CONV_WIDTH, CONV_CH), CONV_WIDTH),
        "conv_b": bias((L, CONV_CH)),
        "conv_ln_g": gain((L, CONV_CH)),
        "conv_ln_b": bias((L, CONV_CH)),
        "q_a_norm": gain((L, Q_LORA)),
        "w_q_b": w((L, Q_LORA, MLA_HEADS * QK_HEAD), Q_LORA),
        "kv_a_norm": gain((L, KV_LORA)),
        "w_kv_b": w((L, KV_LORA, MLA_HEADS * (QK_NOPE + V_HEAD)), KV_LORA),
        "q_norm": gain((L, QK_HEAD)),
        "k_norm": gain((L, QK_HEAD)),
        "w_out": w((L, MIX_OUT, D_MODEL), MIX_OUT),
        "cross_norm": gain((L, D_MODEL)),
        "mem_norm": gain((L, D_MODEL)),
        "w_cq": w((L, D_MODEL, D_MODEL), D_MODEL),
        "w_ck": w((L, D_MODEL, D_MODEL), D_MODEL),
        "w_cv": w((L, D_MODEL, D_MODEL), D_MODEL),
        "cq_norm": gain((L, X_HEAD_DIM)),
        "ck_norm": gain((L, X_HEAD_DIM)),
        "w_co": w((L, D_MODEL, D_MODEL), D_MODEL),
        "ffn2_norm": gain((L, D_MODEL)),
        "ffn2_w_gate": w((L, D_MODEL, D_FF), D_MODEL),
        "ffn2_w_up": w((L, D_MODEL, D_FF), D_MODEL),
        "ffn2_w_down": w((L, D_FF, D_MODEL), D_FF),
    }


def reference(x, mem, positions,
              ffn1_norm, ffn1_w_gate, ffn1_w_up, ffn1_w_down,
              mix_norm, w_in, conv_w, conv_b, conv_ln_g, conv_ln_b,
              q_a_norm, w_q_b, kv_a_norm, w_kv_b, q_norm, k_norm, w_out,
              cross_norm, mem_norm, w_cq, w_ck, w_cv, cq_norm, ck_norm, w_co,
              ffn2_norm, ffn2_w_gate, ffn2_w_up, ffn2_w_down):
    cos, sin = rope_tables(positions)
    o_q = 2 * CONV_CH
    o_kv = o_q + Q_LORA
    o_pe = o_kv + KV_LORA
    for l in range(DEPTH):
        x = x + 0.5 * swiglu_ffn(x, ffn1_norm[l], ffn1_w_gate[l], ffn1_w_up[l], ffn1_w_down[l])

        h = rms_norm(x, mix_norm[l])
        z = h @ w_in[l]
        y_conv = conv_module(z[..., :o_q], conv_w[l], conv_b[l], conv_ln_g[l], conv_ln_b[l])
        y_mla = mla_group(z[..., o_q:o_kv], z[..., o_kv:o_pe], z[..., o_pe:], cos, sin,
                          q_a_norm[l], w_q_b[l], kv_a_norm[l], w_kv_b[l], q_norm[l], k_norm[l])
        x = x + jnp.concatenate([y_conv, y_mla], axis=-1) @ w_out[l]

        x = x + memory_cross_attention(x, mem, cross_norm[l], mem_norm[l], w_cq[l], w_ck[l],
                                       w_cv[l], cq_norm[l], ck_norm[l], w_co[l])

        x = x + 0.5 * swiglu_ffn(x, ffn2_norm[l], ffn2_w_gate[l], ffn2_w_up[l], ffn2_w_down[l])
    return x
```

```python
import contextlib
import numpy as np
import concourse.bass as bass
import concourse.mybir as mybir
from concourse.bass_utils import run_bass_kernel_spmd

F32 = mybir.dt.float32
BF16 = mybir.dt.bfloat16
I32 = mybir.dt.int32
AF = mybir.ActivationFunctionType
ALU = mybir.AluOpType

COMPUTE = ("pe", "act", "dve", "pool")
QUEUES = ("sp", "act", "pool")
NDSEM = 6


class _Op:
    __slots__ = ("eng", "fn", "deps", "ddeps", "dma", "cc", "q_idx", "idx", "milestone", "count")

    def __init__(self, eng, fn, dma, cc):
        self.eng = eng
        self.fn = fn
        self.dma = dma
        self.cc = cc
        self.deps = {}
        self.ddeps = {}
        self.milestone = False
        self.count = 0
        self.q_idx = -1


class Prog:
    def __init__(self, nc):
        self.nc = nc
        self.ops = {e: [] for e in ("pe", "act", "dve", "pool", "sp")}
        self.track = {}
        self.ndma = {q: 0 for q in QUEUES}
        self.ncc = 0

    def _segs(self, space, lo, hi):
        segs = self.track.setdefault(space, [[0, 1 << 60, None, []]])
        out = []
        i = 0
        while i < len(segs):
            s = segs[i]
            if s[1] <= lo:
                i += 1
                continue
            if s[0] >= hi:
                break
            if s[0] < lo:
                segs.insert(i, [s[0], lo, s[2], list(s[3])])
                s[0] = lo
                i += 1
                continue
            if s[1] > hi:
                segs.insert(i + 1, [hi, s[1], s[2], list(s[3])])
                s[1] = hi
            out.append(s)
            i += 1
        return out

    def add(self, eng, fn, reads=(), writes=(), dma=False, cc=False):
        op = _Op(eng, fn, dma, cc)
        lst = self.ops[eng]
        op.idx = len(lst)
        me = (eng, op.idx)
        deps, ddeps, ops = op.deps, op.ddeps, self.ops

        def dep(w):
            if w is None or w == me:
                return
            e, i = w
            t = ops[e][i]
            if t.dma:
                k = (e, t.q_idx % NDSEM)
                if ddeps.get(k, -1) < i:
                    ddeps[k] = i
            elif t.cc:
                if ddeps.get("cc", -1) < i:
                    ddeps["cc"] = i
            elif deps.get(e, -1) < i:
                deps[e] = i

        for (space, lo, hi) in reads:
            for s in self._segs(space, lo, hi):
                dep(s[2])
                s[3].append(me)
        for (space, lo, hi) in writes:
            for s in self._segs(space, lo, hi):
                dep(s[2])
                for r in s[3]:
                    dep(r)
                s[2] = me
                s[3] = []
        if dma:
            op.q_idx = self.ndma[eng]
            self.ndma[eng] += 1
        if cc:
            op.q_idx = self.ncc
            self.ncc += 1
        lst.append(op)
        return op

    def emit(self):
        nc = self.nc
        ops = self.ops
        for e, lst in ops.items():
            for op in lst:
                if e in op.deps and not (op.dma or op.cc):
                    i = op.deps[e]
                    if e == "pe":
                        del op.deps[e]
                    elif op.idx - i > 2:
                        del op.deps[e]
                for de, di in op.deps.items():
                    ops[de][di].milestone = True
        for e, lst in ops.items():
            c = 0
            for op in lst:
                if op.milestone:
                    c += 1
                op.count = c
        with contextlib.ExitStack() as st:
            csem = {e: st.enter_context(nc.semaphore("c_" + e)) for e in COMPUTE}
            dsem = {q: [st.enter_context(nc.semaphore(f"d_{q}{k}")) for k in range(NDSEM)]
                    for q in QUEUES}
            ccsem = st.enter_context(nc.semaphore("ccsem"))
            block = st.enter_context(nc.Block())

            def run(e, h):
                waited = {}

                def wait(sem, key, val):
                    if waited.get(key, 0) >= val:
                        return
                    waited[key] = val
                    h.wait_ge(sem, val)

                for op in ops[e]:
                    for de, di in op.deps.items():
                        wait(csem[de], de, ops[de][di].count)
                    for key, di in op.ddeps.items():
                        if key == "cc":
                            wait(ccsem, "cc", ops["pool"][di].q_idx + 1)
                        else:
                            q, k = key
                            wait(dsem[q][k], key, 16 * (ops[q][di].q_idx // NDSEM + 1))
                    if op.dma:
                        k = op.q_idx % NDSEM
                        if op.q_idx >= NDSEM:
                            wait(dsem[e][k], (e, k), 16 * (op.q_idx // NDSEM))
                        op.fn(h).then_inc(dsem[e][k], 16)
                    elif op.cc:
                        op.fn(h).then_inc(ccsem, 1)
                    else:
                        ins = op.fn(h)
                        if op.milestone:
                            ins.then_inc(csem[e], 1)
                if e in QUEUES:
                    n = self.ndma[e]
                    for k in range(min(NDSEM, n)):
                        cnt = (n - 1 - k) // NDSEM + 1
                        wait(dsem[e][k], (e, k), 16 * cnt)
                if e == "pool" and self.ncc:
                    wait(ccsem, "cc", self.ncc)

            @block.tensor
            def _(h):
                run("pe", h)

            @block.scalar
            def _(h):
                run("act", h)

            @block.vector
            def _(h):
                run("dve", h)

            @block.gpsimd
            def _(h):
                run("pool", h)

            @block.sync
            def _(h):
                run("sp", h)


class View:
    def __init__(self, arena, name, off_bytes, n, w, dtype):
        self.esz = 4 if dtype in (F32, I32) else 2
        assert off_bytes % 4 == 0
        self.space = name
        self.base = off_bytes // 2
        self.n, self.w, self.dtype = n, w, dtype
        u = n * w * self.esz // 2
        v = arena[:, self.base:self.base + u]
        if dtype != BF16:
            v = v.bitcast(dtype)
        self.t = v.rearrange("p (n w) -> p n w", n=n)
        self.nbytes = n * w * self.esz

    def r(self, i, lo=0, hi=None, n=1):
        hi = self.w if hi is None else hi
        u = self.esz // 2 if self.esz > 1 else 1
        if n == 1:
            return (self.space, self.base + (i * self.w + lo) * self.esz // 2,
                    self.base + (i * self.w + hi) * self.esz // 2)
        return (self.space, self.base + i * self.w * self.esz // 2,
                self.base + (i + n) * self.w * self.esz // 2)

    def ap(self, i, lo=0, hi=None, p0=0, p1=128):
        hi = self.w if hi is None else hi
        return self.t[p0:p1, i, lo:hi]

    def ap3(self, i, n, p0=0, p1=128):
        return self.t[p0:p1, i:i + n, :]


class PsumBank:
    def __init__(self, st, nc, name):
        self.name = name
        self.t = st.enter_context(nc.psum_tensor(name, [128, 512], F32))

    def r(self, lo=0, hi=512):
        return (self.name, lo, hi)

    def ap(self, lo=0, hi=512, p0=0, p1=128):
        return self.t[p0:p1, lo:hi]


class DT:
    def __init__(self, ap, name):
        self.a = ap
        self.name = name

    def r(self, c, lo, hi, n=1):
        return [(f"{self.name}:{c + j}", lo, hi) for j in range(n)]


class Cfg:
    def __init__(self, **kw):
        self.D = 2048
        self.F = 5632
        self.NTOK = 2048
        self.TB = 1024
        self.EPS = 1e-6
        self.NH = 8
        self.QL = 768
        self.KVL = 256
        self.XH = 4
        self.NMEM = 256
        self.DEPTH = 2
        self.__dict__.update(kw)
        self.KC = self.D // 128
        self.FC = self.F // 128
        self.CCH = self.D // 2
        self.CC = self.CCH // 128
        self.QC = self.QL // 128
        self.KVC = self.KVL // 128
        self.XHD = self.D // self.XH
        self.XHC = self.XHD // 128
        self.NQB = self.NTOK // 512
        self.NKB = self.NTOK // 128
        self.R_K = self.NH * 192
        self.R_V = self.NH * 128
        self.R_T = self.CCH * 32 // self.NTOK
        self.R = self.R_K + self.R_V + self.R_T


SB_BYTES = 206 * 1024


class Builder:
    def __init__(self, nc, cfg):
        self.nc = nc
        self.cfg = cfg
        self.st = contextlib.ExitStack()
        self.P = Prog(nc)
        self.arena = self.st.enter_context(nc.sbuf_tensor("arena", [128, SB_BYTES // 2], BF16))
        self.banks = [PsumBank(self.st, nc, f"ps{i}") for i in range(8)]
        self.bank_rr = {}
        self.ring_off = 0
        self.dram = {}

    def view(self, off, n, w, dtype):
        return View(self.arena, "arena", off, n, w, dtype)

    def bank(self, grp):
        lo, n = grp
        k = self.bank_rr.get(grp, 0)
        self.bank_rr[grp] = k + 1
        return self.banks[lo + k % n]

    def ring_alloc(self, nbytes):
        r0, r1 = self.ring
        if self.ring_off + nbytes > r1 - r0:
            self.ring_off = 0
        off = r0 + self.ring_off
        self.ring_off += (nbytes + 31) // 32 * 32
        return off

    def load_w(self, src_ap, n, w, reads=()):
        off = self.ring_alloc(n * w * 2)
        v = self.view(off, n, w, BF16)
        self.P.add("pool", lambda h: h.dma_start(out=v.t, in_=src_ap.rearrange("p (n w) -> p n w", n=n),
                                                 max_dma_last_dim=8192),
                   reads=list(reads), writes=[v.r(0, n=n)], dma=True)
        return v

    def mm(self, bank, lo, hi, lhsT, rhs, start, stop, reads, m=128):
        self.P.add("pe", lambda h: h.matmul(bank.ap(lo, hi, 0, m), lhsT, rhs, start=start, stop=stop),
                   reads=reads, writes=[bank.r(lo, hi)])

    def setup_consts(self, off, vecs_ap, nv):
        self.ones = self.view(off, 1, 128, BF16)
        off += 256
        self.epsc = self.view(off, 1, 8, F32)
        off += 32
        self.vecs = self.view(off, 1, nv, F32)
        off += nv * 4
        self.memset("dve", self.ones.ap(0), 1.0, [self.ones.r(0)])
        self.memset("dve", self.epsc.ap(0), self.cfg.EPS, [self.epsc.r(0)])
        self.dma("sp", self.vecs.ap(0), vecs_ap, [], [self.vecs.r(0)])
        return off

    def vcol(self, c, p0=0, p1=128):
        return self.vecs.ap(0, c, c + 1, p0, p1)

    def dma(self, q, out, in_, reads, writes, **kw):
        return self.P.add(q, lambda h: h.dma_start(out=out, in_=in_, **kw), reads, writes, dma=True)

    def act(self, out, in_, func, reads, writes, **kw):
        return self.P.add("act", lambda h: h.activation(out=out, in_=in_, func=func, **kw), reads, writes)

    def tt(self, eng, out, in0, in1, op, reads, writes):
        return self.P.add(eng, lambda h: h.tensor_tensor(out=out, in0=in0, in1=in1, op=op), reads, writes)

    def stt(self, out, in0, scalar, in1, op0, op1, reads, writes):
        return self.P.add("dve", lambda h: h.scalar_tensor_tensor(out=out, in0=in0, scalar=scalar, in1=in1,
                                                                  op0=op0, op1=op1), reads, writes)

    def ts(self, eng, out, in0, s1, s2, op0, op1, reads, writes):
        if op1 is None:
            return self.P.add(eng, lambda h: h.tensor_scalar(out=out, in0=in0, scalar1=s1, scalar2=None, op0=op0),
                              reads, writes)
        return self.P.add(eng, lambda h: h.tensor_scalar(out=out, in0=in0, scalar1=s1, scalar2=s2, op0=op0, op1=op1),
                          reads, writes)

    def recip(self, out, in_, reads, writes):
        return self.P.add("dve", lambda h: h.reciprocal(out=out, in_=in_), reads, writes)

    def copy(self, eng, out, in_, reads, writes):
        return self.P.add(eng, lambda h: h.tensor_copy(out=out, in_=in_), reads, writes)

    def memset(self, eng, out, val, writes):
        return self.P.add(eng, lambda h: h.memset(out, val), (), writes)

    def rmsnorm(self, xs, D, gcols, outs, W, tmp, stat_n=None):
        bank = self.bank(self.G_STAT)
        sqv = tmp["sq"]
        n = len(xs) if stat_n is None else stat_n
        for k, (xap, xr, p) in enumerate(xs[:n]):
            s = tmp["sq_i"] % sqv.n
            tmp["sq_i"] += 1
            self.act(sqv.ap(s, 0, W, 0, p), xap, AF.Square, [xr], [sqv.r(s, 0, W)])
            self.mm(bank, 0, W, self.ones.ap(0, 0, 128, 0, p), sqv.ap(s, 0, W, 0, p), k == 0, k == n - 1,
                    [self.ones.r(0), sqv.r(s, 0, W)])
        rs = tmp["rs"]
        j = tmp["rs_i"] % rs.n
        tmp["rs_i"] += 1
        self.act(rs.ap(j, 0, W), bank.ap(0, W), AF.Sqrt, [bank.r(0, W), self.epsc.r(0)], [rs.r(j, 0, W)],
                 bias=self.epsc.ap(0, 0, 1), scale=1.0 / D)
        self.recip(rs.ap(j, 0, W), rs.ap(j, 0, W), [rs.r(j, 0, W)], [rs.r(j, 0, W)])
        for k, (xap, xr, p) in enumerate(xs):
            oap, orr = outs[k]
            if gcols is None:
                self.tt("dve", oap, xap, rs.ap(j, 0, W, 0, p), ALU.mult, [xr, rs.r(j, 0, W)], [orr])
            else:
                self.stt(oap, xap, self.vcol(gcols[k], 0, p), rs.ap(j, 0, W, 0, p), ALU.mult, ALU.mult,
                         [xr, rs.r(j, 0, W), self.vecs.r(0)], [orr])
        return rs, j

    def ffn(self, src, dst, wgu, wd, gcol0):
        c = self.cfg
        KC, FC, TB = c.KC, c.FC, c.TB
        NS = TB // 512
        off = self.work0
        H = self.view(off, KC, TB, BF16); off += H.nbytes
        A = self.view(off, FC, TB, BF16); off += A.nbytes
        X32 = self.view(off, KC, 512, F32); off += X32.nbytes
        tmp = {"sq": self.view(off, 4, 512, BF16), "sq_i": 0, "rs_i": 0}; off += tmp["sq"].nbytes
        tmp["rs"] = self.view(off, 2, 512, F32); off += tmp["rs"].nbytes
        SG = self.view(off, 2, 512, F32); off += SG.nbytes
        XO = self.view(off, 2, TB, F32); off += XO.nbytes
        self.ring = (off, SB_BYTES)
        self.ring_off = 0
        assert SB_BYTES - off >= 24 * 1024, (off, SB_BYTES)
        for tb in range(c.NTOK // TB):
            t0 = tb * TB
            for sb in range(NS):
                c0 = t0 + sb * 512
                self.dma("sp", X32.t, src.a[:, :, c0:c0 + 512].rearrange("k p t -> p k t"),
                         src.r(0, c0, c0 + 512, n=KC), [X32.r(0, n=KC)])
                xs = [(X32.ap(k), X32.r(k), 128) for k in range(KC)]
                outs = [(H.ap(k, sb * 512, sb * 512 + 512), H.r(k, sb * 512, sb * 512 + 512)) for k in range(KC)]
                self.rmsnorm(xs, c.D, [gcol0 + k for k in range(KC)], outs, 512, tmp)
            for f in range(FC):
                w = self.load_w(wgu[f], 2 * KC, 128)
                for sb in range(NS):
                    s0, s1 = sb * 512, sb * 512 + 512
                    pg = self.bank(self.G_A)
                    pu = self.bank(self.G_B)
                    for k in range(KC):
                        self.mm(pg, 0, 512, w.ap(k), H.ap(k, s0, s1), k == 0, k == KC - 1, [w.r(k), H.r(k, s0, s1)])
                    for k in range(KC):
                        self.mm(pu, 0, 512, w.ap(KC + k), H.ap(k, s0, s1), k == 0, k == KC - 1, [w.r(KC + k), H.r(k, s0, s1)])
                    j = (f * NS + sb) % 2
                    self.act(SG.ap(j), pg.ap(), AF.Silu, [pg.r()], [SG.r(j)])
                    self.tt("dve", A.ap(f, s0, s1), SG.ap(j), pu.ap(), ALU.mult, [SG.r(j), pu.r()], [A.r(f, s0, s1)])
            for dc in range(KC):
                w = self.load_w(wd[dc], FC, 128)
                j = dc % 2
                self.dma("sp", XO.ap(j), src.a[dc, :, t0:t0 + TB], src.r(dc, t0, t0 + TB), [XO.r(j)])
                for sb in range(NS):
                    s0, s1 = sb * 512, sb * 512 + 512
                    pd = self.bank(self.G_C)
                    for f in range(FC):
                        self.mm(pd, 0, 512, w.ap(f), A.ap(f, s0, s1), f == 0, f == FC - 1, [w.r(f), A.r(f, s0, s1)])
                    self.stt(XO.ap(j, s0, s1), pd.ap(), 0.5, XO.ap(j, s0, s1), ALU.mult, ALU.add,
                             [pd.r(), XO.r(j, s0, s1)], [XO.r(j, s0, s1)])
                self.dma("sp", dst.a[dc, :, t0:t0 + TB], XO.ap(j), [XO.r(j)], dst.r(dc, t0, t0 + TB))

    G_A = (0, 2)
    G_B = (2, 2)
    G_C = (4, 2)
    G_STAT = (6, 2)
    rope_eng = "pool"


def _rope_perm():
    return np.concatenate([np.arange(32, 64), np.arange(0, 32)])


def weight_units(c):
    ar = np.arange
    for pre in ("ffn1", "ffn2"):
        for f in range(c.FC):
            cols = f * 128 + ar(128)
            yield (pre + "_gu", f), [(pre + "_w_gate", cols), (pre + "_w_up", cols)]
        for dc in range(c.KC):
            yield (pre + "_d", dc), [(pre + "_w_down", dc * 128 + ar(128))]
    for j in range(c.CC):
        yield ("in_conv", j), [("w_in", j * 128 + ar(128)), ("w_in", c.CCH + j * 128 + ar(128))]
    o_q = 2 * c.CCH
    o_kv = o_q + c.QL
    o_pe = o_kv + c.KVL
    for i in range(c.QC):
        yield ("in_cq", i), [("w_in", o_q + i * 128 + ar(128))]
    for i in range(c.KVC):
        yield ("in_ckv", i), [("w_in", o_kv + i * 128 + ar(128))]
    yield ("in_kpe", 0), [("w_in", o_pe + ar(64))]
    yield ("in_kpe", 1), [("w_in", o_pe + _rope_perm())]
    for h in range(c.NH):
        yield ("qb", h), [("w_q_b", np.concatenate([h * 192 + ar(128), h * 192 + 128 + ar(64),
                                                     h * 192 + 128 + _rope_perm()]))]
        yield ("kvk", h), [("w_kv_b", h * 256 + ar(128))]
    yield ("kvv", 0), [("w_kv_b", np.concatenate([h * 256 + 128 + ar(128) for h in range(c.NH)]))]
    for dc in range(c.KC):
        cols = dc * 128 + ar(128)
        yield ("wout", dc), [("w_out", cols)]
        yield ("wcq", dc), [("w_cq", cols)]
        yield ("wck", dc), [("w_ck", cols)]
        yield ("wco", dc), [("w_co", cols)]
    for cb in range(c.D // 512):
        yield ("wcv", cb), [("w_cv", cb * 512 + ar(512))]


_KDIM = {"ffn1_w_gate": "D", "ffn1_w_up": "D", "ffn1_w_down": "F", "ffn2_w_gate": "D", "ffn2_w_up": "D",
         "ffn2_w_down": "F", "w_in": "D", "w_q_b": "QL", "w_kv_b": "KVL", "w_out": "D", "w_cq": "D",
         "w_ck": "D", "w_cv": "D", "w_co": "D"}


def weight_plan(c):
    plan = {}
    off = 0
    for l in range(c.DEPTH):
        for key, parts in weight_units(c):
            ln = sum(getattr(c, _KDIM[w]) // 128 * len(cols) for w, cols in parts)
            plan[(l,) + key] = (off, ln)
            off += 128 * ln
    return plan, off


def pack_weights(c, inputs):
    plan, total = weight_plan(c)
    flat = np.empty(total, np.float32)
    for l in range(c.DEPTH):
        for key, parts in weight_units(c):
            off, ln = plan[(l,) + key]
            blocks = []
            for w, cols in parts:
                W = np.asarray(inputs[w][l])
                K = W.shape[0]
                blocks.append(W[:, cols].reshape(K // 128, 128, len(cols)).transpose(1, 0, 2).reshape(128, -1))
            flat[off:off + 128 * ln] = np.concatenate(blocks, axis=1).reshape(-1)
    return flat


def vec_plan(c):
    cols = {}
    n = 0

    def add(name, k):
        nonlocal n
        cols[name] = n
        n += k

    for l in range(c.DEPTH):
        for nm in ("ffn1_norm", "mix_norm", "cross_norm", "mem_norm", "ffn2_norm"):
            add((l, nm), c.KC)
        add((l, "conv_w"), c.CC * 31)
        for nm in ("conv_b", "conv_ln_g", "conv_ln_b"):
            add((l, nm), c.CC)
        add((l, "q_a_norm"), c.QC)
        add((l, "kv_a_norm"), c.KVC)
        add((l, "q_norm"), 3)
        add((l, "k_norm"), 3)
        add((l, "cq_norm"), c.XHC)
        add((l, "ck_norm"), c.XHC)
    add("invf", 1)
    add("sgn", 1)
    add("sel", 4)
    add("visb", 3)
    return cols, n


def pack_vecs(c, inputs, rank):
    cols, n = vec_plan(c)
    V = np.zeros((128, n), np.float32)

    def put(name, arr2d):
        c0 = cols[name]
        for i, row in enumerate(arr2d):
            V[:len(row), c0 + i] = row

    for l in range(c.DEPTH):
        for nm in ("ffn1_norm", "mix_norm", "cross_norm", "mem_norm", "ffn2_norm", "conv_b", "conv_ln_g",
                   "conv_ln_b", "q_a_norm", "kv_a_norm", "cq_norm", "ck_norm"):
            put((l, nm), np.asarray(inputs[nm][l]).reshape(-1, 128))
        cw = np.asarray(inputs["conv_w"][l])
        put((l, "conv_w"), cw.reshape(31, c.CC, 128).transpose(1, 0, 2).reshape(c.CC * 31, 128))
        for nm in ("q_norm", "k_norm"):
            g = np.asarray(inputs[nm][l])
            put((l, nm), [g[:128], g[128:192], g[128:192][_rope_perm()]])
    inv = (1.0 / (np.float32(10000.0) ** (np.arange(0, 64, 2, dtype=np.float32) / np.float32(64)))).astype(np.float32)
    put("invf", [np.concatenate([inv, inv])])
    put("sgn", [np.concatenate([-np.ones(32, np.float32), np.ones(32, np.float32)])])
    sel = np.zeros((4, 128), np.float32)
    if rank > 0:
        sel[rank - 1] = 1.0
    put("sel", sel)
    vb = np.zeros((3, 128), np.float32)
    for r in range(3):
        if r >= rank:
            vb[r] = -30000.0
    put("visb", vb)
    return V


class Full(Builder):
    def __init__(self, nc, cfg):
        super().__init__(nc, cfg)
        c = cfg
        self.wplan, wtotal = weight_plan(c)
        self.vcols, nv = vec_plan(c)
        self.nv = nv
        dt = nc.dram_tensor
        self.xT = DT(dt("xT", [c.KC, 128, c.NTOK], F32, kind="ExternalInput").ap(), "xT")
        self.memT = dt("memT", [c.KC, 128, c.NMEM], F32, kind="ExternalInput").ap()
        self.pos = dt("pos", [1, c.NTOK], I32, kind="ExternalInput").ap()
        self.wflat = dt("wflat", [wtotal], F32, kind="ExternalInput").ap()
        self.vecs_in = dt("vecs", [128, nv], F32, kind="ExternalInput").ap()
        self.ident_in = dt("ident", [128, 128], F32, kind="ExternalInput").ap()
        self.outT = DT(dt("outT", [c.KC, 128, c.NTOK], F32, kind="ExternalOutput").ap(), "outT")
        self.XT = DT(dt("XT", [c.KC, 128, c.NTOK], F32).ap(), "XT")
        self.AT = dt("AT", [c.CC, 128, 32 + c.NTOK], BF16).ap()
        self.QT = dt("QT", [c.NH, 2, 128, c.NTOK], BF16).ap()
        self.rk = [192 + (c.R_T if h == 0 else 0) for h in range(c.NH)]
        self.GK = [dt(f"GK{h}", [self.rk[h], c.NTOK], BF16).ap() for h in range(c.NH)]
        self.GOK = [dt(f"GOK{h}", [4 * self.rk[h], c.NTOK], BF16).ap() for h in range(c.NH)]
        self.GV = [dt(f"GV{p}", [256, c.NTOK], BF16).ap() for p in range(c.NH // 2)]
        self.GOV = [dt(f"GOV{p}", [4 * 256, c.NTOK], BF16).ap() for p in range(c.NH // 2)]

    def W(self, l, *key):
        off, ln = self.wplan[(l,) + key]
        return self.wflat[off:off + 128 * ln].rearrange("(p l) -> p l", p=128)

    def vc(self, name, k=0):
        return self.vcols[name] + k

    def gk(self, src, h, part):
        t = self.GK[h] if src is None else self.GOK[h]
        r = (0 if src is None else src * self.rk[h]) + (0 if part == 0 else 128)
        return t[r:r + (128 if part == 0 else 64), :]

    def gv(self, src, h):
        t = self.GV[h // 2] if src is None else self.GOV[h // 2]
        r = (0 if src is None else src * 256) + (h % 2) * 128
        return t[r:r + 128, :].rearrange("r (a d) -> (r a) d", d=128)

    def gt(self, src):
        c = self.cfg
        t = self.GK[0] if src is None else self.GOK[0]
        r = (0 if src is None else src * self.rk[0]) + 192
        return t[r:r + c.R_T, :].rearrange("r (a t) -> (r a) t", t=32)

    def setup(self):
        c = self.cfg
        off = self.setup_consts(0, self.vecs_in, self.nv)
        off = (off + 31) // 32 * 32
        self.ident = self.view(off, 1, 128, BF16); off += 256
        self.dma("pool", self.ident.ap(0), self.ident_in, [], [self.ident.r(0)])
        self.COS = self.view(off, 1, c.NTOK, F32); off += self.COS.nbytes
        self.SINS = self.view(off, 1, c.NTOK, F32); off += self.SINS.nbytes
        self.work0 = off
        self.rope_tables()

    def rope_tables(self):
        c = self.cfg
        N = c.NTOK
        off = self.work0
        PI = self.view(off, 1, N, I32); off += PI.nbytes
        ANG = self.view(off, 1, N, F32); off += ANG.nbytes
        T1 = self.view(off, 1, N, F32); off += T1.nbytes
        KI = self.view(off, 1, N, I32); off += KI.nbytes
        KF = self.view(off, 1, N, F32); off += KF.nbytes
        R = self.view(off, 1, N, F32); off += R.nbytes
        M = self.view(off, 1, N, F32); off += M.nbytes
        p = 64
        a = lambda v: v.ap(0, 0, N, 0, p)
        self.dma("sp", a(PI), self.pos.partition_broadcast(p), [], [PI.r(0)])
        self.copy("dve", a(ANG), a(PI), [PI.r(0)], [ANG.r(0)])
        self.ts("dve", a(ANG), a(ANG), self.vcol(self.vc("invf"), 0, p), None, ALU.mult, None,
                [ANG.r(0), self.vecs.r(0)], [ANG.r(0)])
        TWO_PI = 2.0 * np.pi
        C1 = 6.28125
        C2 = TWO_PI - C1
        for which, dst in (("sin", self.SINS), ("cos", self.COS)):
            shift = 0.0 if which == "sin" else np.pi / 2
            self.ts("dve", a(T1), a(ANG), 1.0 / TWO_PI, 0.5 + shift / TWO_PI, ALU.mult, ALU.add, [ANG.r(0)], [T1.r(0)])
            self.copy("dve", a(KI), a(T1), [T1.r(0)], [KI.r(0)])
            self.copy("dve", a(KF), a(KI), [KI.r(0)], [KF.r(0)])
            self.stt(a(R), a(KF), -C1, a(ANG), ALU.mult, ALU.add, [KF.r(0), ANG.r(0)], [R.r(0)])
            self.stt(a(R), a(KF), -C2, a(R), ALU.mult, ALU.add, [KF.r(0), R.r(0)], [R.r(0)])
            if shift:
                self.ts("dve", a(R), a(R), float(shift), None, ALU.add, None, [R.r(0)], [R.r(0)])
            self.ts("dve", a(M), a(R), float(-np.pi), None, ALU.is_lt, None, [R.r(0)], [M.r(0)])
            self.stt(a(R), a(M), float(TWO_PI), a(R), ALU.mult, ALU.add, [M.r(0), R.r(0)], [R.r(0)])
            self.ts("dve", a(M), a(R), float(np.pi), None, ALU.is_gt, None, [R.r(0)], [M.r(0)])
            self.stt(a(R), a(M), float(-TWO_PI), a(R), ALU.mult, ALU.add, [M.r(0), R.r(0)], [R.r(0)])
            self.ts("dve", a(R), a(R), float(-3.1415925), float(3.1415925), ALU.max, ALU.min, [R.r(0)], [R.r(0)])
            self.act(a(dst), a(R), AF.Sin, [R.r(0)], [dst.r(0)])
        self.ts("dve", a(self.SINS), a(self.SINS), self.vcol(self.vc("sgn"), 0, p), None, ALU.mult, None,
                [self.SINS.r(0), self.vecs.r(0)], [self.SINS.r(0)])

    def norm_phase(self, src, t0, TB, H, X32, gcol0, tmp, WN=256):
        c = self.cfg
        for s in range(TB // WN):
            c0 = t0 + s * WN
            self.dma("sp", X32.t[:, :, 0:WN], src.a[:, :, c0:c0 + WN].rearrange("k p t -> p k t"),
                     src.r(0, c0, c0 + WN, n=c.KC), [X32.r(0, n=c.KC)])
            xs = [(X32.ap(k, 0, WN), X32.r(k, 0, WN), 128) for k in range(c.KC)]
            outs = [(H.ap(k, s * WN, s * WN + WN), H.r(k, s * WN, s * WN + WN)) for k in range(c.KC)]
            self.rmsnorm(xs, c.D, [gcol0 + k for k in range(c.KC)], outs, WN, tmp)

    def mktmp(self, off):
        tmp = {"sq": self.view(off, 4, 512, BF16), "sq_i": 0, "rs_i": 0}
        off += tmp["sq"].nbytes
        tmp["rs"] = self.view(off, 2, 512, F32)
        off += tmp["rs"].nbytes
        return tmp, off

    def proj_residual(self, src, dst, wkeys, ACTV, nk, t0, TB, XO, alpha):
        c = self.cfg
        for dc in range(c.KC):
            w = self.load_w(wkeys(dc), nk, 128)
            j = dc % 2
            self.dma("sp", XO.ap(j, 0, TB), src.a[dc, :, t0:t0 + TB], src.r(dc, t0, t0 + TB), [XO.r(j, 0, TB)])
            for sb in range(TB // 512):
                s0, s1 = sb * 512, sb * 512 + 512
                pd = self.bank(self.G_C)
                for f in range(nk):
                    self.mm(pd, 0, 512, w.ap(f), ACTV.ap(f, s0, s1), f == 0, f == nk - 1, [w.r(f), ACTV.r(f, s0, s1)])
                self.stt(XO.ap(j, s0, s1), pd.ap(), alpha, XO.ap(j, s0, s1), ALU.mult, ALU.add,
                         [pd.r(), XO.r(j, s0, s1)], [XO.r(j, s0, s1)])
            self.dma("sp", dst.a[dc, :, t0:t0 + TB], XO.ap(j, 0, TB), [XO.r(j, 0, TB)], dst.r(dc, t0, t0 + TB))

    def ffn(self, l, pre, src, dst):
        c = self.cfg
        KC, FC, TB = c.KC, c.FC, c.TB
        NS = TB // 512
        off = self.work0
        H = self.view(off, KC, TB, BF16); off += H.nbytes
        A = self.view(off, FC, TB, BF16); off += A.nbytes
        X32 = self.view(off, KC, 256, F32); off += X32.nbytes
        tmp, off = self.mktmp(off)
        SG = self.view(off, 2, 512, F32); off += SG.nbytes
        XO = self.view(off, 2, TB, F32); off += XO.nbytes
        self.ring = (off, SB_BYTES)
        self.ring_off = 0
        assert SB_BYTES - off >= 24 * 1024, (off, SB_BYTES)
        g0 = self.vc((l, pre + "_norm"))
        for tb in range(c.NTOK // TB):
            t0 = tb * TB
            self.norm_phase(src, t0, TB, H, X32, g0, tmp)
            for f in range(FC):
                w = self.load_w(self.W(l, pre + "_gu", f), 2 * KC, 128)
                for sb in range(NS):
                    s0, s1 = sb * 512, sb * 512 + 512
                    pg = self.bank(self.G_A)
                    pu = self.bank(self.G_B)
                    for k in range(KC):
                        self.mm(pg, 0, 512, w.ap(k), H.ap(k, s0, s1), k == 0, k == KC - 1, [w.r(k), H.r(k, s0, s1)])
                    for k in range(KC):
                        self.mm(pu, 0, 512, w.ap(KC + k), H.ap(k, s0, s1), k == 0, k == KC - 1, [w.r(KC + k), H.r(k, s0, s1)])
                    j = (f * NS + sb) % 2
                    self.act(SG.ap(j), pg.ap(), AF.Silu, [pg.r()], [SG.r(j)])
                    self.tt("dve", A.ap(f, s0, s1), SG.ap(j), pu.ap(), ALU.mult, [SG.r(j), pu.r()], [A.r(f, s0, s1)])
            self.proj_residual(src, dst, lambda dc: self.W(l, pre + "_d", dc), A, FC, t0, TB, XO, 0.5)

    def rope(self, R1, R2, j, out_ap, out_r, tok0):
        p = 64
        r1, r2 = R1.ap(j, 0, 512, 0, p), R2.ap(j, 0, 512, 0, p)
        e = self.rope_eng
        self.tt(e, r1, r1, self.COS.ap(0, tok0, tok0 + 512, 0, p), ALU.mult, [R1.r(j), self.COS.r(0, tok0, tok0 + 512)], [R1.r(j)])
        self.tt(e, r2, r2, self.SINS.ap(0, tok0, tok0 + 512, 0, p), ALU.mult, [R2.r(j), self.SINS.r(0, tok0, tok0 + 512)], [R2.r(j)])
        self.tt(e, out_ap, r1, r2, ALU.add, [R1.r(j), R2.r(j)], [out_r])

    def mixA(self, l, src):
        c = self.cfg
        KC, TB, QC, KVC, NH, CC = c.KC, c.TB, c.QC, c.KVC, c.NH, c.CC
        NS = TB // 512
        off = self.work0
        H = self.view(off, KC, TB, BF16); off += H.nbytes
        xa = off
        X32 = self.view(xa, KC, 256, F32)
        CQ = self.view(xa, QC, TB, F32)
        CKV = self.view(xa + CQ.nbytes, KVC, TB, F32)
        off += max(X32.nbytes, CQ.nbytes + CKV.nbytes)
        CQN = self.view(off, QC, TB, BF16); off += CQN.nbytes
        CKVN = self.view(off, KVC, TB, BF16); off += CKVN.nbytes
        KPE = self.view(off, 2, TB, F32); off += KPE.nbytes
        tmp, off = self.mktmp(off)
        SIG = self.view(off, 2, 512, F32); off += SIG.nbytes
        AST = self.view(off, 2, TB, BF16); off += AST.nbytes
        NST = self.view(off, 2, TB, BF16); off += NST.nbytes
        RST = self.view(off, 2, TB, BF16); off += RST.nbytes
        R1 = self.view(off, 2, 512, F32); off += R1.nbytes
        R2 = self.view(off, 2, 512, F32); off += R2.nbytes
        VS = self.view(off, 2, NH * 128, BF16); off += VS.nbytes
        self.ring = (off, SB_BYTES)
        self.ring_off = 0
        assert SB_BYTES - off >= 24 * 1024, (off, SB_BYTES)
        vregs = [self.GV[p].rearrange("(h r) (a d) -> h (r a) d", h=2, d=128) for p in range(NH // 2)]
        tail = self.gt(None)
        cnt = 0
        for tb in range(c.NTOK // TB):
            t0 = tb * TB
            self.norm_phase(src, t0, TB, H, X32, self.vc((l, "mix_norm")), tmp)
            for (key, n, dstv) in (("in_cq", QC, CQ), ("in_ckv", KVC, CKV)):
                for i in range(n):
                    w = self.load_w(self.W(l, key, i), KC, 128)
                    for sb in range(NS):
                        s0, s1 = sb * 512, sb * 512 + 512
                        pb = self.bank(self.G_A)
                        for k in range(KC):
                            self.mm(pb, 0, 512, w.ap(k), H.ap(k, s0, s1), k == 0, k == KC - 1, [w.r(k), H.r(k, s0, s1)])
                        self.act(dstv.ap(i, s0, s1), pb.ap(), AF.Copy, [pb.r()], [dstv.r(i, s0, s1)])
            for i in range(2):
                w = self.load_w(self.W(l, "in_kpe", i), KC, 64)
                for sb in range(NS):
                    s0, s1 = sb * 512, sb * 512 + 512
                    pb = self.bank(self.G_B)
                    for k in range(KC):
                        self.mm(pb, 0, 512, w.ap(k), H.ap(k, s0, s1), k == 0, k == KC - 1, [w.r(k), H.r(k, s0, s1)], m=64)
                    self.act(KPE.ap(i, s0, s1, 0, 64), pb.ap(0, 512, 0, 64), AF.Copy, [pb.r()], [KPE.r(i, s0, s1)])
            for sb in range(NS):
                s0, s1 = sb * 512, sb * 512 + 512
                self.rmsnorm([(CQ.ap(i, s0, s1), CQ.r(i, s0, s1), 128) for i in range(QC)], c.QL,
                             [self.vc((l, "q_a_norm"), i) for i in range(QC)],
                             [(CQN.ap(i, s0, s1), CQN.r(i, s0, s1)) for i in range(QC)], 512, tmp)
                self.rmsnorm([(CKV.ap(i, s0, s1), CKV.r(i, s0, s1), 128) for i in range(KVC)], c.KVL,
                             [self.vc((l, "kv_a_norm"), i) for i in range(KVC)],
                             [(CKVN.ap(i, s0, s1), CKVN.r(i, s0, s1)) for i in range(KVC)], 512, tmp)
            cnt_box = [cnt]
            def conv_pair(j):
                w = self.load_w(self.W(l, "in_conv", j), 2 * KC, 128)
                jj = j % 2
                for sb in range(NS):
                    s0, s1 = sb * 512, sb * 512 + 512
                    pa, pg = self.bank((0, 2)), self.bank((2, 1))
                    for k in range(KC):
                        self.mm(pa, 0, 512, w.ap(k), H.ap(k, s0, s1), k == 0, k == KC - 1, [w.r(k), H.r(k, s0, s1)])
                    for k in range(KC):
                        self.mm(pg, 0, 512, w.ap(KC + k), H.ap(k, s0, s1), k == 0, k == KC - 1, [w.r(KC + k), H.r(k, s0, s1)])
                    js = cnt_box[0] % 2
                    cnt_box[0] += 1
                    self.act(SIG.ap(js), pg.ap(), AF.Sigmoid, [pg.r()], [SIG.r(js)])
                    self.tt("dve", AST.ap(jj, s0, s1), SIG.ap(js), pa.ap(), ALU.mult, [SIG.r(js), pa.r()], [AST.r(jj, s0, s1)])
                self.dma("sp", self.AT[j, :, 32 + t0:32 + t0 + TB], AST.ap(jj), [AST.r(jj)], [(f"AT:{j}", 32 + t0, 32 + t0 + TB)])
                if t0 + TB == c.NTOK:
                    self.dma("sp", tail[j * 128:(j + 1) * 128, :], AST.ap(jj, TB - 32, TB), [AST.r(jj)], [("GIT", j, j + 1)])
            def q_head(h):
                w = self.load_w(self.W(l, "qb", h), QC, 256)
                jj = h % 2
                for sb in range(NS):
                    s0, s1 = sb * 512, sb * 512 + 512
                    pn, pr, ps = self.bank((4, 2)), self.bank((6, 1)), self.bank((7, 1))
                    for k in range(QC):
                        self.mm(pn, 0, 512, w.ap(k, 0, 128), CQN.ap(k, s0, s1), k == 0, k == QC - 1, [w.r(k), CQN.r(k, s0, s1)])
                    for k in range(QC):
                        self.mm(pr, 0, 512, w.ap(k, 128, 192), CQN.ap(k, s0, s1), k == 0, k == QC - 1, [w.r(k), CQN.r(k, s0, s1)], m=64)
                    for k in range(QC):
                        self.mm(ps, 0, 512, w.ap(k, 192, 256), CQN.ap(k, s0, s1), k == 0, k == QC - 1, [w.r(k), CQN.r(k, s0, s1)], m=64)
                    jr = cnt_box[0] % 2
                    cnt_box[0] += 1
                    g = self.vc((l, "q_norm"))
                    self.rmsnorm([(pn.ap(), pn.r(), 128), (pr.ap(0, 512, 0, 64), pr.r(), 64), (ps.ap(0, 512, 0, 64), ps.r(), 64)],
                                 192, [g, g + 1, g + 2],
                                 [(NST.ap(jj, s0, s1), NST.r(jj, s0, s1)), (R1.ap(jr, 0, 512, 0, 64), R1.r(jr)),
                                  (R2.ap(jr, 0, 512, 0, 64), R2.r(jr))], 512, tmp, stat_n=2)
                    self.rope(R1, R2, jr, RST.ap(jj, s0, s1, 0, 64), RST.r(jj, s0, s1), t0 + s0)
                self.dma("sp", self.QT[h, 0, :, t0:t0 + TB], NST.ap(jj), [NST.r(jj)], [(f"QT:{h}", t0, t0 + TB)])
                self.dma("sp", self.QT[h, 1, 0:64, t0:t0 + TB], RST.ap(jj, 0, TB, 0, 64), [RST.r(jj)], [(f"QR:{h}", t0, t0 + TB)])
            def k_head(h):
                w = self.load_w(self.W(l, "kvk", h), KVC, 128)
                jj = h % 2
                for sb in range(NS):
                    s0, s1 = sb * 512, sb * 512 + 512
                    pn = self.bank((4, 2))
                    for k in range(KVC):
                        self.mm(pn, 0, 512, w.ap(k), CKVN.ap(k, s0, s1), k == 0, k == KVC - 1, [w.r(k), CKVN.r(k, s0, s1)])
                    jr = cnt_box[0] % 2
                    cnt_box[0] += 1
                    g = self.vc((l, "k_norm"))
                    self.rmsnorm([(pn.ap(), pn.r(), 128), (KPE.ap(0, s0, s1, 0, 64), KPE.r(0, s0, s1), 64),
                                  (KPE.ap(1, s0, s1, 0, 64), KPE.r(1, s0, s1), 64)],
                                 192, [g, g + 1, g + 2],
                                 [(NST.ap(jj, s0, s1), NST.r(jj, s0, s1)), (R1.ap(jr, 0, 512, 0, 64), R1.r(jr)),
                                  (R2.ap(jr, 0, 512, 0, 64), R2.r(jr))], 512, tmp, stat_n=2)
                    self.rope(R1, R2, jr, RST.ap(jj, s0, s1, 0, 64), RST.r(jj, s0, s1), t0 + s0)
                self.dma("sp", self.gk(None, h, 0)[:, t0:t0 + TB], NST.ap(jj), [NST.r(jj)], [(f"GIK:{h}", t0, t0 + TB)])
                self.dma("sp", self.gk(None, h, 1)[:, t0:t0 + TB], RST.ap(jj, 0, TB, 0, 64), [RST.r(jj)], [(f"GIR:{h}", t0, t0 + TB)])
            self.G_STAT = (3, 1)
            heads = [(q_head, h) for h in range(NH)] + [(k_head, h) for h in range(NH)]
            per = -(-len(heads) // CC)
            for j in range(CC):
                conv_pair(j)
                for fn, h in heads[j * per:(j + 1) * per]:
                    fn(h)
            for fn, h in heads[CC * per:]:
                fn(h)
            self.G_STAT = (6, 2)
            cnt = cnt_box[0]
            wv = self.load_w(self.W(l, "kvv", 0), KVC, NH * 128)
            for tt_ in range(TB // 128):
                jj = tt_ % 2
                cbw = min(512, NH * 128)
                for cb in range(NH * 128 // cbw):
                    pv = self.bank(self.G_B)
                    for k in range(KVC):
                        self.mm(pv, 0, cbw, CKVN.ap(k, tt_ * 128, tt_ * 128 + 128), wv.ap(k, cb * cbw, cb * cbw + cbw),
                                k == 0, k == KVC - 1, [wv.r(k), CKVN.r(k, tt_ * 128, tt_ * 128 + 128)])
                    self.act(VS.ap(jj, cb * cbw, cb * cbw + cbw), pv.ap(0, cbw), AF.Copy, [pv.r(0, cbw)], [VS.r(jj, cb * cbw, cb * cbw + cbw)])
                tk = t0 + tt_ * 128
                for p in range(NH // 2):
                    self.dma("sp", vregs[p][:, tk:tk + 128, :].rearrange("h p d -> p h d"),
                             VS.t[:, jj, p * 256:(p + 1) * 256].rearrange("p (h d) -> p h d", d=128), [VS.r(jj)],
                             [(f"GIV:{2 * p}", tk, tk + 128), (f"GIV:{2 * p + 1}", tk, tk + 128)])

    def exchange(self):
        c = self.cfg
        N = c.NTOK
        rg = [[0, 1, 2, 3], [4, 5, 6, 7]]

        def ag(src, dst, reads, writes):
            self.P.add("pool", lambda h: h.collective_compute("AllGather", ALU.bypass, replica_groups=rg,
                                                             ins=[src.opt()], outs=[dst.opt()]), reads, writes, cc=True)

        for h in range(c.NH):
            reads = [(f"GIK:{h}", 0, N), (f"GIR:{h}", 0, N)] + ([("GIT", 0, c.CC)] if h == 0 else [])
            ag(self.GK[h], self.GOK[h], reads, [(f"GOK:{h}", 0, 1)])
            if h % 2 == 1:
                p = h // 2
                ag(self.GV[p], self.GOV[p], [(f"GIV:{2 * p}", 0, N), (f"GIV:{2 * p + 1}", 0, N)], [(f"GOV:{p}", 0, 1)])
        import os
        lvl = int(os.environ.get("EXLVL", "9"))
        if lvl < 1:
            return
        off = self.work0
        TL = self.view(off, 4, c.CC * 32, BF16); off += TL.nbytes
        HL = self.view(off, 1, c.CC * 32, F32); off += HL.nbytes
        HB = self.view(off, 1, c.CC * 32, BF16); off += HB.nbytes
        for r in range(4):
            self.dma("sp", TL.t[:, r, :].rearrange("p (j t) -> p j t", t=32),
                     self.gt(r).rearrange("(j p) t -> p j t", p=128), [("GOK:0", 0, 1)], [TL.r(r)])
        if lvl < 2:
            return
        s0 = self.vc("sel")
        self.ts("dve", HL.ap(0), TL.ap(0), self.vcol(s0), None, ALU.mult, None, [TL.r(0), self.vecs.r(0)], [HL.r(0)])
        for r in range(1, 4):
            self.stt(HL.ap(0), TL.ap(r), self.vcol(s0 + r), HL.ap(0), ALU.mult, ALU.add, [TL.r(r), HL.r(0), self.vecs.r(0)], [HL.r(0)])
        self.copy("dve", HB.ap(0), HL.ap(0), [HL.r(0)], [HB.r(0)])
        if lvl < 3:
            return
        self.dma("sp", self.AT[:, :, 0:32].rearrange("j p t -> p j t"), HB.t[:, 0, :].rearrange("p (j t) -> p j t", t=32),
                 [HB.r(0)], [(f"AT:{j}", 0, 32) for j in range(c.CC)])

    def mixB(self, l):
        c = self.cfg
        KC, CC, NH, N, NQB, NKB = c.KC, c.CC, c.NH, c.NTOK, c.NQB, c.NKB
        off = self.work0
        CCAT = self.view(off, KC, N, BF16); off += CCAT.nbytes
        base = off
        Y32 = self.view(off, CC, N, F32); off += Y32.nbytes
        AX = self.view(off, 2, 32 + N, BF16); off += (AX.nbytes + 31) // 32 * 32
        DG = self.view(off, 2 * 31, 128, BF16); off += DG.nbytes
        SQ = self.view(off, 4, 512, BF16); off += SQ.nbytes
        YB = self.view(off, 4, 512, BF16); off += YB.nbytes
        ST = self.view(off, 4, 512, F32); off += ST.nbytes
        assert off <= SB_BYTES, off
        cw0 = self.vc((l, "conv_w"))
        for j in range(CC):
            jj = j % 2
            self.dma("sp", AX.ap(jj), self.AT[j], [(f"AT:{j}", 0, 32 + N)], [AX.r(jj)])
            for u in range(31):
                self.ts("dve", DG.ap(jj * 31 + u), self.ident.ap(0), self.vcol(cw0 + j * 31 + u), None, ALU.mult, None,
                        [self.ident.r(0), self.vecs.r(0)], [DG.r(jj * 31 + u)])
            for qb in range(NQB):
                q0 = qb * 512
                pb = self.bank(self.G_A)
                for u in range(31):
                    self.mm(pb, 0, 512, DG.ap(jj * 31 + u), AX.ap(jj, q0 + 2 + u, q0 + 2 + u + 512), u == 0, u == 30,
                            [DG.r(jj * 31 + u), AX.r(jj, q0 + 2 + u, q0 + 2 + u + 512)])
                self.act(Y32.ap(j, q0, q0 + 512), pb.ap(), AF.Identity, [pb.r(), self.vecs.r(0)], [Y32.r(j, q0, q0 + 512)],
                         bias=self.vcol(self.vc((l, "conv_b"), j)))
        sqi = 0
        for qb in range(NQB):
            q0 = qb * 512
            bs, bq = self.bank(self.G_STAT), self.bank(self.G_STAT)
            for j in range(CC):
                s = sqi % 4
                sqi += 1
                self.act(SQ.ap(s), Y32.ap(j, q0, q0 + 512), AF.Square, [Y32.r(j, q0, q0 + 512)], [SQ.r(s)])
                self.copy("pool", YB.ap(s), Y32.ap(j, q0, q0 + 512), [Y32.r(j, q0, q0 + 512)], [YB.r(s)])
                self.mm(bs, 0, 512, self.ones.ap(0), YB.ap(s), j == 0, j == CC - 1, [self.ones.r(0), YB.r(s)])
                self.mm(bq, 0, 512, self.ones.ap(0), SQ.ap(s), j == 0, j == CC - 1, [self.ones.r(0), SQ.r(s)])
            inv = 1.0 / c.CCH
            self.ts("dve", ST.ap(0), bs.ap(), inv, None, ALU.mult, None, [bs.r()], [ST.r(0)])
            self.tt("dve", ST.ap(3), ST.ap(0), ST.ap(0), ALU.mult, [ST.r(0)], [ST.r(3)])
            self.stt(ST.ap(1), bq.ap(), inv, ST.ap(3), ALU.mult, ALU.subtract, [bq.r(), ST.r(3)], [ST.r(1)])
            self.act(ST.ap(1), ST.ap(1), AF.Sqrt, [ST.r(1), self.epsc.r(0)], [ST.r(1)], bias=self.epsc.ap(0, 0, 1))
            self.recip(ST.ap(2), ST.ap(1), [ST.r(1)], [ST.r(2)])
            for j in range(CC):
                y = Y32.ap(j, q0, q0 + 512)
                yr = Y32.r(j, q0, q0 + 512)
                self.tt("dve", y, y, ST.ap(0), ALU.subtract, [yr, ST.r(0)], [yr])
                self.tt("dve", y, y, ST.ap(2), ALU.mult, [yr, ST.r(2)], [yr])
                self.act(CCAT.ap(j, q0, q0 + 512), y, AF.Silu, [yr, self.vecs.r(0)], [CCAT.r(j, q0, q0 + 512)],
                         scale=self.vcol(self.vc((l, "conv_ln_g"), j)), bias=self.vcol(self.vc((l, "conv_ln_b"), j)))
        off = base
        QHN = self.view(off, 2, N, BF16); off += QHN.nbytes
        QHR = self.view(off, 2, N, BF16); off += QHR.nbytes
        KN = self.view(off, 2, N, BF16); off += KN.nbytes
        KR = self.view(off, 2, N, BF16); off += KR.nbytes
        VV = self.view(off, 2, N, BF16); off += VV.nbytes
        PT = self.view(off, 4, 512, BF16); off += PT.nbytes
        OA = self.view(off, NQB, 512, F32); off += OA.nbytes
        PA = self.view(off, 2 * NQB, 512, F32); off += PA.nbytes
        ONEF = self.view(off, 1, 128, F32); off += ONEF.nbytes
        self.memset("dve", ONEF.ap(0), 1.0, [ONEF.r(0)])
        RD = self.view(off, 2, 512, F32); off += RD.nbytes
        XO = self.view(off, 2, N, F32); off += XO.nbytes
        self.ring = (off, SB_BYTES)
        self.ring_off = 0
        assert SB_BYTES - off >= 16 * 1024, (off, SB_BYTES)
        scale = 192 ** -0.5
        pti = 0
        srcs = [0, 1, 2, None]
        for h in range(NH):
            jq = h % 2
            self.dma("sp", QHN.ap(jq), self.QT[h, 0], [(f"QT:{h}", 0, N)], [QHN.r(jq)])
            self.dma("sp", QHR.ap(jq, 0, N, 0, 64), self.QT[h, 1, 0:64, :], [(f"QR:{h}", 0, N)], [QHR.r(jq)])
            for si, rk in enumerate(srcs):
                jk = (h * len(srcs) + si) % 2
                own = rk is None
                self.dma("sp", KN.ap(jk), self.gk(rk, h, 0), [(f"GIK:{h}", 0, N)] if own else [(f"GOK:{h}", 0, 1)], [KN.r(jk)])
                self.dma("sp", KR.ap(jk, 0, N, 0, 64), self.gk(rk, h, 1), [(f"GIR:{h}", 0, N)] if own else [(f"GOK:{h}", 0, 1)], [KR.r(jk)])
                self.dma("sp", VV.t[:, jk, :].rearrange("p (j d) -> p j d", d=128),
                         self.gv(rk, h).rearrange("(j p) d -> p j d", p=128),
                         [(f"GIV:{h}", 0, N)] if own else [(f"GOV:{h // 2}", 0, 1)], [VV.r(jk)])
                for qb in range(NQB):
                    q0 = qb * 512
                    bo = self.bank(self.G_B)
                    items = []
                    for kb in range(NKB if not own else 4 * qb + 4):
                        c0 = 128 * (kb - 4 * qb) if (own and kb >= 4 * qb) else 0
                        items.append((kb, c0, own and kb >= 4 * qb))

                    def emit_s(kb, c0):
                        bsb = self.bank(self.G_A)
                        k0 = kb * 128
                        self.mm(bsb, c0, 512, KN.ap(jk, k0, k0 + 128), QHN.ap(jq, q0 + c0, q0 + 512), True, False,
                                [KN.r(jk, k0, k0 + 128), QHN.r(jq, q0 + c0, q0 + 512)])
                        self.mm(bsb, c0, 512, KR.ap(jk, k0, k0 + 128, 0, 64), QHR.ap(jq, q0 + c0, q0 + 512, 0, 64), False, True,
                                [KR.r(jk, k0, k0 + 128), QHR.r(jq, q0 + c0, q0 + 512)])
                        return bsb

                    nxt = emit_s(items[0][0], items[0][1])
                    for i, (kb, c0, diag) in enumerate(items):
                        bsb = nxt
                        if i + 1 < len(items):
                            nxt = emit_s(items[i + 1][0], items[i + 1][1])
                        p = pti % 4
                        pti += 1
                        if own:
                            self.act(PT.ap(p, c0, 512), bsb.ap(c0, 512), AF.Exp, [bsb.r(c0, 512)], [PT.r(p, c0, 512)], scale=scale)
                        else:
                            self.act(PT.ap(p, c0, 512), bsb.ap(c0, 512), AF.Exp, [bsb.r(c0, 512), self.vecs.r(0)], [PT.r(p, c0, 512)],
                                     scale=scale, bias=self.vcol(self.vc("visb") + rk))
                        if diag:
                            self.memset("dve", PT.ap(p, c0, c0 + 64, 64, 128), 0.0, [PT.r(p, c0, c0 + 64)])
                        first, last = i == 0, i == len(items) - 1
                        self.mm(bo, c0, 512, VV.ap(jk, kb * 128, kb * 128 + 128), PT.ap(p, c0, 512), first, last,
                                [VV.r(jk, kb * 128, kb * 128 + 128), PT.r(p, c0, 512)])
                        e = i % 2
                        eng = "dve" if e == 0 else "pool"
                        pa = e * NQB + qb
                        if si == 0 and i < 2:
                            self.copy(eng, PA.ap(pa), PT.ap(p), [PT.r(p)], [PA.r(pa)])
                        else:
                            self.tt(eng, PA.ap(pa, c0, 512), PA.ap(pa, c0, 512), PT.ap(p, c0, 512), ALU.add,
                                    [PA.r(pa, c0, 512), PT.r(p, c0, 512)], [PA.r(pa, c0, 512)])
                    if si == 0:
                        self.act(OA.ap(qb), bo.ap(), AF.Copy, [bo.r()], [OA.r(qb)])
                    else:
                        self.tt("dve", OA.ap(qb), OA.ap(qb), bo.ap(), ALU.add, [OA.r(qb), bo.r()], [OA.r(qb)])
            for qb in range(NQB):
                q0 = qb * 512
                jr = qb % 2
                self.tt("dve", PA.ap(qb), PA.ap(qb), PA.ap(NQB + qb), ALU.add, [PA.r(qb), PA.r(NQB + qb)], [PA.r(qb)])
                bd = self.bank(self.G_C)
                self.mm(bd, 0, 512, ONEF.ap(0), PA.ap(qb), True, True, [ONEF.r(0), PA.r(qb)])
                self.recip(RD.ap(jr), bd.ap(), [bd.r()], [RD.r(jr)])
                self.tt("dve", CCAT.ap(CC + h, q0, q0 + 512), OA.ap(qb), RD.ap(jr), ALU.mult, [OA.r(qb), RD.r(jr)],
                        [CCAT.r(CC + h, q0, q0 + 512)])
        import os
        if os.environ.get("DBG") == "ccat":
            for k in range(KC):
                self.dma("pool", self.outT.a[k], CCAT.ap(k), [CCAT.r(k)], self.outT.r(k, 0, N))
            self.dbg_done = True
            return
        self.proj_residual(self.XT, self.XT, lambda dc: self.W(l, "wout", dc), CCAT, KC, 0, N, XO, 1.0)

    def cross(self, l):
        c = self.cfg
        KC, TB, XH, XHC, NM = c.KC, c.TB, c.XH, c.XHC, c.NMEM
        NS = TB // 512
        MT = NM // 128
        off = self.work0
        KCN = self.view(off, KC, NM, BF16); off += KCN.nbytes
        VC = self.view(off, MT, c.D, BF16); off += VC.nbytes
        tmp, off = self.mktmp(off)
        base = off
        M32 = self.view(off, KC, NM, F32); off += M32.nbytes
        MN = self.view(off, KC, NM, BF16); off += MN.nbytes
        K32 = self.view(off, KC, NM, F32); off += K32.nbytes
        self.ring = (off, SB_BYTES)
        self.ring_off = 0
        self.dma("sp", M32.t, self.memT.rearrange("k p t -> p k t"), [], [M32.r(0, n=KC)])
        self.rmsnorm([(M32.ap(k), M32.r(k), 128) for k in range(KC)], c.D, [self.vc((l, "mem_norm"), k) for k in range(KC)],
                     [(MN.ap(k), MN.r(k)) for k in range(KC)], NM, tmp)
        for oc in range(KC):
            w = self.load_w(self.W(l, "wck", oc), KC, 128)
            pb = self.bank(self.G_A)
            for k in range(KC):
                self.mm(pb, 0, NM, w.ap(k), MN.ap(k), k == 0, k == KC - 1, [w.r(k), MN.r(k)])
            self.act(K32.ap(oc), pb.ap(0, NM), AF.Copy, [pb.r(0, NM)], [K32.r(oc)])
        for hh in range(XH):
            ch = [hh * XHC + i for i in range(XHC)]
            self.rmsnorm([(K32.ap(k), K32.r(k), 128) for k in ch], c.XHD, [self.vc((l, "ck_norm"), i) for i in range(XHC)],
                         [(KCN.ap(k), KCN.r(k)) for k in ch], NM, tmp)
        for cb in range(c.D // 512):
            w = self.load_w(self.W(l, "wcv", cb), KC, 512)
            for mt in range(MT):
                pb = self.bank(self.G_B)
                for k in range(KC):
                    self.mm(pb, 0, 512, MN.ap(k, mt * 128, mt * 128 + 128), w.ap(k), k == 0, k == KC - 1,
                            [w.r(k), MN.r(k, mt * 128, mt * 128 + 128)])
                self.act(VC.ap(mt, cb * 512, cb * 512 + 512), pb.ap(), AF.Copy, [pb.r()], [VC.r(mt, cb * 512, cb * 512 + 512)])
        off = base
        H = self.view(off, KC, TB, BF16); off += H.nbytes
        OC = self.view(off, KC, TB, BF16); off += OC.nbytes
        X32 = self.view(off, KC, 256, F32); off += X32.nbytes
        Q32 = self.view(off, 2 * XHC, TB, F32); off += Q32.nbytes
        QN = self.view(off, XHC, TB, BF16); off += QN.nbytes
        PT = self.view(off, 2 * MT, 512, BF16); off += PT.nbytes
        RD = self.view(off, 2, 512, F32); off += RD.nbytes
        XO = self.view(off, 2, TB, F32); off += XO.nbytes
        self.ring = (off, SB_BYTES)
        self.ring_off = 0
        assert SB_BYTES - off >= 16 * 1024, (off, SB_BYTES)
        scale = c.XHD ** -0.5
        G_S, G_O, G_D, G_P = (0, 2), (2, 2), (4, 1), (6, 2)
        pti = [0]
        for tb in range(c.NTOK // TB):
            t0 = tb * TB
            self.norm_phase(self.XT, t0, TB, H, X32, self.vc((l, "cross_norm")), tmp)

            def proj(hh):
                par = (hh % 2) * XHC
                for i in range(XHC):
                    w = self.load_w(self.W(l, "wcq", hh * XHC + i), KC, 128)
                    for sb in range(NS):
                        s0, s1 = sb * 512, sb * 512 + 512
                        pb = self.bank(G_P)
                        for k in range(KC):
                            self.mm(pb, 0, 512, w.ap(k), H.ap(k, s0, s1), k == 0, k == KC - 1, [w.r(k), H.r(k, s0, s1)])
                        self.act(Q32.ap(par + i, s0, s1), pb.ap(), AF.Copy, [pb.r()], [Q32.r(par + i, s0, s1)])

            def attn(hh):
                par = (hh % 2) * XHC
                self.G_STAT = (5, 1)
                for sb in range(NS):
                    s0, s1 = sb * 512, sb * 512 + 512
                    self.rmsnorm([(Q32.ap(par + i, s0, s1), Q32.r(par + i, s0, s1), 128) for i in range(XHC)], c.XHD,
                                 [self.vc((l, "cq_norm"), i) for i in range(XHC)],
                                 [(QN.ap(i, s0, s1), QN.r(i, s0, s1)) for i in range(XHC)], 512, tmp)
                self.G_STAT = (6, 2)
                pts = {}
                for sb in range(NS):
                    s0, s1 = sb * 512, sb * 512 + 512
                    for mt in range(MT):
                        bsb = self.bank(G_S)
                        for i in range(XHC):
                            kc = hh * XHC + i
                            self.mm(bsb, 0, 512, KCN.ap(kc, mt * 128, mt * 128 + 128), QN.ap(i, s0, s1), i == 0, i == XHC - 1,
                                    [KCN.r(kc, mt * 128, mt * 128 + 128), QN.r(i, s0, s1)])
                        p = pti[0] % (2 * MT)
                        pti[0] += 1
                        self.act(PT.ap(p), bsb.ap(), AF.Exp, [bsb.r()], [PT.r(p)], scale=scale)
                        pts[(sb, mt)] = p
                for sb in range(NS):
                    bd = self.bank(G_D)
                    for mt in range(MT):
                        self.mm(bd, 0, 512, self.ones.ap(0), PT.ap(pts[(sb, mt)]), mt == 0, mt == MT - 1,
                                [self.ones.r(0), PT.r(pts[(sb, mt)])])
                    self.recip(RD.ap(sb % 2), bd.ap(), [bd.r()], [RD.r(sb % 2)])
                for sb in range(NS):
                    s0, s1 = sb * 512, sb * 512 + 512
                    for dv in range(XHC):
                        oc = hh * XHC + dv
                        bo = self.bank(G_O)
                        for mt in range(MT):
                            self.mm(bo, 0, 512, VC.ap(mt, oc * 128, oc * 128 + 128), PT.ap(pts[(sb, mt)]), mt == 0, mt == MT - 1,
                                    [VC.r(mt, oc * 128, oc * 128 + 128), PT.r(pts[(sb, mt)])])
                        self.tt("dve", OC.ap(oc, s0, s1), bo.ap(), RD.ap(sb % 2), ALU.mult, [bo.r(), RD.r(sb % 2)], [OC.r(oc, s0, s1)])

            proj(0)
            for hh in range(XH):
                if hh + 1 < XH:
                    proj(hh + 1)
                attn(hh)
            self.proj_residual(self.XT, self.XT, lambda dc: self.W(l, "wco", dc), OC, KC, t0, TB, XO, 1.0)

    def build(self, upto=None):
        c = self.cfg
        with self.st:
            self.setup()
            stages = []
            for l in range(c.DEPTH):
                last = l == c.DEPTH - 1
                stages += [lambda l=l: self.ffn(l, "ffn1", self.xT if l == 0 else self.XT, self.XT),
                           lambda l=l: self.mixA(l, self.XT),
                           lambda l=l: self.exchange(),
                           lambda l=l: self.mixB(l),
                           lambda l=l: self.cross(l),
                           lambda l=l, last=last: self.ffn(l, "ffn2", self.XT, self.outT if last else self.XT)]
            if upto is not None:
                stages = stages[:upto]
            for s in stages:
                s()
            import os
            dbg = os.environ.get("DBG", "")
            if dbg in ("q", "k"):
                flat = self.outT.a.rearrange("k p t -> (k p) t")
                for h in range(c.NH):
                    if dbg == "q":
                        srcs = [(self.QT[h, 0], f"QT:{h}"), (self.QT[h, 1, 0:64, :], f"QR:{h}")]
                    else:
                        srcs = [(self.gk(None, h, 0), f"GIK:{h}"), (self.gk(None, h, 1), f"GIR:{h}")]
                    self.dma("pool", flat[h * 192:h * 192 + 128, :], srcs[0][0], [(srcs[0][1], 0, c.NTOK)], [("dbgout", 2 * h, 2 * h + 1)])
                    self.dma("pool", flat[h * 192 + 128:h * 192 + 192, :], srcs[1][0], [(srcs[1][1], 0, c.NTOK)], [("dbgout", 2 * h + 1, 2 * h + 2)])
                self.dbg_done = True
            if upto is not None and upto != c.DEPTH * 6 and not getattr(self, "dbg_done", False):
                for k in range(c.KC):
                    self.dma("sp", self.outT.a[k], self.XT.a[k], self.XT.r(k, 0, c.NTOK), self.outT.r(k, 0, c.NTOK))
            self.P.emit()


def prepare_inputs(c, inputs):
    x = np.asarray(inputs["x"])
    mem = np.asarray(inputs["mem"])
    pos = np.asarray(inputs["positions"])
    B, S, D = x.shape
    ncore = B * S // c.NTOK
    per_b = S // c.NTOK
    wflat = pack_weights(c, inputs)
    ident = np.eye(128, dtype=np.float32)
    maps = []
    for core in range(ncore):
        b, r = divmod(core, per_b)
        xs = x[b, r * c.NTOK:(r + 1) * c.NTOK, :]
        maps.append({
            "xT": np.ascontiguousarray(xs.T).reshape(c.KC, 128, c.NTOK),
            "memT": np.ascontiguousarray(mem[b].T).reshape(c.KC, 128, c.NMEM),
            "pos": np.ascontiguousarray(pos[b, r * c.NTOK:(r + 1) * c.NTOK]).reshape(1, c.NTOK).astype(np.int32),
            "wflat": wflat,
            "vecs": pack_vecs(c, inputs, r),
            "ident": ident,
        })
    return maps


def assemble_output(c, results, B, S):
    per_b = S // c.NTOK
    out = np.empty((B, S, c.D), np.float32)
    for core, r in enumerate(results):
        b, rk = divmod(core, per_b)
        out[b, rk * c.NTOK:(rk + 1) * c.NTOK, :] = np.asarray(r["outT"]).reshape(c.D, c.NTOK).T
    return out


_CACHE = {}


def kernel(**inputs):
    c = Cfg()
    if "nc" not in _CACHE:
        nc = bass.Bass("TRN2", target_bir_lowering=False)
        Full(nc, c).build()
        _CACHE["nc"] = nc
    nc = _CACHE["nc"]
    maps = prepare_inputs(c, inputs)
    B, S, _ = np.asarray(inputs["x"]).shape
    res = run_bass_kernel_spmd(nc, maps, core_ids=list(range(len(maps))))
    return assemble_output(c, res.results, B, S)
```

```python
import contextlib
import os
import numpy as np
import concourse.bass as bass
import concourse.mybir as mybir
from concourse.bass_utils import run_bass_kernel_spmd

F32 = mybir.dt.float32
BF16 = mybir.dt.bfloat16
I32 = mybir.dt.int32
AF = mybir.ActivationFunctionType
ALU = mybir.AluOpType

COMPUTE = ("pe", "act", "dve", "pool")
QUEUES = ("sp", "act", "pool")
NDSEM = 6


class _Op:
    __slots__ = ("eng", "fn", "deps", "ddeps", "dma", "cc", "q_idx", "idx", "milestone", "count")

    def __init__(self, eng, fn, dma, cc):
        self.eng = eng
        self.fn = fn
        self.dma = dma
        self.cc = cc
        self.deps = {}
        self.ddeps = {}
        self.milestone = False
        self.count = 0
        self.q_idx = -1


class Prog:
    def __init__(self, nc):
        self.nc = nc
        self.ops = {e: [] for e in ("pe", "act", "dve", "pool", "sp")}
        self.track = {}
        self.ndma = {q: 0 for q in QUEUES}
        self.ncc = 0

    def _segs(self, space, lo, hi):
        segs = self.track.setdefault(space, [[0, 1 << 60, None, []]])
        out = []
        i = 0
        while i < len(segs):
            s = segs[i]
            if s[1] <= lo:
                i += 1
                continue
            if s[0] >= hi:
                break
            if s[0] < lo:
                segs.insert(i, [s[0], lo, s[2], list(s[3])])
                s[0] = lo
                i += 1
                continue
            if s[1] > hi:
                segs.insert(i + 1, [hi, s[1], s[2], list(s[3])])
                s[1] = hi
            out.append(s)
            i += 1
        return out

    def add(self, eng, fn, reads=(), writes=(), dma=False, cc=False):
        op = _Op(eng, fn, dma, cc)
        lst = self.ops[eng]
        op.idx = len(lst)
        me = (eng, op.idx)
        deps, ddeps, ops = op.deps, op.ddeps, self.ops

        def dep(w):
            if w is None or w == me:
                return
            e, i = w
            t = ops[e][i]
            if t.dma:
                k = (e, t.q_idx % NDSEM)
                if ddeps.get(k, -1) < i:
                    ddeps[k] = i
            elif t.cc:
                if ddeps.get("cc", -1) < i:
                    ddeps["cc"] = i
            elif deps.get(e, -1) < i:
                deps[e] = i

        for (space, lo, hi) in reads:
            for s in self._segs(space, lo, hi):
                dep(s[2])
                s[3].append(me)
        for (space, lo, hi) in writes:
            for s in self._segs(space, lo, hi):
                dep(s[2])
                for r in s[3]:
                    dep(r)
                s[2] = me
                s[3] = []
        if dma:
            op.q_idx = self.ndma[eng]
            self.ndma[eng] += 1
        if cc:
            op.q_idx = self.ncc
            self.ncc += 1
        lst.append(op)
        return op

    def emit(self):
        nc = self.nc
        ops = self.ops
        for e, lst in ops.items():
            for op in lst:
                if e in op.deps and not (op.dma or op.cc):
                    i = op.deps[e]
                    if e == "pe":
                        del op.deps[e]
                    elif op.idx - i > 2:
                        del op.deps[e]
                for de, di in op.deps.items():
                    ops[de][di].milestone = True
        for e, lst in ops.items():
            c = 0
            for op in lst:
                if op.milestone:
                    c += 1
                op.count = c
        with contextlib.ExitStack() as st:
            csem = {e: st.enter_context(nc.semaphore("c_" + e)) for e in COMPUTE}
            dsem = {q: [st.enter_context(nc.semaphore(f"d_{q}{k}")) for k in range(NDSEM)]
                    for q in QUEUES}
            ccsem = st.enter_context(nc.semaphore("ccsem"))
            block = st.enter_context(nc.Block())

            def run(e, h):
                waited = {}

                def wait(sem, key, val):
                    if waited.get(key, 0) >= val:
                        return
                    waited[key] = val
                    h.wait_ge(sem, val)

                for op in ops[e]:
                    for de, di in op.deps.items():
                        wait(csem[de], de, ops[de][di].count)
                    for key, di in op.ddeps.items():
                        if key == "cc":
                            wait(ccsem, "cc", ops["pool"][di].q_idx + 1)
                        else:
                            q, k = key
                            wait(dsem[q][k], key, 16 * (ops[q][di].q_idx // NDSEM + 1))
                    if op.dma:
                        k = op.q_idx % NDSEM
                        if op.q_idx >= NDSEM:
                            wait(dsem[e][k], (e, k), 16 * (op.q_idx // NDSEM))
                        op.fn(h).then_inc(dsem[e][k], 16)
                    elif op.cc:
                        op.fn(h).then_inc(ccsem, 1)
                    else:
                        ins = op.fn(h)
                        if op.milestone:
                            ins.then_inc(csem[e], 1)
                if e in QUEUES:
                    n = self.ndma[e]
                    for k in range(min(NDSEM, n)):
                        cnt = (n - 1 - k) // NDSEM + 1
                        wait(dsem[e][k], (e, k), 16 * cnt)
                if e == "pool" and self.ncc:
                    wait(ccsem, "cc", self.ncc)

            @block.tensor
            def _(h):
                run("pe", h)

            @block.scalar
            def _(h):
                run("act", h)

            @block.vector
            def _(h):
                run("dve", h)

            @block.gpsimd
            def _(h):
                run("pool", h)

            @block.sync
            def _(h):
                run("sp", h)


class View:
    def __init__(self, arena, name, off_bytes, n, w, dtype):
        self.esz = 4 if dtype in (F32, I32) else 2
        assert off_bytes % 4 == 0
        self.space = name
        self.base = off_bytes // 2
        self.n, self.w, self.dtype = n, w, dtype
        u = n * w * self.esz // 2
        v = arena[:, self.base:self.base + u]
        if dtype != BF16:
            v = v.bitcast(dtype)
        self.t = v.rearrange("p (n w) -> p n w", n=n)
        self.nbytes = n * w * self.esz

    def r(self, i, lo=0, hi=None, n=1):
        hi = self.w if hi is None else hi
        u = self.esz // 2 if self.esz > 1 else 1
        if n == 1:
            return (self.space, self.base + (i * self.w + lo) * self.esz // 2,
                    self.base + (i * self.w + hi) * self.esz // 2)
        return (self.space, self.base + i * self.w * self.esz // 2,
                self.base + (i + n) * self.w * self.esz // 2)

    def ap(self, i, lo=0, hi=None, p0=0, p1=128):
        hi = self.w if hi is None else hi
        return self.t[p0:p1, i, lo:hi]

    def ap3(self, i, n, p0=0, p1=128):
        return self.t[p0:p1, i:i + n, :]


class PsumBank:
    def __init__(self, st, nc, name):
        self.name = name
        self.t = st.enter_context(nc.psum_tensor(name, [128, 512], F32))

    def r(self, lo=0, hi=512):
        return (self.name, lo, hi)

    def ap(self, lo=0, hi=512, p0=0, p1=128):
        return self.t[p0:p1, lo:hi]


class DT:
    def __init__(self, ap, name):
        self.a = ap
        self.name = name

    def r(self, c, lo, hi, n=1):
        return [(f"{self.name}:{c + j}", lo, hi) for j in range(n)]


class Cfg:
    def __init__(self, **kw):
        self.D = 2048
        self.F = 5632
        self.NTOK = 2048
        self.TB = 1024
        self.EPS = 1e-6
        self.NH = 8
        self.QL = 768
        self.KVL = 256
        self.XH = 4
        self.NMEM = 256
        self.DEPTH = 2
        self.__dict__.update(kw)
        self.KC = self.D // 128
        self.FC = self.F // 128
        self.CCH = self.D // 2
        self.CC = self.CCH // 128
        self.QC = self.QL // 128
        self.KVC = self.KVL // 128
        self.XHD = self.D // self.XH
        self.XHC = self.XHD // 128
        self.NQB = self.NTOK // 512
        self.NKB = self.NTOK // 128
        self.R_K = self.NH * 192
        self.R_V = self.NH * 128
        self.R_T = self.CCH * 32 // self.NTOK
        self.R = self.R_K + self.R_V + self.R_T


SB_BYTES = 206 * 1024
DEN_MODE = os.environ.get("DEN_MODE", "pe")
MIXA_IL = int(os.environ.get("MIXA_IL", "0"))


class Builder:
    def __init__(self, nc, cfg):
        self.nc = nc
        self.cfg = cfg
        self.st = contextlib.ExitStack()
        self.P = Prog(nc)
        self.arena = self.st.enter_context(nc.sbuf_tensor("arena", [128, SB_BYTES // 2], BF16))
        self.banks = [PsumBank(self.st, nc, f"ps{i}") for i in range(8)]
        self.bank_rr = {}
        self.ring_off = 0
        self.dram = {}

    def view(self, off, n, w, dtype):
        return View(self.arena, "arena", off, n, w, dtype)

    def bank(self, grp):
        lo, n = grp
        k = self.bank_rr.get(grp, 0)
        self.bank_rr[grp] = k + 1
        return self.banks[lo + k % n]

    def ring_alloc(self, nbytes):
        r0, r1 = self.ring
        if self.ring_off + nbytes > r1 - r0:
            self.ring_off = 0
        off = r0 + self.ring_off
        self.ring_off += (nbytes + 31) // 32 * 32
        return off

    def load_w(self, src_ap, n, w, reads=()):
        off = self.ring_alloc(n * w * 2)
        v = self.view(off, n, w, BF16)
        self.P.add("pool", lambda h: h.dma_start(out=v.t, in_=src_ap.rearrange("p (n w) -> p n w", n=n),
                                                 max_dma_last_dim=8192),
                   reads=list(reads), writes=[v.r(0, n=n)], dma=True)
        return v

    def mm(self, bank, lo, hi, lhsT, rhs, start, stop, reads, m=128):
        self.P.add("pe", lambda h: h.matmul(bank.ap(lo, hi, 0, m), lhsT, rhs, start=start, stop=stop),
                   reads=reads, writes=[bank.r(lo, hi)])

    def setup_consts(self, off, vecs_ap, nv):
        self.ones = self.view(off, 1, 128, BF16)
        off += 256
        self.epsc = self.view(off, 1, 8, F32)
        off += 32
        self.vecs = self.view(off, 1, nv, F32)
        off += nv * 4
        self.memset("dve", self.ones.ap(0), 1.0, [self.ones.r(0)])
        self.memset("dve", self.epsc.ap(0), self.cfg.EPS, [self.epsc.r(0)])
        self.dma("sp", self.vecs.ap(0), vecs_ap, [], [self.vecs.r(0)])
        return off

    def vcol(self, c, p0=0, p1=128):
        return self.vecs.ap(0, c, c + 1, p0, p1)

    def dma(self, q, out, in_, reads, writes, **kw):
        return self.P.add(q, lambda h: h.dma_start(out=out, in_=in_, **kw), reads, writes, dma=True)

    def act(self, out, in_, func, reads, writes, **kw):
        return self.P.add("act", lambda h: h.activation(out=out, in_=in_, func=func, **kw), reads, writes)

    def tt(self, eng, out, in0, in1, op, reads, writes):
        return self.P.add(eng, lambda h: h.tensor_tensor(out=out, in0=in0, in1=in1, op=op), reads, writes)

    def stt(self, out, in0, scalar, in1, op0, op1, reads, writes):
        return self.P.add("dve", lambda h: h.scalar_tensor_tensor(out=out, in0=in0, scalar=scalar, in1=in1,
                                                                  op0=op0, op1=op1), reads, writes)

    def ts(self, eng, out, in0, s1, s2, op0, op1, reads, writes):
        if op1 is None:
            return self.P.add(eng, lambda h: h.tensor_scalar(out=out, in0=in0, scalar1=s1, scalar2=None, op0=op0),
                              reads, writes)
        return self.P.add(eng, lambda h: h.tensor_scalar(out=out, in0=in0, scalar1=s1, scalar2=s2, op0=op0, op1=op1),
                          reads, writes)

    def recip(self, out, in_, reads, writes):
        return self.P.add("dve", lambda h: h.reciprocal(out=out, in_=in_), reads, writes)

    def copy(self, eng, out, in_, reads, writes):
        return self.P.add(eng, lambda h: h.tensor_copy(out=out, in_=in_), reads, writes)

    def memset(self, eng, out, val, writes):
        return self.P.add(eng, lambda h: h.memset(out, val), (), writes)

    def rmsnorm(self, xs, D, gcols, outs, W, tmp, stat_n=None):
        bank = self.bank(self.G_STAT)
        sqv = tmp["sq"]
        n = len(xs) if stat_n is None else stat_n
        for k, (xap, xr, p) in enumerate(xs[:n]):
            s = tmp["sq_i"] % sqv.n
            tmp["sq_i"] += 1
            self.act(sqv.ap(s, 0, W, 0, p), xap, AF.Square, [xr], [sqv.r(s, 0, W)])
            self.mm(bank, 0, W, self.ones.ap(0, 0, 128, 0, p), sqv.ap(s, 0, W, 0, p), k == 0, k == n - 1,
                    [self.ones.r(0), sqv.r(s, 0, W)])
        rs = tmp["rs"]
        j = tmp["rs_i"] % rs.n
        tmp["rs_i"] += 1
        self.act(rs.ap(j, 0, W), bank.ap(0, W), AF.Sqrt, [bank.r(0, W), self.epsc.r(0)], [rs.r(j, 0, W)],
                 bias=self.epsc.ap(0, 0, 1), scale=1.0 / D)
        self.recip(rs.ap(j, 0, W), rs.ap(j, 0, W), [rs.r(j, 0, W)], [rs.r(j, 0, W)])
        for k, (xap, xr, p) in enumerate(xs):
            oap, orr = outs[k]
            if gcols is None:
                self.tt("dve", oap, xap, rs.ap(j, 0, W, 0, p), ALU.mult, [xr, rs.r(j, 0, W)], [orr])
            else:
                self.stt(oap, xap, self.vcol(gcols[k], 0, p), rs.ap(j, 0, W, 0, p), ALU.mult, ALU.mult,
                         [xr, rs.r(j, 0, W), self.vecs.r(0)], [orr])
        return rs, j

    def ffn(self, src, dst, wgu, wd, gcol0):
        c = self.cfg
        KC, FC, TB = c.KC, c.FC, c.TB
        NS = TB // 512
        off = self.work0
        H = self.view(off, KC, TB, BF16); off += H.nbytes
        A = self.view(off, FC, TB, BF16); off += A.nbytes
        X32 = self.view(off, KC, 512, F32); off += X32.nbytes
        tmp = {"sq": self.view(off, 4, 512, BF16), "sq_i": 0, "rs_i": 0}; off += tmp["sq"].nbytes
        tmp["rs"] = self.view(off, 2, 512, F32); off += tmp["rs"].nbytes
        SG = self.view(off, 2, 512, F32); off += SG.nbytes
        XO = self.view(off, 2, TB, F32); off += XO.nbytes
        self.ring = (off, SB_BYTES)
        self.ring_off = 0
        assert SB_BYTES - off >= 24 * 1024, (off, SB_BYTES)
        for tb in range(c.NTOK // TB):
            t0 = tb * TB
            for sb in range(NS):
                c0 = t0 + sb * 512
                self.dma("sp", X32.t, src.a[:, :, c0:c0 + 512].rearrange("k p t -> p k t"),
                         src.r(0, c0, c0 + 512, n=KC), [X32.r(0, n=KC)])
                xs = [(X32.ap(k), X32.r(k), 128) for k in range(KC)]
                outs = [(H.ap(k, sb * 512, sb * 512 + 512), H.r(k, sb * 512, sb * 512 + 512)) for k in range(KC)]
                self.rmsnorm(xs, c.D, [gcol0 + k for k in range(KC)], outs, 512, tmp)
            for f in range(FC):
                w = self.load_w(wgu[f], 2 * KC, 128)
                for sb in range(NS):
                    s0, s1 = sb * 512, sb * 512 + 512
                    pg = self.bank(self.G_A)
                    pu = self.bank(self.G_B)
                    for k in range(KC):
                        self.mm(pg, 0, 512, w.ap(k), H.ap(k, s0, s1), k == 0, k == KC - 1, [w.r(k), H.r(k, s0, s1)])
                    for k in range(KC):
                        self.mm(pu, 0, 512, w.ap(KC + k), H.ap(k, s0, s1), k == 0, k == KC - 1, [w.r(KC + k), H.r(k, s0, s1)])
                    j = (f * NS + sb) % 2
                    self.act(SG.ap(j), pg.ap(), AF.Silu, [pg.r()], [SG.r(j)])
                    self.tt("dve", A.ap(f, s0, s1), SG.ap(j), pu.ap(), ALU.mult, [SG.r(j), pu.r()], [A.r(f, s0, s1)])
            for dc in range(KC):
                w = self.load_w(wd[dc], FC, 128)
                j = dc % 2
                self.dma("sp", XO.ap(j), src.a[dc, :, t0:t0 + TB], src.r(dc, t0, t0 + TB), [XO.r(j)])
                for sb in range(NS):
                    s0, s1 = sb * 512, sb * 512 + 512
                    pd = self.bank(self.G_C)
                    for f in range(FC):
                        self.mm(pd, 0, 512, w.ap(f), A.ap(f, s0, s1), f == 0, f == FC - 1, [w.r(f), A.r(f, s0, s1)])
                    self.stt(XO.ap(j, s0, s1), pd.ap(), 0.5, XO.ap(j, s0, s1), ALU.mult, ALU.add,
                             [pd.r(), XO.r(j, s0, s1)], [XO.r(j, s0, s1)])
                self.dma("sp", dst.a[dc, :, t0:t0 + TB], XO.ap(j), [XO.r(j)], dst.r(dc, t0, t0 + TB))

    G_A = (0, 2)
    G_B = (2, 2)
    G_C = (4, 2)
    G_STAT = (6, 2)
    rope_eng = os.environ.get("ROPE_ENG", "dve")


def _rope_perm():
    return np.concatenate([np.arange(32, 64), np.arange(0, 32)])


def weight_units(c):
    ar = np.arange
    for pre in ("ffn1", "ffn2"):
        for f in range(c.FC):
            cols = f * 128 + ar(128)
            yield (pre + "_gu", f), [(pre + "_w_gate", cols), (pre + "_w_up", cols)]
        for dc in range(c.KC):
            yield (pre + "_d", dc), [(pre + "_w_down", dc * 128 + ar(128))]
    for j in range(c.CC):
        yield ("in_conv", j), [("w_in", j * 128 + ar(128)), ("w_in", c.CCH + j * 128 + ar(128))]
    o_q = 2 * c.CCH
    o_kv = o_q + c.QL
    o_pe = o_kv + c.KVL
    for i in range(c.QC):
        yield ("in_cq", i), [("w_in", o_q + i * 128 + ar(128))]
    for i in range(c.KVC):
        yield ("in_ckv", i), [("w_in", o_kv + i * 128 + ar(128))]
    yield ("in_kpe", 0), [("w_in", o_pe + ar(64))]
    yield ("in_kpe", 1), [("w_in", o_pe + _rope_perm())]
    for h in range(c.NH):
        yield ("qb", h), [("w_q_b", np.concatenate([h * 192 + ar(128), h * 192 + 128 + ar(64),
                                                     h * 192 + 128 + _rope_perm()]))]
        yield ("kvk", h), [("w_kv_b", h * 256 + ar(128))]
    yield ("kvv", 0), [("w_kv_b", np.concatenate([h * 256 + 128 + ar(128) for h in range(c.NH)]))]
    for dc in range(c.KC):
        cols = dc * 128 + ar(128)
        yield ("wout", dc), [("w_out", cols)]
        yield ("wcq", dc), [("w_cq", cols)]
        yield ("wck", dc), [("w_ck", cols)]
        yield ("wco", dc), [("w_co", cols)]
    for cb in range(c.D // 512):
        yield ("wcv", cb), [("w_cv", cb * 512 + ar(512))]


_KDIM = {"ffn1_w_gate": "D", "ffn1_w_up": "D", "ffn1_w_down": "F", "ffn2_w_gate": "D", "ffn2_w_up": "D",
         "ffn2_w_down": "F", "w_in": "D", "w_q_b": "QL", "w_kv_b": "KVL", "w_out": "D", "w_cq": "D",
         "w_ck": "D", "w_cv": "D", "w_co": "D"}


def weight_plan(c):
    plan = {}
    off = 0
    for l in range(c.DEPTH):
        for key, parts in weight_units(c):
            ln = sum(getattr(c, _KDIM[w]) // 128 * len(cols) for w, cols in parts)
            plan[(l,) + key] = (off, ln)
            off += 128 * ln
    return plan, off


def pack_weights(c, inputs):
    plan, total = weight_plan(c)
    flat = np.empty(total, np.float32)
    for l in range(c.DEPTH):
        for key, parts in weight_units(c):
            off, ln = plan[(l,) + key]
            blocks = []
            for w, cols in parts:
                W = np.asarray(inputs[w][l])
                K = W.shape[0]
                blocks.append(W[:, cols].reshape(K // 128, 128, len(cols)).transpose(1, 0, 2).reshape(128, -1))
            flat[off:off + 128 * ln] = np.concatenate(blocks, axis=1).reshape(-1)
    return flat


def vec_plan(c):
    cols = {}
    n = 0

    def add(name, k):
        nonlocal n
        cols[name] = n
        n += k

    for l in range(c.DEPTH):
        for nm in ("ffn1_norm", "mix_norm", "cross_norm", "mem_norm", "ffn2_norm"):
            add((l, nm), c.KC)
        add((l, "conv_w"), c.CC * 31)
        for nm in ("conv_b", "conv_ln_g", "conv_ln_b"):
            add((l, nm), c.CC)
        add((l, "q_a_norm"), c.QC)
        add((l, "kv_a_norm"), c.KVC)
        add((l, "q_norm"), 3)
        add((l, "k_norm"), 3)
        add((l, "cq_norm"), c.XHC)
        add((l, "ck_norm"), c.XHC)
    add("invf", 1)
    add("sgn", 1)
    add("sel", 4)
    add("visb", 3)
    return cols, n


def pack_vecs(c, inputs, rank):
    cols, n = vec_plan(c)
    V = np.zeros((128, n), np.float32)

    def put(name, arr2d):
        c0 = cols[name]
        for i, row in enumerate(arr2d):
            V[:len(row), c0 + i] = row

    for l in range(c.DEPTH):
        for nm in ("ffn1_norm", "mix_norm", "cross_norm", "mem_norm", "ffn2_norm", "conv_b", "conv_ln_g",
                   "conv_ln_b", "q_a_norm", "kv_a_norm", "cq_norm", "ck_norm"):
            put((l, nm), np.asarray(inputs[nm][l]).reshape(-1, 128))
        cw = np.asarray(inputs["conv_w"][l])
        put((l, "conv_w"), cw.reshape(31, c.CC, 128).transpose(1, 0, 2).reshape(c.CC * 31, 128))
        for nm in ("q_norm", "k_norm"):
            g = np.asarray(inputs[nm][l])
            put((l, nm), [g[:128], g[128:192], g[128:192][_rope_perm()]])
    inv = (1.0 / (np.float32(10000.0) ** (np.arange(0, 64, 2, dtype=np.float32) / np.float32(64)))).astype(np.float32)
    put("invf", [np.concatenate([inv, inv])])
    put("sgn", [np.concatenate([-np.ones(32, np.float32), np.ones(32, np.float32)])])
    sel = np.zeros((4, 128), np.float32)
    if rank > 0:
        sel[rank - 1] = 1.0
    put("sel", sel)
    vb = np.zeros((3, 128), np.float32)
    for r in range(3):
        if r >= rank:
            vb[r] = -30000.0
    put("visb", vb)
    return V


class Full(Builder):
    def __init__(self, nc, cfg):
        super().__init__(nc, cfg)
        c = cfg
        self.wplan, wtotal = weight_plan(c)
        self.vcols, nv = vec_plan(c)
        self.nv = nv
        dt = nc.dram_tensor
        self.xT = DT(dt("xT", [c.KC, 128, c.NTOK], F32, kind="ExternalInput").ap(), "xT")
        self.memT = dt("memT", [c.KC, 128, c.NMEM], F32, kind="ExternalInput").ap()
        self.pos = dt("pos", [1, c.NTOK], I32, kind="ExternalInput").ap()
        self.wflat = dt("wflat", [wtotal], F32, kind="ExternalInput").ap()
        self.vecs_in = dt("vecs", [128, nv], F32, kind="ExternalInput").ap()
        self.ident_in = dt("ident", [128, 128], F32, kind="ExternalInput").ap()
        self.outT = DT(dt("outT", [c.KC, 128, c.NTOK], F32, kind="ExternalOutput").ap(), "outT")
        self.XT = DT(dt("XT", [c.KC, 128, c.NTOK], F32).ap(), "XT")
        self.AT = dt("AT", [c.CC, 128, 32 + c.NTOK], BF16).ap()
        self.QT = dt("QT", [c.NH, 2, 128, c.NTOK], BF16).ap()
        self.rk = [192 + (c.R_T if h == 0 else 0) for h in range(c.NH)]
        self.GK = [dt(f"GK{h}", [self.rk[h], c.NTOK], BF16).ap() for h in range(c.NH)]
        self.GOK = [dt(f"GOK{h}", [4 * self.rk[h], c.NTOK], BF16).ap() for h in range(c.NH)]
        self.GV = [dt(f"GV{p}", [256, c.NTOK], BF16).ap() for p in range(c.NH // 2)]
        self.GOV = [dt(f"GOV{p}", [4 * 256, c.NTOK], BF16).ap() for p in range(c.NH // 2)]

    def W(self, l, *key):
        off, ln = self.wplan[(l,) + key]
        return self.wflat[off:off + 128 * ln].rearrange("(p l) -> p l", p=128)

    def vc(self, name, k=0):
        return self.vcols[name] + k

    def gk(self, src, h, part):
        t = self.GK[h] if src is None else self.GOK[h]
        r = (0 if src is None else src * self.rk[h]) + (0 if part == 0 else 128)
        return t[r:r + (128 if part == 0 else 64), :]

    def gv(self, src, h):
        t = self.GV[h // 2] if src is None else self.GOV[h // 2]
        r = (0 if src is None else src * 256) + (h % 2) * 128
        return t[r:r + 128, :].rearrange("r (a d) -> (r a) d", d=128)

    def gt(self, src):
        c = self.cfg
        t = self.GK[0] if src is None else self.GOK[0]
        r = (0 if src is None else src * self.rk[0]) + 192
        return t[r:r + c.R_T, :].rearrange("r (a t) -> (r a) t", t=32)

    def setup(self):
        c = self.cfg
        off = self.setup_consts(0, self.vecs_in, self.nv)
        off = (off + 31) // 32 * 32
        self.ident = self.view(off, 1, 128, BF16); off += 256
        self.dma("pool", self.ident.ap(0), self.ident_in, [], [self.ident.r(0)])
        self.COS = self.view(off, 1, c.NTOK, F32); off += self.COS.nbytes
        self.SINS = self.view(off, 1, c.NTOK, F32); off += self.SINS.nbytes
        self.work0 = off
        self.rope_tables()

    def rope_tables(self):
        c = self.cfg
        N = c.NTOK
        off = self.work0
        PI = self.view(off, 1, N, I32); off += PI.nbytes
        ANG = self.view(off, 1, N, F32); off += ANG.nbytes
        T1 = self.view(off, 1, N, F32); off += T1.nbytes
        KI = self.view(off, 1, N, I32); off += KI.nbytes
        KF = self.view(off, 1, N, F32); off += KF.nbytes
        R = self.view(off, 1, N, F32); off += R.nbytes
        M = self.view(off, 1, N, F32); off += M.nbytes
        p = 64
        a = lambda v: v.ap(0, 0, N, 0, p)
        self.dma("sp", a(PI), self.pos.partition_broadcast(p), [], [PI.r(0)])
        self.copy("dve", a(ANG), a(PI), [PI.r(0)], [ANG.r(0)])
        self.ts("dve", a(ANG), a(ANG), self.vcol(self.vc("invf"), 0, p), None, ALU.mult, None,
                [ANG.r(0), self.vecs.r(0)], [ANG.r(0)])
        TWO_PI = 2.0 * np.pi
        C1 = 6.28125
        C2 = TWO_PI - C1
        for which, dst in (("sin", self.SINS), ("cos", self.COS)):
            shift = 0.0 if which == "sin" else np.pi / 2
            self.ts("dve", a(T1), a(ANG), 1.0 / TWO_PI, 0.5 + shift / TWO_PI, ALU.mult, ALU.add, [ANG.r(0)], [T1.r(0)])
            self.copy("dve", a(KI), a(T1), [T1.r(0)], [KI.r(0)])
            self.copy("dve", a(KF), a(KI), [KI.r(0)], [KF.r(0)])
            self.stt(a(R), a(KF), -C1, a(ANG), ALU.mult, ALU.add, [KF.r(0), ANG.r(0)], [R.r(0)])
            self.stt(a(R), a(KF), -C2, a(R), ALU.mult, ALU.add, [KF.r(0), R.r(0)], [R.r(0)])
            if shift:
                self.ts("dve", a(R), a(R), float(shift), None, ALU.add, None, [R.r(0)], [R.r(0)])
            self.ts("dve", a(M), a(R), float(-np.pi), None, ALU.is_lt, None, [R.r(0)], [M.r(0)])
            self.stt(a(R), a(M), float(TWO_PI), a(R), ALU.mult, ALU.add, [M.r(0), R.r(0)], [R.r(0)])
            self.ts("dve", a(M), a(R), float(np.pi), None, ALU.is_gt, None, [R.r(0)], [M.r(0)])
            self.stt(a(R), a(M), float(-TWO_PI), a(R), ALU.mult, ALU.add, [M.r(0), R.r(0)], [R.r(0)])
            self.ts("dve", a(R), a(R), float(-3.1415925), float(3.1415925), ALU.max, ALU.min, [R.r(0)], [R.r(0)])
            self.act(a(dst), a(R), AF.Sin, [R.r(0)], [dst.r(0)])
        self.ts("dve", a(self.SINS), a(self.SINS), self.vcol(self.vc("sgn"), 0, p), None, ALU.mult, None,
                [self.SINS.r(0), self.vecs.r(0)], [self.SINS.r(0)])

    def norm_phase(self, src, t0, TB, H, X32, gcol0, tmp, WN=256):
        c = self.cfg
        for s in range(TB // WN):
            c0 = t0 + s * WN
            self.dma("sp", X32.t[:, :, 0:WN], src.a[:, :, c0:c0 + WN].rearrange("k p t -> p k t"),
                     src.r(0, c0, c0 + WN, n=c.KC), [X32.r(0, n=c.KC)])
            xs = [(X32.ap(k, 0, WN), X32.r(k, 0, WN), 128) for k in range(c.KC)]
            outs = [(H.ap(k, s * WN, s * WN + WN), H.r(k, s * WN, s * WN + WN)) for k in range(c.KC)]
            self.rmsnorm(xs, c.D, [gcol0 + k for k in range(c.KC)], outs, WN, tmp)

    def mktmp(self, off):
        tmp = {"sq": self.view(off, 4, 512, BF16), "sq_i": 0, "rs_i": 0}
        off += tmp["sq"].nbytes
        tmp["rs"] = self.view(off, 2, 512, F32)
        off += tmp["rs"].nbytes
        return tmp, off

    def proj_residual(self, src, dst, wkeys, ACTV, nk, t0, TB, XO, alpha):
        c = self.cfg
        for dc in range(c.KC):
            w = self.load_w(wkeys(dc), nk, 128)
            j = dc % 2
            self.dma("sp", XO.ap(j, 0, TB), src.a[dc, :, t0:t0 + TB], src.r(dc, t0, t0 + TB), [XO.r(j, 0, TB)])
            for sb in range(TB // 512):
                s0, s1 = sb * 512, sb * 512 + 512
                pd = self.bank(self.G_C)
                for f in range(nk):
                    self.mm(pd, 0, 512, w.ap(f), ACTV.ap(f, s0, s1), f == 0, f == nk - 1, [w.r(f), ACTV.r(f, s0, s1)])
                self.stt(XO.ap(j, s0, s1), pd.ap(), alpha, XO.ap(j, s0, s1), ALU.mult, ALU.add,
                         [pd.r(), XO.r(j, s0, s1)], [XO.r(j, s0, s1)])
            self.dma("sp", dst.a[dc, :, t0:t0 + TB], XO.ap(j, 0, TB), [XO.r(j, 0, TB)], dst.r(dc, t0, t0 + TB))

    def ffn(self, l, pre, src, dst):
        c = self.cfg
        KC, FC, TB = c.KC, c.FC, c.TB
        NS = TB // 512
        off = self.work0
        H = self.view(off, KC, TB, BF16); off += H.nbytes
        A = self.view(off, FC, TB, BF16); off += A.nbytes
        X32 = self.view(off, KC, 256, F32); off += X32.nbytes
        tmp, off = self.mktmp(off)
        SG = self.view(off, 2, 512, F32); off += SG.nbytes
        XO = self.view(off, 2, TB, F32); off += XO.nbytes
        self.ring = (off, SB_BYTES)
        self.ring_off = 0
        assert SB_BYTES - off >= 24 * 1024, (off, SB_BYTES)
        g0 = self.vc((l, pre + "_norm"))
        for tb in range(c.NTOK // TB):
            t0 = tb * TB
            self.norm_phase(src, t0, TB, H, X32, g0, tmp)
            for f in range(FC):
                w = self.load_w(self.W(l, pre + "_gu", f), 2 * KC, 128)
                for sb in range(NS):
                    s0, s1 = sb * 512, sb * 512 + 512
                    pg = self.bank(self.G_A)
                    pu = self.bank(self.G_B)
                    for k in range(KC):
                        self.mm(pg, 0, 512, w.ap(k), H.ap(k, s0, s1), k == 0, k == KC - 1, [w.r(k), H.r(k, s0, s1)])
                    for k in range(KC):
                        self.mm(pu, 0, 512, w.ap(KC + k), H.ap(k, s0, s1), k == 0, k == KC - 1, [w.r(KC + k), H.r(k, s0, s1)])
                    j = (f * NS + sb) % 2
                    self.act(SG.ap(j), pg.ap(), AF.Silu, [pg.r()], [SG.r(j)])
                    self.tt("dve", A.ap(f, s0, s1), SG.ap(j), pu.ap(), ALU.mult, [SG.r(j), pu.r()], [A.r(f, s0, s1)])
            self.proj_residual(src, dst, lambda dc: self.W(l, pre + "_d", dc), A, FC, t0, TB, XO, 0.5)

    def rope(self, R1, R2, j, out_ap, out_r, tok0):
        p = 64
        r1, r2 = R1.ap(j, 0, 512, 0, p), R2.ap(j, 0, 512, 0, p)
        e = self.rope_eng
        self.tt(e, r1, r1, self.COS.ap(0, tok0, tok0 + 512, 0, p), ALU.mult, [R1.r(j), self.COS.r(0, tok0, tok0 + 512)], [R1.r(j)])
        self.tt(e, r2, r2, self.SINS.ap(0, tok0, tok0 + 512, 0, p), ALU.mult, [R2.r(j), self.SINS.r(0, tok0, tok0 + 512)], [R2.r(j)])
        self.tt(e, out_ap, r1, r2, ALU.add, [R1.r(j), R2.r(j)], [out_r])

    def mixA(self, l, src):
        c = self.cfg
        KC, TB, QC, KVC, NH, CC = c.KC, c.TB, c.QC, c.KVC, c.NH, c.CC
        NS = TB // 512
        off = self.work0
        H = self.view(off, KC, TB, BF16); off += H.nbytes
        xa = off
        X32 = self.view(xa, KC, 256, F32)
        CQ = self.view(xa, QC, TB, F32)
        CKV = self.view(xa + CQ.nbytes, KVC, TB, F32)
        off += max(X32.nbytes, CQ.nbytes + CKV.nbytes)
        CQN = self.view(off, QC, TB, BF16); off += CQN.nbytes
        CKVN = self.view(off, KVC, TB, BF16); off += CKVN.nbytes
        KPE = self.view(off, 2, TB, F32); off += KPE.nbytes
        tmp, off = self.mktmp(off)
        SIG = self.view(off, 2, 512, F32); off += SIG.nbytes
        AST = self.view(off, 2, TB, BF16); off += AST.nbytes
        NST = self.view(off, 2, TB, BF16); off += NST.nbytes
        RST = self.view(off, 2, TB, BF16); off += RST.nbytes
        R1 = self.view(off, 2, 512, F32); off += R1.nbytes
        R2 = self.view(off, 2, 512, F32); off += R2.nbytes
        VS = self.view(off, 2, NH * 128, BF16); off += VS.nbytes
        self.ring = (off, SB_BYTES)
        self.ring_off = 0
        assert SB_BYTES - off >= 24 * 1024, (off, SB_BYTES)
        vregs = [self.GV[p].rearrange("(h r) (a d) -> h (r a) d", h=2, d=128) for p in range(NH // 2)]
        tail = self.gt(None)
        cnt = 0
        for tb in range(c.NTOK // TB):
            t0 = tb * TB
            self.norm_phase(src, t0, TB, H, X32, self.vc((l, "mix_norm")), tmp)
            for (key, n, dstv) in (("in_cq", QC, CQ), ("in_ckv", KVC, CKV)):
                for i in range(n):
                    w = self.load_w(self.W(l, key, i), KC, 128)
                    for sb in range(NS):
                        s0, s1 = sb * 512, sb * 512 + 512
                        pb = self.bank(self.G_A)
                        for k in range(KC):
                            self.mm(pb, 0, 512, w.ap(k), H.ap(k, s0, s1), k == 0, k == KC - 1, [w.r(k), H.r(k, s0, s1)])
                        self.act(dstv.ap(i, s0, s1), pb.ap(), AF.Copy, [pb.r()], [dstv.r(i, s0, s1)])
            for i in range(2):
                w = self.load_w(self.W(l, "in_kpe", i), KC, 64)
                for sb in range(NS):
                    s0, s1 = sb * 512, sb * 512 + 512
                    pb = self.bank(self.G_B)
                    for k in range(KC):
                        self.mm(pb, 0, 512, w.ap(k), H.ap(k, s0, s1), k == 0, k == KC - 1, [w.r(k), H.r(k, s0, s1)], m=64)
                    self.act(KPE.ap(i, s0, s1, 0, 64), pb.ap(0, 512, 0, 64), AF.Copy, [pb.r()], [KPE.r(i, s0, s1)])
            for sb in range(NS):
                s0, s1 = sb * 512, sb * 512 + 512
                self.rmsnorm([(CQ.ap(i, s0, s1), CQ.r(i, s0, s1), 128) for i in range(QC)], c.QL,
                             [self.vc((l, "q_a_norm"), i) for i in range(QC)],
                             [(CQN.ap(i, s0, s1), CQN.r(i, s0, s1)) for i in range(QC)], 512, tmp)
                self.rmsnorm([(CKV.ap(i, s0, s1), CKV.r(i, s0, s1), 128) for i in range(KVC)], c.KVL,
                             [self.vc((l, "kv_a_norm"), i) for i in range(KVC)],
                             [(CKVN.ap(i, s0, s1), CKVN.r(i, s0, s1)) for i in range(KVC)], 512, tmp)
            cnt_box = [cnt]
            def conv_pair(j):
                w = self.load_w(self.W(l, "in_conv", j), 2 * KC, 128)
                jj = j % 2
                for sb in range(NS):
                    s0, s1 = sb * 512, sb * 512 + 512
                    pa, pg = self.bank((0, 2)), self.bank((2, 1))
                    for k in range(KC):
                        self.mm(pa, 0, 512, w.ap(k), H.ap(k, s0, s1), k == 0, k == KC - 1, [w.r(k), H.r(k, s0, s1)])
                    for k in range(KC):
                        self.mm(pg, 0, 512, w.ap(KC + k), H.ap(k, s0, s1), k == 0, k == KC - 1, [w.r(KC + k), H.r(k, s0, s1)])
                    js = cnt_box[0] % 2
                    cnt_box[0] += 1
                    self.act(SIG.ap(js), pg.ap(), AF.Sigmoid, [pg.r()], [SIG.r(js)])
                    self.tt("dve", AST.ap(jj, s0, s1), SIG.ap(js), pa.ap(), ALU.mult, [SIG.r(js), pa.r()], [AST.r(jj, s0, s1)])
                self.dma("sp", self.AT[j, :, 32 + t0:32 + t0 + TB], AST.ap(jj), [AST.r(jj)], [(f"AT:{j}", 32 + t0, 32 + t0 + TB)])
                if t0 + TB == c.NTOK:
                    self.dma("sp", tail[j * 128:(j + 1) * 128, :], AST.ap(jj, TB - 32, TB), [AST.r(jj)], [("GIT", j, j + 1)])
            def q_head(h):
                w = self.load_w(self.W(l, "qb", h), QC, 256)
                jj = h % 2
                for sb in range(NS):
                    s0, s1 = sb * 512, sb * 512 + 512
                    pn, pr, ps = self.bank((4, 2)), self.bank((6, 1)), self.bank((7, 1))
                    for k in range(QC):
                        self.mm(pn, 0, 512, w.ap(k, 0, 128), CQN.ap(k, s0, s1), k == 0, k == QC - 1, [w.r(k), CQN.r(k, s0, s1)])
                    for k in range(QC):
                        self.mm(pr, 0, 512, w.ap(k, 128, 192), CQN.ap(k, s0, s1), k == 0, k == QC - 1, [w.r(k), CQN.r(k, s0, s1)], m=64)
                    for k in range(QC):
                        self.mm(ps, 0, 512, w.ap(k, 192, 256), CQN.ap(k, s0, s1), k == 0, k == QC - 1, [w.r(k), CQN.r(k, s0, s1)], m=64)
                    jr = cnt_box[0] % 2
                    cnt_box[0] += 1
                    g = self.vc((l, "q_norm"))
                    self.rmsnorm([(pn.ap(), pn.r(), 128), (pr.ap(0, 512, 0, 64), pr.r(), 64), (ps.ap(0, 512, 0, 64), ps.r(), 64)],
                                 192, [g, g + 1, g + 2],
                                 [(NST.ap(jj, s0, s1), NST.r(jj, s0, s1)), (R1.ap(jr, 0, 512, 0, 64), R1.r(jr)),
                                  (R2.ap(jr, 0, 512, 0, 64), R2.r(jr))], 512, tmp, stat_n=2)
                    self.rope(R1, R2, jr, RST.ap(jj, s0, s1, 0, 64), RST.r(jj, s0, s1), t0 + s0)
                self.dma("sp", self.QT[h, 0, :, t0:t0 + TB], NST.ap(jj), [NST.r(jj)], [(f"QT:{h}", t0, t0 + TB)])
                self.dma("sp", self.QT[h, 1, 0:64, t0:t0 + TB], RST.ap(jj, 0, TB, 0, 64), [RST.r(jj)], [(f"QR:{h}", t0, t0 + TB)])
            def k_head(h):
                w = self.load_w(self.W(l, "kvk", h), KVC, 128)
                jj = h % 2
                for sb in range(NS):
                    s0, s1 = sb * 512, sb * 512 + 512
                    pn = self.bank((4, 2))
                    for k in range(KVC):
                        self.mm(pn, 0, 512, w.ap(k), CKVN.ap(k, s0, s1), k == 0, k == KVC - 1, [w.r(k), CKVN.r(k, s0, s1)])
                    jr = cnt_box[0] % 2
                    cnt_box[0] += 1
                    g = self.vc((l, "k_norm"))
                    self.rmsnorm([(pn.ap(), pn.r(), 128), (KPE.ap(0, s0, s1, 0, 64), KPE.r(0, s0, s1), 64),
                                  (KPE.ap(1, s0, s1, 0, 64), KPE.r(1, s0, s1), 64)],
                                 192, [g, g + 1, g + 2],
                                 [(NST.ap(jj, s0, s1), NST.r(jj, s0, s1)), (R1.ap(jr, 0, 512, 0, 64), R1.r(jr)),
                                  (R2.ap(jr, 0, 512, 0, 64), R2.r(jr))], 512, tmp, stat_n=2)
                    self.rope(R1, R2, jr, RST.ap(jj, s0, s1, 0, 64), RST.r(jj, s0, s1), t0 + s0)
                self.dma("sp", self.gk(None, h, 0)[:, t0:t0 + TB], NST.ap(jj), [NST.r(jj)], [(f"GIK:{h}", t0, t0 + TB)])
                self.dma("sp", self.gk(None, h, 1)[:, t0:t0 + TB], RST.ap(jj, 0, TB, 0, 64), [RST.r(jj)], [(f"GIR:{h}", t0, t0 + TB)])
            self.G_STAT = (3, 1)
            heads = [(q_head, h) for h in range(NH)] + [(k_head, h) for h in range(NH)]
            per = -(-len(heads) // CC) if MIXA_IL else 0
            for j in range(CC):
                conv_pair(j)
                for fn, h in heads[j * per:(j + 1) * per]:
                    fn(h)
            for fn, h in heads[CC * per:]:
                fn(h)
            self.G_STAT = (6, 2)
            cnt = cnt_box[0]
            wv = self.load_w(self.W(l, "kvv", 0), KVC, NH * 128)
            for tt_ in range(TB // 128):
                jj = tt_ % 2
                cbw = min(512, NH * 128)
                for cb in range(NH * 128 // cbw):
                    pv = self.bank(self.G_B)
                    for k in range(KVC):
                        self.mm(pv, 0, cbw, CKVN.ap(k, tt_ * 128, tt_ * 128 + 128), wv.ap(k, cb * cbw, cb * cbw + cbw),
                                k == 0, k == KVC - 1, [wv.r(k), CKVN.r(k, tt_ * 128, tt_ * 128 + 128)])
                    self.act(VS.ap(jj, cb * cbw, cb * cbw + cbw), pv.ap(0, cbw), AF.Copy, [pv.r(0, cbw)], [VS.r(jj, cb * cbw, cb * cbw + cbw)])
                tk = t0 + tt_ * 128
                for p in range(NH // 2):
                    self.dma("sp", vregs[p][:, tk:tk + 128, :].rearrange("h p d -> p h d"),
                             VS.t[:, jj, p * 256:(p + 1) * 256].rearrange("p (h d) -> p h d", d=128), [VS.r(jj)],
                             [(f"GIV:{2 * p}", tk, tk + 128), (f"GIV:{2 * p + 1}", tk, tk + 128)])

    def exchange(self):
        c = self.cfg
        N = c.NTOK
        rg = [[0, 1, 2, 3], [4, 5, 6, 7]]

        def ag(src, dst, reads, writes):
            self.P.add("pool", lambda h: h.collective_compute("AllGather", ALU.bypass, replica_groups=rg,
                                                             ins=[src.opt()], outs=[dst.opt()]), reads, writes, cc=True)

        for h in range(c.NH):
            reads = [(f"GIK:{h}", 0, N), (f"GIR:{h}", 0, N)] + ([("GIT", 0, c.CC)] if h == 0 else [])
            ag(self.GK[h], self.GOK[h], reads, [(f"GOK:{h}", 0, 1)])
            if h % 2 == 1:
                p = h // 2
                ag(self.GV[p], self.GOV[p], [(f"GIV:{2 * p}", 0, N), (f"GIV:{2 * p + 1}", 0, N)], [(f"GOV:{p}", 0, 1)])
        import os
        lvl = int(os.environ.get("EXLVL", "9"))
        if lvl < 1:
            return
        off = self.work0
        TL = self.view(off, 4, c.CC * 32, BF16); off += TL.nbytes
        HL = self.view(off, 1, c.CC * 32, F32); off += HL.nbytes
        HB = self.view(off, 1, c.CC * 32, BF16); off += HB.nbytes
        for r in range(4):
            self.dma("sp", TL.t[:, r, :].rearrange("p (j t) -> p j t", t=32),
                     self.gt(r).rearrange("(j p) t -> p j t", p=128), [("GOK:0", 0, 1)], [TL.r(r)])
        if lvl < 2:
            return
        s0 = self.vc("sel")
        self.ts("dve", HL.ap(0), TL.ap(0), self.vcol(s0), None, ALU.mult, None, [TL.r(0), self.vecs.r(0)], [HL.r(0)])
        for r in range(1, 4):
            self.stt(HL.ap(0), TL.ap(r), self.vcol(s0 + r), HL.ap(0), ALU.mult, ALU.add, [TL.r(r), HL.r(0), self.vecs.r(0)], [HL.r(0)])
        self.copy("dve", HB.ap(0), HL.ap(0), [HL.r(0)], [HB.r(0)])
        if lvl < 3:
            return
        self.dma("sp", self.AT[:, :, 0:32].rearrange("j p t -> p j t"), HB.t[:, 0, :].rearrange("p (j t) -> p j t", t=32),
                 [HB.r(0)], [(f"AT:{j}", 0, 32) for j in range(c.CC)])

    def mixB(self, l):
        c = self.cfg
        KC, CC, NH, N, NQB, NKB = c.KC, c.CC, c.NH, c.NTOK, c.NQB, c.NKB
        off = self.work0
        CCAT = self.view(off, KC, N, BF16); off += CCAT.nbytes
        base = off
        Y32 = self.view(off, CC, N, F32); off += Y32.nbytes
        AX = self.view(off, 2, 32 + N, BF16); off += (AX.nbytes + 31) // 32 * 32
        DG = self.view(off, 2 * 31, 128, BF16); off += DG.nbytes
        SQ = self.view(off, 4, 512, BF16); off += SQ.nbytes
        YB = self.view(off, 4, 512, BF16); off += YB.nbytes
        ST = self.view(off, 4, 512, F32); off += ST.nbytes
        assert off <= SB_BYTES, off
        cw0 = self.vc((l, "conv_w"))
        for j in range(CC):
            jj = j % 2
            self.dma("sp", AX.ap(jj), self.AT[j], [(f"AT:{j}", 0, 32 + N)], [AX.r(jj)])
            for u in range(31):
                self.ts("dve", DG.ap(jj * 31 + u), self.ident.ap(0), self.vcol(cw0 + j * 31 + u), None, ALU.mult, None,
                        [self.ident.r(0), self.vecs.r(0)], [DG.r(jj * 31 + u)])
            for qb in range(NQB):
                q0 = qb * 512
                pb = self.bank(self.G_A)
                for u in range(31):
                    self.mm(pb, 0, 512, DG.ap(jj * 31 + u), AX.ap(jj, q0 + 2 + u, q0 + 2 + u + 512), u == 0, u == 30,
                            [DG.r(jj * 31 + u), AX.r(jj, q0 + 2 + u, q0 + 2 + u + 512)])
                self.act(Y32.ap(j, q0, q0 + 512), pb.ap(), AF.Identity, [pb.r(), self.vecs.r(0)], [Y32.r(j, q0, q0 + 512)],
                         bias=self.vcol(self.vc((l, "conv_b"), j)))
        sqi = 0
        for qb in range(NQB):
            q0 = qb * 512
            bs, bq = self.bank(self.G_STAT), self.bank(self.G_STAT)
            for j in range(CC):
                s = sqi % 4
                sqi += 1
                self.act(SQ.ap(s), Y32.ap(j, q0, q0 + 512), AF.Square, [Y32.r(j, q0, q0 + 512)], [SQ.r(s)])
                self.copy("pool", YB.ap(s), Y32.ap(j, q0, q0 + 512), [Y32.r(j, q0, q0 + 512)], [YB.r(s)])
                self.mm(bs, 0, 512, self.ones.ap(0), YB.ap(s), j == 0, j == CC - 1, [self.ones.r(0), YB.r(s)])
                self.mm(bq, 0, 512, self.ones.ap(0), SQ.ap(s), j == 0, j == CC - 1, [self.ones.r(0), SQ.r(s)])
            inv = 1.0 / c.CCH
            self.ts("dve", ST.ap(0), bs.ap(), inv, None, ALU.mult, None, [bs.r()], [ST.r(0)])
            self.tt("dve", ST.ap(3), ST.ap(0), ST.ap(0), ALU.mult, [ST.r(0)], [ST.r(3)])
            self.stt(ST.ap(1), bq.ap(), inv, ST.ap(3), ALU.mult, ALU.subtract, [bq.r(), ST.r(3)], [ST.r(1)])
            self.act(ST.ap(1), ST.ap(1), AF.Sqrt, [ST.r(1), self.epsc.r(0)], [ST.r(1)], bias=self.epsc.ap(0, 0, 1))
            self.recip(ST.ap(2), ST.ap(1), [ST.r(1)], [ST.r(2)])
            for j in range(CC):
                y = Y32.ap(j, q0, q0 + 512)
                yr = Y32.r(j, q0, q0 + 512)
                self.tt("dve", y, y, ST.ap(0), ALU.subtract, [yr, ST.r(0)], [yr])
                self.tt("dve", y, y, ST.ap(2), ALU.mult, [yr, ST.r(2)], [yr])
                self.act(CCAT.ap(j, q0, q0 + 512), y, AF.Silu, [yr, self.vecs.r(0)], [CCAT.r(j, q0, q0 + 512)],
                         scale=self.vcol(self.vc((l, "conv_ln_g"), j)), bias=self.vcol(self.vc((l, "conv_ln_b"), j)))
        off = base
        QHN = self.view(off, 2, N, BF16); off += QHN.nbytes
        QHR = self.view(off, 2, N, BF16); off += QHR.nbytes
        KN = self.view(off, 2, N, BF16); off += KN.nbytes
        KR = self.view(off, 2, N, BF16); off += KR.nbytes
        VV = self.view(off, 2, N, BF16); off += VV.nbytes
        PT = self.view(off, 4, 512, BF16); off += PT.nbytes
        OA = self.view(off, NQB, 512, F32); off += OA.nbytes
        PA = self.view(off, 2 * NQB, 512, F32); off += PA.nbytes
        ONEF = self.view(off, 1, 128, F32); off += ONEF.nbytes
        self.memset("dve", ONEF.ap(0), 1.0, [ONEF.r(0)])
        RD = self.view(off, 2, 512, F32); off += RD.nbytes
        XO = self.view(off, 2, N, F32); off += XO.nbytes
        self.ring = (off, SB_BYTES)
        self.ring_off = 0
        assert SB_BYTES - off >= 16 * 1024, (off, SB_BYTES)
        scale = 192 ** -0.5
        pti = 0
        srcs = [0, 1, 2, None]
        for h in range(NH):
            jq = h % 2
            self.dma("sp", QHN.ap(jq), self.QT[h, 0], [(f"QT:{h}", 0, N)], [QHN.r(jq)])
            self.dma("sp", QHR.ap(jq, 0, N, 0, 64), self.QT[h, 1, 0:64, :], [(f"QR:{h}", 0, N)], [QHR.r(jq)])
            for si, rk in enumerate(srcs):
                jk = (h * len(srcs) + si) % 2
                own = rk is None
                self.dma("sp", KN.ap(jk), self.gk(rk, h, 0), [(f"GIK:{h}", 0, N)] if own else [(f"GOK:{h}", 0, 1)], [KN.r(jk)])
                self.dma("sp", KR.ap(jk, 0, N, 0, 64), self.gk(rk, h, 1), [(f"GIR:{h}", 0, N)] if own else [(f"GOK:{h}", 0, 1)], [KR.r(jk)])
                self.dma("sp", VV.t[:, jk, :].rearrange("p (j d) -> p j d", d=128),
                         self.gv(rk, h).rearrange("(j p) d -> p j d", p=128),
                         [(f"GIV:{h}", 0, N)] if own else [(f"GOV:{h // 2}", 0, 1)], [VV.r(jk)])
                for qb in range(NQB):
                    q0 = qb * 512
                    bo = self.bank(self.G_B)
                    bd_box = [None]
                    items = []
                    for kb in range(NKB if not own else 4 * qb + 4):
                        c0 = 128 * (kb - 4 * qb) if (own and kb >= 4 * qb) else 0
                        items.append((kb, c0, own and kb >= 4 * qb))

                    def emit_s(kb, c0):
                        bsb = self.bank(self.G_A)
                        k0 = kb * 128
                        self.mm(bsb, c0, 512, KN.ap(jk, k0, k0 + 128), QHN.ap(jq, q0 + c0, q0 + 512), True, False,
                                [KN.r(jk, k0, k0 + 128), QHN.r(jq, q0 + c0, q0 + 512)])
                        self.mm(bsb, c0, 512, KR.ap(jk, k0, k0 + 128, 0, 64), QHR.ap(jq, q0 + c0, q0 + 512, 0, 64), False, True,
                                [KR.r(jk, k0, k0 + 128), QHR.r(jq, q0 + c0, q0 + 512)])
                        return bsb

                    nxt = emit_s(items[0][0], items[0][1])
                    for i, (kb, c0, diag) in enumerate(items):
                        bsb = nxt
                        if i + 1 < len(items):
                            nxt = emit_s(items[i + 1][0], items[i + 1][1])
                        p = pti % 4
                        pti += 1
                        if own:
                            self.act(PT.ap(p, c0, 512), bsb.ap(c0, 512), AF.Exp, [bsb.r(c0, 512)], [PT.r(p, c0, 512)], scale=scale)
                        else:
                            self.act(PT.ap(p, c0, 512), bsb.ap(c0, 512), AF.Exp, [bsb.r(c0, 512), self.vecs.r(0)], [PT.r(p, c0, 512)],
                                     scale=scale, bias=self.vcol(self.vc("visb") + rk))
                        if diag:
                            self.memset("dve", PT.ap(p, c0, c0 + 64, 64, 128), 0.0, [PT.r(p, c0, c0 + 64)])
                        first, last = i == 0, i == len(items) - 1
                        self.mm(bo, c0, 512, VV.ap(jk, kb * 128, kb * 128 + 128), PT.ap(p, c0, 512), first, last,
                                [VV.r(jk, kb * 128, kb * 128 + 128), PT.r(p, c0, 512)])
                        if DEN_MODE == "pe":
                            if first:
                                bd_box[0] = self.bank(self.G_C)
                            self.mm(bd_box[0], c0, 512, self.ones.ap(0), PT.ap(p, c0, 512), first, last, [self.ones.r(0), PT.r(p, c0, 512)])
                            if last:
                                pa = qb
                                if si == 0:
                                    self.copy("dve", PA.ap(pa), bd_box[0].ap(), [bd_box[0].r()], [PA.r(pa)])
                                else:
                                    self.tt("dve", PA.ap(pa), PA.ap(pa), bd_box[0].ap(), ALU.add, [PA.r(pa), bd_box[0].r()], [PA.r(pa)])
                            continue
                        e = i % 2
                        eng = "dve" if (e == 0 or DEN_MODE == "dve") else "pool"
                        pa = e * NQB + qb
                        if si == 0 and i < 2:
                            self.copy(eng, PA.ap(pa), PT.ap(p), [PT.r(p)], [PA.r(pa)])
                        else:
                            self.tt(eng, PA.ap(pa, c0, 512), PA.ap(pa, c0, 512), PT.ap(p, c0, 512), ALU.add,
                                    [PA.r(pa, c0, 512), PT.r(p, c0, 512)], [PA.r(pa, c0, 512)])
                    if si == 0:
                        self.act(OA.ap(qb), bo.ap(), AF.Copy, [bo.r()], [OA.r(qb)])
                    else:
                        self.tt("dve", OA.ap(qb), OA.ap(qb), bo.ap(), ALU.add, [OA.r(qb), bo.r()], [OA.r(qb)])
            for qb in range(NQB):
                q0 = qb * 512
                jr = qb % 2
                if DEN_MODE == "pe":
                    self.recip(RD.ap(jr), PA.ap(qb), [PA.r(qb)], [RD.r(jr)])
                else:
                    self.tt("dve", PA.ap(qb), PA.ap(qb), PA.ap(NQB + qb), ALU.add, [PA.r(qb), PA.r(NQB + qb)], [PA.r(qb)])
                    bd = self.bank(self.G_C)
                    self.mm(bd, 0, 512, ONEF.ap(0), PA.ap(qb), True, True, [ONEF.r(0), PA.r(qb)])
                    self.recip(RD.ap(jr), bd.ap(), [bd.r()], [RD.r(jr)])
                self.tt("dve", CCAT.ap(CC + h, q0, q0 + 512), OA.ap(qb), RD.ap(jr), ALU.mult, [OA.r(qb), RD.r(jr)],
                        [CCAT.r(CC + h, q0, q0 + 512)])
        import os
        if os.environ.get("DBG") == "ccat":
            for k in range(KC):
                self.dma("pool", self.outT.a[k], CCAT.ap(k), [CCAT.r(k)], self.outT.r(k, 0, N))
            self.dbg_done = True
            return
        self.proj_residual(self.XT, self.XT, lambda dc: self.W(l, "wout", dc), CCAT, KC, 0, N, XO, 1.0)

    def cross(self, l):
        c = self.cfg
        KC, TB, XH, XHC, NM = c.KC, c.TB, c.XH, c.XHC, c.NMEM
        NS = TB // 512
        MT = NM // 128
        off = self.work0
        KCN = self.view(off, KC, NM, BF16); off += KCN.nbytes
        VC = self.view(off, MT, c.D, BF16); off += VC.nbytes
        tmp, off = self.mktmp(off)
        base = off
        M32 = self.view(off, KC, NM, F32); off += M32.nbytes
        MN = self.view(off, KC, NM, BF16); off += MN.nbytes
        K32 = self.view(off, KC, NM, F32); off += K32.nbytes
        self.ring = (off, SB_BYTES)
        self.ring_off = 0
        self.dma("sp", M32.t, self.memT.rearrange("k p t -> p k t"), [], [M32.r(0, n=KC)])
        self.rmsnorm([(M32.ap(k), M32.r(k), 128) for k in range(KC)], c.D, [self.vc((l, "mem_norm"), k) for k in range(KC)],
                     [(MN.ap(k), MN.r(k)) for k in range(KC)], NM, tmp)
        for oc in range(KC):
            w = self.load_w(self.W(l, "wck", oc), KC, 128)
            pb = self.bank(self.G_A)
            for k in range(KC):
                self.mm(pb, 0, NM, w.ap(k), MN.ap(k), k == 0, k == KC - 1, [w.r(k), MN.r(k)])
            self.act(K32.ap(oc), pb.ap(0, NM), AF.Copy, [pb.r(0, NM)], [K32.r(oc)])
        for hh in range(XH):
            ch = [hh * XHC + i for i in range(XHC)]
            self.rmsnorm([(K32.ap(k), K32.r(k), 128) for k in ch], c.XHD, [self.vc((l, "ck_norm"), i) for i in range(XHC)],
                         [(KCN.ap(k), KCN.r(k)) for k in ch], NM, tmp)
        for cb in range(c.D // 512):
            w = self.load_w(self.W(l, "wcv", cb), KC, 512)
            for mt in range(MT):
                pb = self.bank(self.G_B)
                for k in range(KC):
                    self.mm(pb, 0, 512, MN.ap(k, mt * 128, mt * 128 + 128), w.ap(k), k == 0, k == KC - 1,
                            [w.r(k), MN.r(k, mt * 128, mt * 128 + 128)])
                self.act(VC.ap(mt, cb * 512, cb * 512 + 512), pb.ap(), AF.Copy, [pb.r()], [VC.r(mt, cb * 512, cb * 512 + 512)])
        off = base
        H = self.view(off, KC, TB, BF16); off += H.nbytes
        OC = self.view(off, KC, TB, BF16); off += OC.nbytes
        X32 = self.view(off, KC, 256, F32); off += X32.nbytes
        Q32 = self.view(off, 2 * XHC, TB, F32); off += Q32.nbytes
        QN = self.view(off, XHC, TB, BF16); off += QN.nbytes
        PT = self.view(off, 2 * MT, 512, BF16); off += PT.nbytes
        RD = self.view(off, 2, 512, F32); off += RD.nbytes
        XO = self.view(off, 2, TB, F32); off += XO.nbytes
        self.ring = (off, SB_BYTES)
        self.ring_off = 0
        assert SB_BYTES - off >= 16 * 1024, (off, SB_BYTES)
        scale = c.XHD ** -0.5
        G_S, G_O, G_D, G_P = (0, 2), (2, 2), (4, 1), (6, 2)
        pti = [0]
        for tb in range(c.NTOK // TB):
            t0 = tb * TB
            self.norm_phase(self.XT, t0, TB, H, X32, self.vc((l, "cross_norm")), tmp)

            def proj(hh):
                par = (hh % 2) * XHC
                for i in range(XHC):
                    w = self.load_w(self.W(l, "wcq", hh * XHC + i), KC, 128)
                    for sb in range(NS):
                        s0, s1 = sb * 512, sb * 512 + 512
                        pb = self.bank(G_P)
                        for k in range(KC):
                            self.mm(pb, 0, 512, w.ap(k), H.ap(k, s0, s1), k == 0, k == KC - 1, [w.r(k), H.r(k, s0, s1)])
                        self.act(Q32.ap(par + i, s0, s1), pb.ap(), AF.Copy, [pb.r()], [Q32.r(par + i, s0, s1)])

            def attn(hh):
                par = (hh % 2) * XHC
                self.G_STAT = (5, 1)
                for sb in range(NS):
                    s0, s1 = sb * 512, sb * 512 + 512
                    self.rmsnorm([(Q32.ap(par + i, s0, s1), Q32.r(par + i, s0, s1), 128) for i in range(XHC)], c.XHD,
                                 [self.vc((l, "cq_norm"), i) for i in range(XHC)],
                                 [(QN.ap(i, s0, s1), QN.r(i, s0, s1)) for i in range(XHC)], 512, tmp)
                self.G_STAT = (6, 2)
                pts = {}
                for sb in range(NS):
                    s0, s1 = sb * 512, sb * 512 + 512
                    for mt in range(MT):
                        bsb = self.bank(G_S)
                        for i in range(XHC):
                            kc = hh * XHC + i
                            self.mm(bsb, 0, 512, KCN.ap(kc, mt * 128, mt * 128 + 128), QN.ap(i, s0, s1), i == 0, i == XHC - 1,
                                    [KCN.r(kc, mt * 128, mt * 128 + 128), QN.r(i, s0, s1)])
                        p = pti[0] % (2 * MT)
                        pti[0] += 1
                        self.act(PT.ap(p), bsb.ap(), AF.Exp, [bsb.r()], [PT.r(p)], scale=scale)
                        pts[(sb, mt)] = p
                for sb in range(NS):
                    bd = self.bank(G_D)
                    for mt in range(MT):
                        self.mm(bd, 0, 512, self.ones.ap(0), PT.ap(pts[(sb, mt)]), mt == 0, mt == MT - 1,
                                [self.ones.r(0), PT.r(pts[(sb, mt)])])
                    self.recip(RD.ap(sb % 2), bd.ap(), [bd.r()], [RD.r(sb % 2)])
                for sb in range(NS):
                    s0, s1 = sb * 512, sb * 512 + 512
                    for dv in range(XHC):
                        oc = hh * XHC + dv
                        bo = self.bank(G_O)
                        for mt in range(MT):
                            self.mm(bo, 0, 512, VC.ap(mt, oc * 128, oc * 128 + 128), PT.ap(pts[(sb, mt)]), mt == 0, mt == MT - 1,
                                    [VC.r(mt, oc * 128, oc * 128 + 128), PT.r(pts[(sb, mt)])])
                        self.tt("dve", OC.ap(oc, s0, s1), bo.ap(), RD.ap(sb % 2), ALU.mult, [bo.r(), RD.r(sb % 2)], [OC.r(oc, s0, s1)])

            proj(0)
            for hh in range(XH):
                if hh + 1 < XH:
                    proj(hh + 1)
                attn(hh)
            self.proj_residual(self.XT, self.XT, lambda dc: self.W(l, "wco", dc), OC, KC, t0, TB, XO, 1.0)

    def build(self, upto=None):
        c = self.cfg
        with self.st:
            self.setup()
            stages = []
            for l in range(c.DEPTH):
                last = l == c.DEPTH - 1
                stages += [lambda l=l: self.ffn(l, "ffn1", self.xT if l == 0 else self.XT, self.XT),
                           lambda l=l: self.mixA(l, self.XT),
                           lambda l=l: self.exchange(),
                           lambda l=l: self.mixB(l),
                           lambda l=l: self.cross(l),
                           lambda l=l, last=last: self.ffn(l, "ffn2", self.XT, self.outT if last else self.XT)]
            if upto is not None:
                stages = stages[:upto]
            for s in stages:
                s()
            import os
            dbg = os.environ.get("DBG", "")
            if dbg in ("q", "k"):
                flat = self.outT.a.rearrange("k p t -> (k p) t")
                for h in range(c.NH):
                    if dbg == "q":
                        srcs = [(self.QT[h, 0], f"QT:{h}"), (self.QT[h, 1, 0:64, :], f"QR:{h}")]
                    else:
                        srcs = [(self.gk(None, h, 0), f"GIK:{h}"), (self.gk(None, h, 1), f"GIR:{h}")]
                    self.dma("pool", flat[h * 192:h * 192 + 128, :], srcs[0][0], [(srcs[0][1], 0, c.NTOK)], [("dbgout", 2 * h, 2 * h + 1)])
                    self.dma("pool", flat[h * 192 + 128:h * 192 + 192, :], srcs[1][0], [(srcs[1][1], 0, c.NTOK)], [("dbgout", 2 * h + 1, 2 * h + 2)])
                self.dbg_done = True
            if upto is not None and upto != c.DEPTH * 6 and not getattr(self, "dbg_done", False):
                for k in range(c.KC):
                    self.dma("sp", self.outT.a[k], self.XT.a[k], self.XT.r(k, 0, c.NTOK), self.outT.r(k, 0, c.NTOK))
            self.P.emit()


def prepare_inputs(c, inputs):
    x = np.asarray(inputs["x"])
    mem = np.asarray(inputs["mem"])
    pos = np.asarray(inputs["positions"])
    B, S, D = x.shape
    ncore = B * S // c.NTOK
    per_b = S // c.NTOK
    wflat = pack_weights(c, inputs)
    ident = np.eye(128, dtype=np.float32)
    maps = []
    for core in range(ncore):
        b, r = divmod(core, per_b)
        xs = x[b, r * c.NTOK:(r + 1) * c.NTOK, :]
        maps.append({
            "xT": np.ascontiguousarray(xs.T).reshape(c.KC, 128, c.NTOK),
            "memT": np.ascontiguousarray(mem[b].T).reshape(c.KC, 128, c.NMEM),
            "pos": np.ascontiguousarray(pos[b, r * c.NTOK:(r + 1) * c.NTOK]).reshape(1, c.NTOK).astype(np.int32),
            "wflat": wflat,
            "vecs": pack_vecs(c, inputs, r),
            "ident": ident,
        })
    return maps


def assemble_output(c, results, B, S):
    per_b = S // c.NTOK
    out = np.empty((B, S, c.D), np.float32)
    for core, r in enumerate(results):
        b, rk = divmod(core, per_b)
        out[b, rk * c.NTOK:(rk + 1) * c.NTOK, :] = np.asarray(r["outT"]).reshape(c.D, c.NTOK).T
    return out


_CACHE = {}


def kernel(**inputs):
    c = Cfg()
    if "nc" not in _CACHE:
        nc = bass.Bass("TRN2", target_bir_lowering=False)
        Full(nc, c).build()
        _CACHE["nc"] = nc
    nc = _CACHE["nc"]
    maps = prepare_inputs(c, inputs)
    B, S, _ = np.asarray(inputs["x"]).shape
    res = run_bass_kernel_spmd(nc, maps, core_ids=list(range(len(maps))))
    return assemble_output(c, res.results, B, S)
```

```python
import contextlib
import os
import numpy as np
import concourse.bass as bass
import concourse.mybir as mybir
from concourse.bass_utils import run_bass_kernel_spmd

F32 = mybir.dt.float32
BF16 = mybir.dt.bfloat16
I32 = mybir.dt.int32
AF = mybir.ActivationFunctionType
ALU = mybir.AluOpType

COMPUTE = ("pe", "act", "dve", "pool")
QUEUES = ("sp", "act", "pool")
NDSEM = 6


class _Op:
    __slots__ = ("eng", "fn", "deps", "ddeps", "dma", "cc", "q_idx", "idx", "milestone", "count")

    def __init__(self, eng, fn, dma, cc):
        self.eng = eng
        self.fn = fn
        self.dma = dma
        self.cc = cc
        self.deps = {}
        self.ddeps = {}
        self.milestone = False
        self.count = 0
        self.q_idx = -1


class Prog:
    def __init__(self, nc):
        self.nc = nc
        self.ops = {e: [] for e in ("pe", "act", "dve", "pool", "sp")}
        self.track = {}
        self.ndma = {q: 0 for q in QUEUES}
        self.ncc = 0

    def _segs(self, space, lo, hi):
        segs = self.track.setdefault(space, [[0, 1 << 60, None, []]])
        out = []
        i = 0
        while i < len(segs):
            s = segs[i]
            if s[1] <= lo:
                i += 1
                continue
            if s[0] >= hi:
                break
            if s[0] < lo:
                segs.insert(i, [s[0], lo, s[2], list(s[3])])
                s[0] = lo
                i += 1
                continue
            if s[1] > hi:
                segs.insert(i + 1, [hi, s[1], s[2], list(s[3])])
                s[1] = hi
            out.append(s)
            i += 1
        return out

    def add(self, eng, fn, reads=(), writes=(), dma=False, cc=False):
        op = _Op(eng, fn, dma, cc)
        lst = self.ops[eng]
        op.idx = len(lst)
        me = (eng, op.idx)
        deps, ddeps, ops = op.deps, op.ddeps, self.ops

        def dep(w):
            if w is None or w == me:
                return
            e, i = w
            t = ops[e][i]
            if t.dma:
                k = (e, t.q_idx % NDSEM)
                if ddeps.get(k, -1) < i:
                    ddeps[k] = i
            elif t.cc:
                if ddeps.get("cc", -1) < i:
                    ddeps["cc"] = i
            elif deps.get(e, -1) < i:
                deps[e] = i

        for (space, lo, hi) in reads:
            for s in self._segs(space, lo, hi):
                dep(s[2])
                s[3].append(me)
        for (space, lo, hi) in writes:
            for s in self._segs(space, lo, hi):
                dep(s[2])
                for r in s[3]:
                    dep(r)
                s[2] = me
                s[3] = []
        if dma:
            op.q_idx = self.ndma[eng]
            self.ndma[eng] += 1
        if cc:
            op.q_idx = self.ncc
            self.ncc += 1
        lst.append(op)
        return op

    def emit(self):
        nc = self.nc
        ops = self.ops
        for e, lst in ops.items():
            for op in lst:
                if e in op.deps and not (op.dma or op.cc):
                    i = op.deps[e]
                    if e == "pe":
                        del op.deps[e]
                    elif op.idx - i > 2:
                        del op.deps[e]
                for de, di in op.deps.items():
                    ops[de][di].milestone = True
        for e, lst in ops.items():
            c = 0
            for op in lst:
                if op.milestone:
                    c += 1
                op.count = c
        with contextlib.ExitStack() as st:
            csem = {e: st.enter_context(nc.semaphore("c_" + e)) for e in COMPUTE}
            dsem = {q: [st.enter_context(nc.semaphore(f"d_{q}{k}")) for k in range(NDSEM)]
                    for q in QUEUES}
            ccsem = st.enter_context(nc.semaphore("ccsem"))
            block = st.enter_context(nc.Block())

            def run(e, h):
                waited = {}

                def wait(sem, key, val):
                    if waited.get(key, 0) >= val:
                        return
                    waited[key] = val
                    h.wait_ge(sem, val)

                for op in ops[e]:
                    for de, di in op.deps.items():
                        wait(csem[de], de, ops[de][di].count)
                    for key, di in op.ddeps.items():
                        if key == "cc":
                            wait(ccsem, "cc", ops["pool"][di].q_idx + 1)
                        else:
                            q, k = key
                            wait(dsem[q][k], key, 16 * (ops[q][di].q_idx // NDSEM + 1))
                    if op.dma:
                        k = op.q_idx % NDSEM
                        if op.q_idx >= NDSEM:
                            wait(dsem[e][k], (e, k), 16 * (op.q_idx // NDSEM))
                        op.fn(h).then_inc(dsem[e][k], 16)
                    elif op.cc:
                        op.fn(h).then_inc(ccsem, 1)
                    else:
                        ins = op.fn(h)
                        if op.milestone:
                            ins.then_inc(csem[e], 1)
                if e in QUEUES:
                    n = self.ndma[e]
                    for k in range(min(NDSEM, n)):
                        cnt = (n - 1 - k) // NDSEM + 1
                        wait(dsem[e][k], (e, k), 16 * cnt)
                if e == "pool" and self.ncc:
                    wait(ccsem, "cc", self.ncc)

            @block.tensor
            def _(h):
                run("pe", h)

            @block.scalar
            def _(h):
                run("act", h)

            @block.vector
            def _(h):
                run("dve", h)

            @block.gpsimd
            def _(h):
                run("pool", h)

            @block.sync
            def _(h):
                run("sp", h)


class View:
    def __init__(self, arena, name, off_bytes, n, w, dtype):
        self.esz = 4 if dtype in (F32, I32) else 2
        assert off_bytes % 4 == 0
        self.space = name
        self.base = off_bytes // 2
        self.n, self.w, self.dtype = n, w, dtype
        u = n * w * self.esz // 2
        v = arena[:, self.base:self.base + u]
        if dtype != BF16:
            v = v.bitcast(dtype)
        self.t = v.rearrange("p (n w) -> p n w", n=n)
        self.nbytes = n * w * self.esz

    def r(self, i, lo=0, hi=None, n=1):
        hi = self.w if hi is None else hi
        u = self.esz // 2 if self.esz > 1 else 1
        if n == 1:
            return (self.space, self.base + (i * self.w + lo) * self.esz // 2,
                    self.base + (i * self.w + hi) * self.esz // 2)
        return (self.space, self.base + i * self.w * self.esz // 2,
                self.base + (i + n) * self.w * self.esz // 2)

    def ap(self, i, lo=0, hi=None, p0=0, p1=128):
        hi = self.w if hi is None else hi
        return self.t[p0:p1, i, lo:hi]

    def ap3(self, i, n, p0=0, p1=128):
        return self.t[p0:p1, i:i + n, :]


class PsumBank:
    def __init__(self, st, nc, name):
        self.name = name
        self.t = st.enter_context(nc.psum_tensor(name, [128, 512], F32))

    def r(self, lo=0, hi=512):
        return (self.name, lo, hi)

    def ap(self, lo=0, hi=512, p0=0, p1=128):
        return self.t[p0:p1, lo:hi]


class DT:
    def __init__(self, ap, name):
        self.a = ap
        self.name = name

    def r(self, c, lo, hi, n=1):
        return [(f"{self.name}:{c + j}", lo, hi) for j in range(n)]


class Cfg:
    def __init__(self, **kw):
        self.D = 2048
        self.F = 5632
        self.NTOK = 2048
        self.TB = 1024
        self.EPS = 1e-6
        self.NH = 8
        self.QL = 768
        self.KVL = 256
        self.XH = 4
        self.NMEM = 256
        self.DEPTH = 2
        self.__dict__.update(kw)
        self.KC = self.D // 128
        self.FC = self.F // 128
        self.CCH = self.D // 2
        self.CC = self.CCH // 128
        self.QC = self.QL // 128
        self.KVC = self.KVL // 128
        self.XHD = self.D // self.XH
        self.XHC = self.XHD // 128
        self.NQB = self.NTOK // 512
        self.NKB = self.NTOK // 128
        self.R_K = self.NH * 192
        self.R_V = self.NH * 128
        self.R_T = self.CCH * 32 // self.NTOK
        self.R = self.R_K + self.R_V + self.R_T


SB_BYTES = 206 * 1024
DEN_MODE = os.environ.get("DEN_MODE", "pe")
MIXA_IL = int(os.environ.get("MIXA_IL", "0"))


class Builder:
    def __init__(self, nc, cfg):
        self.nc = nc
        self.cfg = cfg
        self.st = contextlib.ExitStack()
        self.P = Prog(nc)
        self.arena = self.st.enter_context(nc.sbuf_tensor("arena", [128, SB_BYTES // 2], BF16))
        self.banks = [PsumBank(self.st, nc, f"ps{i}") for i in range(8)]
        self.bank_rr = {}
        self.ring_off = 0
        self.dram = {}

    def view(self, off, n, w, dtype):
        return View(self.arena, "arena", off, n, w, dtype)

    def bank(self, grp):
        lo, n = grp
        k = self.bank_rr.get(grp, 0)
        self.bank_rr[grp] = k + 1
        return self.banks[lo + k % n]

    def ring_alloc(self, nbytes):
        r0, r1 = self.ring
        if self.ring_off + nbytes > r1 - r0:
            self.ring_off = 0
        off = r0 + self.ring_off
        self.ring_off += (nbytes + 31) // 32 * 32
        return off

    def load_w(self, src_ap, n, w, reads=()):
        off = self.ring_alloc(n * w * 2)
        v = self.view(off, n, w, BF16)
        self.P.add("pool", lambda h: h.dma_start(out=v.t, in_=src_ap.rearrange("p (n w) -> p n w", n=n),
                                                 max_dma_last_dim=8192),
                   reads=list(reads), writes=[v.r(0, n=n)], dma=True)
        return v

    def mm(self, bank, lo, hi, lhsT, rhs, start, stop, reads, m=128):
        self.P.add("pe", lambda h: h.matmul(bank.ap(lo, hi, 0, m), lhsT, rhs, start=start, stop=stop),
                   reads=reads, writes=[bank.r(lo, hi)])

    def setup_consts(self, off, vecs_ap, nv):
        self.ones = self.view(off, 1, 128, BF16)
        off += 256
        self.epsc = self.view(off, 1, 8, F32)
        off += 32
        self.vecs = self.view(off, 1, nv, F32)
        off += nv * 4
        self.memset("dve", self.ones.ap(0), 1.0, [self.ones.r(0)])
        self.memset("dve", self.epsc.ap(0), self.cfg.EPS, [self.epsc.r(0)])
        self.dma("sp", self.vecs.ap(0), vecs_ap, [], [self.vecs.r(0)])
        return off

    def vcol(self, c, p0=0, p1=128):
        return self.vecs.ap(0, c, c + 1, p0, p1)

    def dma(self, q, out, in_, reads, writes, **kw):
        return self.P.add(q, lambda h: h.dma_start(out=out, in_=in_, **kw), reads, writes, dma=True)

    def act(self, out, in_, func, reads, writes, **kw):
        return self.P.add("act", lambda h: h.activation(out=out, in_=in_, func=func, **kw), reads, writes)

    def tt(self, eng, out, in0, in1, op, reads, writes):
        return self.P.add(eng, lambda h: h.tensor_tensor(out=out, in0=in0, in1=in1, op=op), reads, writes)

    def stt(self, out, in0, scalar, in1, op0, op1, reads, writes):
        return self.P.add("dve", lambda h: h.scalar_tensor_tensor(out=out, in0=in0, scalar=scalar, in1=in1,
                                                                  op0=op0, op1=op1), reads, writes)

    def ts(self, eng, out, in0, s1, s2, op0, op1, reads, writes):
        if op1 is None:
            return self.P.add(eng, lambda h: h.tensor_scalar(out=out, in0=in0, scalar1=s1, scalar2=None, op0=op0),
                              reads, writes)
        return self.P.add(eng, lambda h: h.tensor_scalar(out=out, in0=in0, scalar1=s1, scalar2=s2, op0=op0, op1=op1),
                          reads, writes)

    def recip(self, out, in_, reads, writes):
        return self.P.add("dve", lambda h: h.reciprocal(out=out, in_=in_), reads, writes)

    def copy(self, eng, out, in_, reads, writes):
        return self.P.add(eng, lambda h: h.tensor_copy(out=out, in_=in_), reads, writes)

    def memset(self, eng, out, val, writes):
        return self.P.add(eng, lambda h: h.memset(out, val), (), writes)

    def rmsnorm(self, xs, D, gcols, outs, W, tmp, stat_n=None):
        bank = self.bank(self.G_STAT)
        sqv = tmp["sq"]
        n = len(xs) if stat_n is None else stat_n
        for k, (xap, xr, p) in enumerate(xs[:n]):
            s = tmp["sq_i"] % sqv.n
            tmp["sq_i"] += 1
            self.act(sqv.ap(s, 0, W, 0, p), xap, AF.Square, [xr], [sqv.r(s, 0, W)])
            self.mm(bank, 0, W, self.ones.ap(0, 0, 128, 0, p), sqv.ap(s, 0, W, 0, p), k == 0, k == n - 1,
                    [self.ones.r(0), sqv.r(s, 0, W)])
        rs = tmp["rs"]
        j = tmp["rs_i"] % rs.n
        tmp["rs_i"] += 1
        self.act(rs.ap(j, 0, W), bank.ap(0, W), AF.Sqrt, [bank.r(0, W), self.epsc.r(0)], [rs.r(j, 0, W)],
                 bias=self.epsc.ap(0, 0, 1), scale=1.0 / D)
        self.recip(rs.ap(j, 0, W), rs.ap(j, 0, W), [rs.r(j, 0, W)], [rs.r(j, 0, W)])
        for k, (xap, xr, p) in enumerate(xs):
            oap, orr = outs[k]
            if gcols is None:
                self.tt("dve", oap, xap, rs.ap(j, 0, W, 0, p), ALU.mult, [xr, rs.r(j, 0, W)], [orr])
            else:
                self.stt(oap, xap, self.vcol(gcols[k], 0, p), rs.ap(j, 0, W, 0, p), ALU.mult, ALU.mult,
                         [xr, rs.r(j, 0, W), self.vecs.r(0)], [orr])
        return rs, j

    def ffn(self, src, dst, wgu, wd, gcol0):
        c = self.cfg
        KC, FC, TB = c.KC, c.FC, c.TB
        NS = TB // 512
        off = self.work0
        H = self.view(off, KC, TB, BF16); off += H.nbytes
        A = self.view(off, FC, TB, BF16); off += A.nbytes
        X32 = self.view(off, KC, 512, F32); off += X32.nbytes
        tmp = {"sq": self.view(off, 4, 512, BF16), "sq_i": 0, "rs_i": 0}; off += tmp["sq"].nbytes
        tmp["rs"] = self.view(off, 2, 512, F32); off += tmp["rs"].nbytes
        SG = self.view(off, 2, 512, F32); off += SG.nbytes
        XO = self.view(off, 2, TB, F32); off += XO.nbytes
        self.ring = (off, SB_BYTES)
        self.ring_off = 0
        assert SB_BYTES - off >= 24 * 1024, (off, SB_BYTES)
        for tb in range(c.NTOK // TB):
            t0 = tb * TB
            for sb in range(NS):
                c0 = t0 + sb * 512
                self.dma("sp", X32.t, src.a[:, :, c0:c0 + 512].rearrange("k p t -> p k t"),
                         src.r(0, c0, c0 + 512, n=KC), [X32.r(0, n=KC)])
                xs = [(X32.ap(k), X32.r(k), 128) for k in range(KC)]
                outs = [(H.ap(k, sb * 512, sb * 512 + 512), H.r(k, sb * 512, sb * 512 + 512)) for k in range(KC)]
                self.rmsnorm(xs, c.D, [gcol0 + k for k in range(KC)], outs, 512, tmp)
            for f in range(FC):
                w = self.load_w(wgu[f], 2 * KC, 128)
                for sb in range(NS):
                    s0, s1 = sb * 512, sb * 512 + 512
                    pg = self.bank(self.G_A)
                    pu = self.bank(self.G_B)
                    for k in range(KC):
                        self.mm(pg, 0, 512, w.ap(k), H.ap(k, s0, s1), k == 0, k == KC - 1, [w.r(k), H.r(k, s0, s1)])
                    for k in range(KC):
                        self.mm(pu, 0, 512, w.ap(KC + k), H.ap(k, s0, s1), k == 0, k == KC - 1, [w.r(KC + k), H.r(k, s0, s1)])
                    j = (f * NS + sb) % 2
                    self.act(SG.ap(j), pg.ap(), AF.Silu, [pg.r()], [SG.r(j)])
                    self.tt("dve", A.ap(f, s0, s1), SG.ap(j), pu.ap(), ALU.mult, [SG.r(j), pu.r()], [A.r(f, s0, s1)])
            for dc in range(KC):
                w = self.load_w(wd[dc], FC, 128)
                j = dc % 2
                self.dma("sp", XO.ap(j), src.a[dc, :, t0:t0 + TB], src.r(dc, t0, t0 + TB), [XO.r(j)])
                for sb in range(NS):
                    s0, s1 = sb * 512, sb * 512 + 512
                    pd = self.bank(self.G_C)
                    for f in range(FC):
                        self.mm(pd, 0, 512, w.ap(f), A.ap(f, s0, s1), f == 0, f == FC - 1, [w.r(f), A.r(f, s0, s1)])
                    self.stt(XO.ap(j, s0, s1), pd.ap(), 0.5, XO.ap(j, s0, s1), ALU.mult, ALU.add,
                             [pd.r(), XO.r(j, s0, s1)], [XO.r(j, s0, s1)])
                self.dma("sp", dst.a[dc, :, t0:t0 + TB], XO.ap(j), [XO.r(j)], dst.r(dc, t0, t0 + TB))

    G_A = (0, 2)
    G_B = (2, 2)
    G_C = (4, 2)
    G_STAT = (6, 2)
    rope_eng = os.environ.get("ROPE_ENG", "dve")


def _rope_perm():
    return np.concatenate([np.arange(32, 64), np.arange(0, 32)])


def weight_units(c):
    ar = np.arange
    for pre in ("ffn1", "ffn2"):
        for f in range(c.FC):
            cols = f * 128 + ar(128)
            yield (pre + "_gu", f), [(pre + "_w_gate", cols), (pre + "_w_up", cols)]
        for dc in range(c.KC):
            yield (pre + "_d", dc), [(pre + "_w_down", dc * 128 + ar(128))]
    for j in range(c.CC):
        yield ("in_conv", j), [("w_in", j * 128 + ar(128)), ("w_in", c.CCH + j * 128 + ar(128))]
    o_q = 2 * c.CCH
    o_kv = o_q + c.QL
    o_pe = o_kv + c.KVL
    for i in range(c.QC):
        yield ("in_cq", i), [("w_in", o_q + i * 128 + ar(128))]
    for i in range(c.KVC):
        yield ("in_ckv", i), [("w_in", o_kv + i * 128 + ar(128))]
    yield ("in_kpe", 0), [("w_in", o_pe + ar(64))]
    yield ("in_kpe", 1), [("w_in", o_pe + _rope_perm())]
    for h in range(c.NH):
        yield ("qb", h), [("w_q_b", np.concatenate([h * 192 + ar(128), h * 192 + 128 + ar(64),
                                                     h * 192 + 128 + _rope_perm()]))]
        yield ("kvk", h), [("w_kv_b", h * 256 + ar(128))]
    yield ("kvv", 0), [("w_kv_b", np.concatenate([h * 256 + 128 + ar(128) for h in range(c.NH)]))]
    for dc in range(c.KC):
        cols = dc * 128 + ar(128)
        yield ("wout", dc), [("w_out", cols)]
        yield ("wcq", dc), [("w_cq", cols)]
        yield ("wck", dc), [("w_ck", cols)]
        yield ("wco", dc), [("w_co", cols)]
    for cb in range(c.D // 512):
        yield ("wcv", cb), [("w_cv", cb * 512 + ar(512))]


_KDIM = {"ffn1_w_gate": "D", "ffn1_w_up": "D", "ffn1_w_down": "F", "ffn2_w_gate": "D", "ffn2_w_up": "D",
         "ffn2_w_down": "F", "w_in": "D", "w_q_b": "QL", "w_kv_b": "KVL", "w_out": "D", "w_cq": "D",
         "w_ck": "D", "w_cv": "D", "w_co": "D"}


def weight_plan(c):
    plan = {}
    off = 0
    for l in range(c.DEPTH):
        for key, parts in weight_units(c):
            ln = sum(getattr(c, _KDIM[w]) // 128 * len(cols) for w, cols in parts)
            plan[(l,) + key] = (off, ln)
            off += 128 * ln
    return plan, off


def pack_weights(c, inputs):
    plan, total = weight_plan(c)
    flat = np.empty(total, np.float32)
    for l in range(c.DEPTH):
        for key, parts in weight_units(c):
            off, ln = plan[(l,) + key]
            blocks = []
            for w, cols in parts:
                W = np.asarray(inputs[w][l])
                K = W.shape[0]
                blocks.append(W[:, cols].reshape(K // 128, 128, len(cols)).transpose(1, 0, 2).reshape(128, -1))
            flat[off:off + 128 * ln] = np.concatenate(blocks, axis=1).reshape(-1)
    return flat


def vec_plan(c):
    cols = {}
    n = 0

    def add(name, k):
        nonlocal n
        cols[name] = n
        n += k

    for l in range(c.DEPTH):
        for nm in ("ffn1_norm", "mix_norm", "cross_norm", "mem_norm", "ffn2_norm"):
            add((l, nm), c.KC)
        add((l, "conv_w"), c.CC * 31)
        for nm in ("conv_b", "conv_ln_g", "conv_ln_b"):
            add((l, nm), c.CC)
        add((l, "q_a_norm"), c.QC)
        add((l, "kv_a_norm"), c.KVC)
        add((l, "q_norm"), 3)
        add((l, "k_norm"), 3)
        add((l, "cq_norm"), c.XHC)
        add((l, "ck_norm"), c.XHC)
    add("invf", 1)
    add("sgn", 1)
    add("sel", 4)
    add("visb", 3)
    return cols, n


def pack_vecs(c, inputs, rank):
    cols, n = vec_plan(c)
    V = np.zeros((128, n), np.float32)

    def put(name, arr2d):
        c0 = cols[name]
        for i, row in enumerate(arr2d):
            V[:len(row), c0 + i] = row

    for l in range(c.DEPTH):
        for nm in ("ffn1_norm", "mix_norm", "cross_norm", "mem_norm", "ffn2_norm", "conv_b", "conv_ln_g",
                   "conv_ln_b", "q_a_norm", "kv_a_norm", "cq_norm", "ck_norm"):
            put((l, nm), np.asarray(inputs[nm][l]).reshape(-1, 128))
        cw = np.asarray(inputs["conv_w"][l])
        put((l, "conv_w"), cw.reshape(31, c.CC, 128).transpose(1, 0, 2).reshape(c.CC * 31, 128))
        for nm in ("q_norm", "k_norm"):
            g = np.asarray(inputs[nm][l])
            put((l, nm), [g[:128], g[128:192], g[128:192][_rope_perm()]])
    inv = (1.0 / (np.float32(10000.0) ** (np.arange(0, 64, 2, dtype=np.float32) / np.float32(64)))).astype(np.float32)
    put("invf", [np.concatenate([inv, inv])])
    put("sgn", [np.concatenate([-np.ones(32, np.float32), np.ones(32, np.float32)])])
    sel = np.zeros((4, 128), np.float32)
    if rank > 0:
        sel[rank - 1] = 1.0
    put("sel", sel)
    vb = np.zeros((3, 128), np.float32)
    for r in range(3):
        if r >= rank:
            vb[r] = -30000.0
    put("visb", vb)
    return V


class Full(Builder):
    def __init__(self, nc, cfg):
        super().__init__(nc, cfg)
        c = cfg
        self.wplan, wtotal = weight_plan(c)
        self.vcols, nv = vec_plan(c)
        self.nv = nv
        dt = nc.dram_tensor
        self.xT = DT(dt("xT", [c.KC, 128, c.NTOK], F32, kind="ExternalInput").ap(), "xT")
        self.memT = dt("memT", [c.KC, 128, c.NMEM], F32, kind="ExternalInput").ap()
        self.pos = dt("pos", [1, c.NTOK], I32, kind="ExternalInput").ap()
        self.wflat = dt("wflat", [wtotal], F32, kind="ExternalInput").ap()
        self.vecs_in = dt("vecs", [128, nv], F32, kind="ExternalInput").ap()
        self.ident_in = dt("ident", [128, 128], F32, kind="ExternalInput").ap()
        self.outT = DT(dt("outT", [c.KC, 128, c.NTOK], F32, kind="ExternalOutput").ap(), "outT")
        self.XT = DT(dt("XT", [c.KC, 128, c.NTOK], F32).ap(), "XT")
        self.AT = dt("AT", [c.CC, 128, 32 + c.NTOK], BF16).ap()
        self.QT = dt("QT", [c.NH, 2, 128, c.NTOK], BF16).ap()
        self.rk = [192 + (c.R_T if h == 0 else 0) for h in range(c.NH)]
        self.GK = [dt(f"GK{h}", [self.rk[h], c.NTOK], BF16).ap() for h in range(c.NH)]
        self.GOK = [dt(f"GOK{h}", [4 * self.rk[h], c.NTOK], BF16).ap() for h in range(c.NH)]
        self.GV = [dt(f"GV{p}", [256, c.NTOK], BF16).ap() for p in range(c.NH // 2)]
        self.GOV = [dt(f"GOV{p}", [4 * 256, c.NTOK], BF16).ap() for p in range(c.NH // 2)]

    def W(self, l, *key):
        off, ln = self.wplan[(l,) + key]
        return self.wflat[off:off + 128 * ln].rearrange("(p l) -> p l", p=128)

    def vc(self, name, k=0):
        return self.vcols[name] + k

    def gk(self, src, h, part):
        t = self.GK[h] if src is None else self.GOK[h]
        r = (0 if src is None else src * self.rk[h]) + (0 if part == 0 else 128)
        return t[r:r + (128 if part == 0 else 64), :]

    def gv(self, src, h):
        t = self.GV[h // 2] if src is None else self.GOV[h // 2]
        r = (0 if src is None else src * 256) + (h % 2) * 128
        return t[r:r + 128, :].rearrange("r (a d) -> (r a) d", d=128)

    def gt(self, src):
        c = self.cfg
        t = self.GK[0] if src is None else self.GOK[0]
        r = (0 if src is None else src * self.rk[0]) + 192
        return t[r:r + c.R_T, :].rearrange("r (a t) -> (r a) t", t=32)

    def setup(self):
        c = self.cfg
        off = self.setup_consts(0, self.vecs_in, self.nv)
        off = (off + 31) // 32 * 32
        self.ident = self.view(off, 1, 128, BF16); off += 256
        self.dma("pool", self.ident.ap(0), self.ident_in, [], [self.ident.r(0)])
        self.COS = self.view(off, 1, c.NTOK, F32); off += self.COS.nbytes
        self.SINS = self.view(off, 1, c.NTOK, F32); off += self.SINS.nbytes
        self.work0 = off
        self.rope_tables()

    def rope_tables(self):
        c = self.cfg
        N = c.NTOK
        off = self.work0
        PI = self.view(off, 1, N, I32); off += PI.nbytes
        ANG = self.view(off, 1, N, F32); off += ANG.nbytes
        T1 = self.view(off, 1, N, F32); off += T1.nbytes
        KI = self.view(off, 1, N, I32); off += KI.nbytes
        KF = self.view(off, 1, N, F32); off += KF.nbytes
        R = self.view(off, 1, N, F32); off += R.nbytes
        M = self.view(off, 1, N, F32); off += M.nbytes
        p = 64
        a = lambda v: v.ap(0, 0, N, 0, p)
        self.dma("sp", a(PI), self.pos.partition_broadcast(p), [], [PI.r(0)])
        self.copy("dve", a(ANG), a(PI), [PI.r(0)], [ANG.r(0)])
        self.ts("dve", a(ANG), a(ANG), self.vcol(self.vc("invf"), 0, p), None, ALU.mult, None,
                [ANG.r(0), self.vecs.r(0)], [ANG.r(0)])
        TWO_PI = 2.0 * np.pi
        C1 = 6.28125
        C2 = TWO_PI - C1
        for which, dst in (("sin", self.SINS), ("cos", self.COS)):
            shift = 0.0 if which == "sin" else np.pi / 2
            self.ts("dve", a(T1), a(ANG), 1.0 / TWO_PI, 0.5 + shift / TWO_PI, ALU.mult, ALU.add, [ANG.r(0)], [T1.r(0)])
            self.copy("dve", a(KI), a(T1), [T1.r(0)], [KI.r(0)])
            self.copy("dve", a(KF), a(KI), [KI.r(0)], [KF.r(0)])
            self.stt(a(R), a(KF), -C1, a(ANG), ALU.mult, ALU.add, [KF.r(0), ANG.r(0)], [R.r(0)])
            self.stt(a(R), a(KF), -C2, a(R), ALU.mult, ALU.add, [KF.r(0), R.r(0)], [R.r(0)])
            if shift:
                self.ts("dve", a(R), a(R), float(shift), None, ALU.add, None, [R.r(0)], [R.r(0)])
            self.ts("dve", a(M), a(R), float(-np.pi), None, ALU.is_lt, None, [R.r(0)], [M.r(0)])
            self.stt(a(R), a(M), float(TWO_PI), a(R), ALU.mult, ALU.add, [M.r(0), R.r(0)], [R.r(0)])
            self.ts("dve", a(M), a(R), float(np.pi), None, ALU.is_gt, None, [R.r(0)], [M.r(0)])
            self.stt(a(R), a(M), float(-TWO_PI), a(R), ALU.mult, ALU.add, [M.r(0), R.r(0)], [R.r(0)])
            self.ts("dve", a(R), a(R), float(-3.1415925), float(3.1415925), ALU.max, ALU.min, [R.r(0)], [R.r(0)])
            self.act(a(dst), a(R), AF.Sin, [R.r(0)], [dst.r(0)])
        self.ts("dve", a(self.SINS), a(self.SINS), self.vcol(self.vc("sgn"), 0, p), None, ALU.mult, None,
                [self.SINS.r(0), self.vecs.r(0)], [self.SINS.r(0)])

    def norm_phase(self, src, t0, TB, H, X32, gcol0, tmp, WN=256, pieces=None):
        c = self.cfg
        for s in (range(TB // WN) if pieces is None else pieces):
            c0 = t0 + s * WN
            self.dma("sp", X32.t[:, :, 0:WN], src.a[:, :, c0:c0 + WN].rearrange("k p t -> p k t"),
                     src.r(0, c0, c0 + WN, n=c.KC), [X32.r(0, n=c.KC)])
            xs = [(X32.ap(k, 0, WN), X32.r(k, 0, WN), 128) for k in range(c.KC)]
            outs = [(H.ap(k, s * WN, s * WN + WN), H.r(k, s * WN, s * WN + WN)) for k in range(c.KC)]
            self.rmsnorm(xs, c.D, [gcol0 + k for k in range(c.KC)], outs, WN, tmp)

    def mktmp(self, off):
        tmp = {"sq": self.view(off, 4, 512, BF16), "sq_i": 0, "rs_i": 0}
        off += tmp["sq"].nbytes
        tmp["rs"] = self.view(off, 2, 512, F32)
        off += tmp["rs"].nbytes
        return tmp, off

    def proj_residual(self, src, dst, wkeys, ACTV, nk, t0, TB, XO, alpha, hook=None):
        c = self.cfg
        for dc in range(c.KC):
            if hook is not None:
                hook(dc)
            w = self.load_w(wkeys(dc), nk, 128)
            j = dc % 2
            self.dma("sp", XO.ap(j, 0, TB), src.a[dc, :, t0:t0 + TB], src.r(dc, t0, t0 + TB), [XO.r(j, 0, TB)])
            for sb in range(TB // 512):
                s0, s1 = sb * 512, sb * 512 + 512
                pd = self.bank(self.G_C)
                for f in range(nk):
                    self.mm(pd, 0, 512, w.ap(f), ACTV.ap(f, s0, s1), f == 0, f == nk - 1, [w.r(f), ACTV.r(f, s0, s1)])
                self.stt(XO.ap(j, s0, s1), pd.ap(), alpha, XO.ap(j, s0, s1), ALU.mult, ALU.add,
                         [pd.r(), XO.r(j, s0, s1)], [XO.r(j, s0, s1)])
            self.dma("sp", dst.a[dc, :, t0:t0 + TB], XO.ap(j, 0, TB), [XO.r(j, 0, TB)], dst.r(dc, t0, t0 + TB))

    def ffn(self, l, pre, src, dst):
        c = self.cfg
        KC, FC, TB = c.KC, c.FC, c.TB
        NS = TB // 512
        off = self.work0
        H = self.view(off, KC, TB, BF16); off += H.nbytes
        A = self.view(off, FC, TB, BF16); off += A.nbytes
        X32 = self.view(off, KC, 256, F32); off += X32.nbytes
        tmp, off = self.mktmp(off)
        SG = self.view(off, 2, 512, F32); off += SG.nbytes
        XO = self.view(off, 2, TB, F32); off += XO.nbytes
        self.ring = (off, SB_BYTES)
        self.ring_off = 0
        assert SB_BYTES - off >= 24 * 1024, (off, SB_BYTES)
        g0 = self.vc((l, pre + "_norm"))
        NTB = c.NTOK // TB
        npc = TB // 256
        for tb in range(NTB):
            t0 = tb * TB
            if tb == 0:
                self.norm_phase(src, t0, TB, H, X32, g0, tmp)
            for f in range(FC):
                w = self.load_w(self.W(l, pre + "_gu", f), 2 * KC, 128)
                for sb in range(NS):
                    s0, s1 = sb * 512, sb * 512 + 512
                    pg = self.bank(self.G_A)
                    pu = self.bank(self.G_B)
                    for k in range(KC):
                        self.mm(pg, 0, 512, w.ap(k), H.ap(k, s0, s1), k == 0, k == KC - 1, [w.r(k), H.r(k, s0, s1)])
                    for k in range(KC):
                        self.mm(pu, 0, 512, w.ap(KC + k), H.ap(k, s0, s1), k == 0, k == KC - 1, [w.r(KC + k), H.r(k, s0, s1)])
                    j = (f * NS + sb) % 2
                    self.act(SG.ap(j), pg.ap(), AF.Silu, [pg.r()], [SG.r(j)])
                    self.tt("dve", A.ap(f, s0, s1), SG.ap(j), pu.ap(), ALU.mult, [SG.r(j), pu.r()], [A.r(f, s0, s1)])
            hook = None
            if tb + 1 < NTB and src is not dst:
                pass
            if tb + 1 < NTB:
                def hook(dc, tb=tb):
                    if dc % (KC // npc) == 0 and dc // (KC // npc) < npc:
                        self.norm_phase(src, (tb + 1) * TB, TB, H, X32, g0, tmp, pieces=[dc // (KC // npc)])
            self.proj_residual(src, dst, lambda dc: self.W(l, pre + "_d", dc), A, FC, t0, TB, XO, 0.5, hook=hook)

    def rope(self, R1, R2, j, out_ap, out_r, tok0):
        p = 64
        r1, r2 = R1.ap(j, 0, 512, 0, p), R2.ap(j, 0, 512, 0, p)
        e = self.rope_eng
        self.tt(e, r1, r1, self.COS.ap(0, tok0, tok0 + 512, 0, p), ALU.mult, [R1.r(j), self.COS.r(0, tok0, tok0 + 512)], [R1.r(j)])
        self.tt(e, r2, r2, self.SINS.ap(0, tok0, tok0 + 512, 0, p), ALU.mult, [R2.r(j), self.SINS.r(0, tok0, tok0 + 512)], [R2.r(j)])
        self.tt(e, out_ap, r1, r2, ALU.add, [R1.r(j), R2.r(j)], [out_r])

    def mixA(self, l, src):
        c = self.cfg
        KC, TB, QC, KVC, NH, CC = c.KC, c.TB, c.QC, c.KVC, c.NH, c.CC
        NS = TB // 512
        off = self.work0
        H = self.view(off, KC, TB, BF16); off += H.nbytes
        xa = off
        X32 = self.view(xa, KC, 256, F32)
        CQ = self.view(xa, QC, TB, F32)
        CKV = self.view(xa + CQ.nbytes, KVC, TB, F32)
        off += max(X32.nbytes, CQ.nbytes + CKV.nbytes)
        CQN = self.view(off, QC, TB, BF16); off += CQN.nbytes
        CKVN = self.view(off, KVC, TB, BF16); off += CKVN.nbytes
        KPE = self.view(off, 2, TB, F32); off += KPE.nbytes
        tmp, off = self.mktmp(off)
        SIG = self.view(off, 2, 512, F32); off += SIG.nbytes
        AST = self.view(off, 2, TB, BF16); off += AST.nbytes
        NST = self.view(off, 2, TB, BF16); off += NST.nbytes
        RST = self.view(off, 2, TB, BF16); off += RST.nbytes
        R1 = self.view(off, 2, 512, F32); off += R1.nbytes
        R2 = self.view(off, 2, 512, F32); off += R2.nbytes
        VS = self.view(off, 2, NH * 128, BF16); off += VS.nbytes
        self.ring = (off, SB_BYTES)
        self.ring_off = 0
        assert SB_BYTES - off >= 24 * 1024, (off, SB_BYTES)
        vregs = [self.GV[p].rearrange("(h r) (a d) -> h (r a) d", h=2, d=128) for p in range(NH // 2)]
        tail = self.gt(None)
        cnt = 0
        for tb in range(c.NTOK // TB):
            t0 = tb * TB
            self.norm_phase(src, t0, TB, H, X32, self.vc((l, "mix_norm")), tmp)
            for (key, n, dstv) in (("in_cq", QC, CQ), ("in_ckv", KVC, CKV)):
                for i in range(n):
                    w = self.load_w(self.W(l, key, i), KC, 128)
                    for sb in range(NS):
                        s0, s1 = sb * 512, sb * 512 + 512
                        pb = self.bank(self.G_A)
                        for k in range(KC):
                            self.mm(pb, 0, 512, w.ap(k), H.ap(k, s0, s1), k == 0, k == KC - 1, [w.r(k), H.r(k, s0, s1)])
                        self.act(dstv.ap(i, s0, s1), pb.ap(), AF.Copy, [pb.r()], [dstv.r(i, s0, s1)])
            for i in range(2):
                w = self.load_w(self.W(l, "in_kpe", i), KC, 64)
                for sb in range(NS):
                    s0, s1 = sb * 512, sb * 512 + 512
                    pb = self.bank(self.G_B)
                    for k in range(KC):
                        self.mm(pb, 0, 512, w.ap(k), H.ap(k, s0, s1), k == 0, k == KC - 1, [w.r(k), H.r(k, s0, s1)], m=64)
                    self.act(KPE.ap(i, s0, s1, 0, 64), pb.ap(0, 512, 0, 64), AF.Copy, [pb.r()], [KPE.r(i, s0, s1)])
            for sb in range(NS):
                s0, s1 = sb * 512, sb * 512 + 512
                self.rmsnorm([(CQ.ap(i, s0, s1), CQ.r(i, s0, s1), 128) for i in range(QC)], c.QL,
                             [self.vc((l, "q_a_norm"), i) for i in range(QC)],
                             [(CQN.ap(i, s0, s1), CQN.r(i, s0, s1)) for i in range(QC)], 512, tmp)
                self.rmsnorm([(CKV.ap(i, s0, s1), CKV.r(i, s0, s1), 128) for i in range(KVC)], c.KVL,
                             [self.vc((l, "kv_a_norm"), i) for i in range(KVC)],
                             [(CKVN.ap(i, s0, s1), CKVN.r(i, s0, s1)) for i in range(KVC)], 512, tmp)
            cnt_box = [cnt]
            def conv_pair(j):
                w = self.load_w(self.W(l, "in_conv", j), 2 * KC, 128)
                jj = j % 2
                for sb in range(NS):
                    s0, s1 = sb * 512, sb * 512 + 512
                    pa, pg = self.bank((0, 2)), self.bank((2, 1))
                    for k in range(KC):
                        self.mm(pa, 0, 512, w.ap(k), H.ap(k, s0, s1), k == 0, k == KC - 1, [w.r(k), H.r(k, s0, s1)])
                    for k in range(KC):
                        self.mm(pg, 0, 512, w.ap(KC + k), H.ap(k, s0, s1), k == 0, k == KC - 1, [w.r(KC + k), H.r(k, s0, s1)])
                    js = cnt_box[0] % 2
                    cnt_box[0] += 1
                    self.act(SIG.ap(js), pg.ap(), AF.Sigmoid, [pg.r()], [SIG.r(js)])
                    self.tt("dve", AST.ap(jj, s0, s1), SIG.ap(js), pa.ap(), ALU.mult, [SIG.r(js), pa.r()], [AST.r(jj, s0, s1)])
                self.dma("sp", self.AT[j, :, 32 + t0:32 + t0 + TB], AST.ap(jj), [AST.r(jj)], [(f"AT:{j}", 32 + t0, 32 + t0 + TB)])
                if t0 + TB == c.NTOK:
                    self.dma("sp", tail[j * 128:(j + 1) * 128, :], AST.ap(jj, TB - 32, TB), [AST.r(jj)], [("GIT", j, j + 1)])
            def q_head(h):
                w = self.load_w(self.W(l, "qb", h), QC, 256)
                jj = h % 2
                for sb in range(NS):
                    s0, s1 = sb * 512, sb * 512 + 512
                    pn, pr, ps = self.bank((4, 2)), self.bank((6, 1)), self.bank((7, 1))
                    for k in range(QC):
                        self.mm(pn, 0, 512, w.ap(k, 0, 128), CQN.ap(k, s0, s1), k == 0, k == QC - 1, [w.r(k), CQN.r(k, s0, s1)])
                    for k in range(QC):
                        self.mm(pr, 0, 512, w.ap(k, 128, 192), CQN.ap(k, s0, s1), k == 0, k == QC - 1, [w.r(k), CQN.r(k, s0, s1)], m=64)
                    for k in range(QC):
                        self.mm(ps, 0, 512, w.ap(k, 192, 256), CQN.ap(k, s0, s1), k == 0, k == QC - 1, [w.r(k), CQN.r(k, s0, s1)], m=64)
                    jr = cnt_box[0] % 2
                    cnt_box[0] += 1
                    g = self.vc((l, "q_norm"))
                    self.rmsnorm([(pn.ap(), pn.r(), 128), (pr.ap(0, 512, 0, 64), pr.r(), 64), (ps.ap(0, 512, 0, 64), ps.r(), 64)],
                                 192, [g, g + 1, g + 2],
                                 [(NST.ap(jj, s0, s1), NST.r(jj, s0, s1)), (R1.ap(jr, 0, 512, 0, 64), R1.r(jr)),
                                  (R2.ap(jr, 0, 512, 0, 64), R2.r(jr))], 512, tmp, stat_n=2)
                    self.rope(R1, R2, jr, RST.ap(jj, s0, s1, 0, 64), RST.r(jj, s0, s1), t0 + s0)
                self.dma("sp", self.QT[h, 0, :, t0:t0 + TB], NST.ap(jj), [NST.r(jj)], [(f"QT:{h}", t0, t0 + TB)])
                self.dma("sp", self.QT[h, 1, 0:64, t0:t0 + TB], RST.ap(jj, 0, TB, 0, 64), [RST.r(jj)], [(f"QR:{h}", t0, t0 + TB)])
            def k_head(h):
                w = self.load_w(self.W(l, "kvk", h), KVC, 128)
                jj = h % 2
                for sb in range(NS):
                    s0, s1 = sb * 512, sb * 512 + 512
                    pn = self.bank((4, 2))
                    for k in range(KVC):
                        self.mm(pn, 0, 512, w.ap(k), CKVN.ap(k, s0, s1), k == 0, k == KVC - 1, [w.r(k), CKVN.r(k, s0, s1)])
                    jr = cnt_box[0] % 2
                    cnt_box[0] += 1
                    g = self.vc((l, "k_norm"))
                    self.rmsnorm([(pn.ap(), pn.r(), 128), (KPE.ap(0, s0, s1, 0, 64), KPE.r(0, s0, s1), 64),
                                  (KPE.ap(1, s0, s1, 0, 64), KPE.r(1, s0, s1), 64)],
                                 192, [g, g + 1, g + 2],
                                 [(NST.ap(jj, s0, s1), NST.r(jj, s0, s1)), (R1.ap(jr, 0, 512, 0, 64), R1.r(jr)),
                                  (R2.ap(jr, 0, 512, 0, 64), R2.r(jr))], 512, tmp, stat_n=2)
                    self.rope(R1, R2, jr, RST.ap(jj, s0, s1, 0, 64), RST.r(jj, s0, s1), t0 + s0)
                self.dma("sp", self.gk(None, h, 0)[:, t0:t0 + TB], NST.ap(jj), [NST.r(jj)], [(f"GIK:{h}", t0, t0 + TB)])
                self.dma("sp", self.gk(None, h, 1)[:, t0:t0 + TB], RST.ap(jj, 0, TB, 0, 64), [RST.r(jj)], [(f"GIR:{h}", t0, t0 + TB)])
            self.G_STAT = (3, 1)
            heads = [(q_head, h) for h in range(NH)] + [(k_head, h) for h in range(NH)]
            per = -(-len(heads) // CC) if MIXA_IL else 0
            for j in range(CC):
                conv_pair(j)
                for fn, h in heads[j * per:(j + 1) * per]:
                    fn(h)
            for fn, h in heads[CC * per:]:
                fn(h)
            self.G_STAT = (6, 2)
            cnt = cnt_box[0]
            wv = self.load_w(self.W(l, "kvv", 0), KVC, NH * 128)
            for tt_ in range(TB // 128):
                jj = tt_ % 2
                cbw = min(512, NH * 128)
                for cb in range(NH * 128 // cbw):
                    pv = self.bank(self.G_B)
                    for k in range(KVC):
                        self.mm(pv, 0, cbw, CKVN.ap(k, tt_ * 128, tt_ * 128 + 128), wv.ap(k, cb * cbw, cb * cbw + cbw),
                                k == 0, k == KVC - 1, [wv.r(k), CKVN.r(k, tt_ * 128, tt_ * 128 + 128)])
                    self.act(VS.ap(jj, cb * cbw, cb * cbw + cbw), pv.ap(0, cbw), AF.Copy, [pv.r(0, cbw)], [VS.r(jj, cb * cbw, cb * cbw + cbw)])
                tk = t0 + tt_ * 128
                for p in range(NH // 2):
                    self.dma("sp", vregs[p][:, tk:tk + 128, :].rearrange("h p d -> p h d"),
                             VS.t[:, jj, p * 256:(p + 1) * 256].rearrange("p (h d) -> p h d", d=128), [VS.r(jj)],
                             [(f"GIV:{2 * p}", tk, tk + 128), (f"GIV:{2 * p + 1}", tk, tk + 128)])

    def exchange(self):
        c = self.cfg
        N = c.NTOK
        rg = [[0, 1, 2, 3], [4, 5, 6, 7]]

        def ag(src, dst, reads, writes):
            self.P.add("pool", lambda h: h.collective_compute("AllGather", ALU.bypass, replica_groups=rg,
                                                             ins=[src.opt()], outs=[dst.opt()]), reads, writes, cc=True)

        for h in range(c.NH):
            reads = [(f"GIK:{h}", 0, N), (f"GIR:{h}", 0, N)] + ([("GIT", 0, c.CC)] if h == 0 else [])
            ag(self.GK[h], self.GOK[h], reads, [(f"GOK:{h}", 0, 1)])
            if h % 2 == 1:
                p = h // 2
                ag(self.GV[p], self.GOV[p], [(f"GIV:{2 * p}", 0, N), (f"GIV:{2 * p + 1}", 0, N)], [(f"GOV:{p}", 0, 1)])
        import os
        lvl = int(os.environ.get("EXLVL", "9"))
        if lvl < 1:
            return
        off = self.work0
        TL = self.view(off, 4, c.CC * 32, BF16); off += TL.nbytes
        HL = self.view(off, 1, c.CC * 32, F32); off += HL.nbytes
        HB = self.view(off, 1, c.CC * 32, BF16); off += HB.nbytes
        for r in range(4):
            self.dma("sp", TL.t[:, r, :].rearrange("p (j t) -> p j t", t=32),
                     self.gt(r).rearrange("(j p) t -> p j t", p=128), [("GOK:0", 0, 1)], [TL.r(r)])
        if lvl < 2:
            return
        s0 = self.vc("sel")
        self.ts("dve", HL.ap(0), TL.ap(0), self.vcol(s0), None, ALU.mult, None, [TL.r(0), self.vecs.r(0)], [HL.r(0)])
        for r in range(1, 4):
            self.stt(HL.ap(0), TL.ap(r), self.vcol(s0 + r), HL.ap(0), ALU.mult, ALU.add, [TL.r(r), HL.r(0), self.vecs.r(0)], [HL.r(0)])
        self.copy("dve", HB.ap(0), HL.ap(0), [HL.r(0)], [HB.r(0)])
        if lvl < 3:
            return
        self.dma("sp", self.AT[:, :, 0:32].rearrange("j p t -> p j t"), HB.t[:, 0, :].rearrange("p (j t) -> p j t", t=32),
                 [HB.r(0)], [(f"AT:{j}", 0, 32) for j in range(c.CC)])

    def mixB(self, l):
        c = self.cfg
        KC, CC, NH, N, NQB, NKB = c.KC, c.CC, c.NH, c.NTOK, c.NQB, c.NKB
        off = self.work0
        CCAT = self.view(off, KC, N, BF16); off += CCAT.nbytes
        base = off
        Y32 = self.view(off, CC, N, F32); off += Y32.nbytes
        AX = self.view(off, 2, 32 + N, BF16); off += (AX.nbytes + 31) // 32 * 32
        DG = self.view(off, 2 * 31, 128, BF16); off += DG.nbytes
        SQ = self.view(off, 4, 512, BF16); off += SQ.nbytes
        YB = self.view(off, 4, 512, BF16); off += YB.nbytes
        ST = self.view(off, 4, 512, F32); off += ST.nbytes
        assert off <= SB_BYTES, off
        cw0 = self.vc((l, "conv_w"))
        for j in range(CC):
            jj = j % 2
            self.dma("sp", AX.ap(jj), self.AT[j], [(f"AT:{j}", 0, 32 + N)], [AX.r(jj)])
            for u in range(31):
                self.ts("dve", DG.ap(jj * 31 + u), self.ident.ap(0), self.vcol(cw0 + j * 31 + u), None, ALU.mult, None,
                        [self.ident.r(0), self.vecs.r(0)], [DG.r(jj * 31 + u)])
            for qb in range(NQB):
                q0 = qb * 512
                pb = self.bank(self.G_A)
                for u in range(31):
                    self.mm(pb, 0, 512, DG.ap(jj * 31 + u), AX.ap(jj, q0 + 2 + u, q0 + 2 + u + 512), u == 0, u == 30,
                            [DG.r(jj * 31 + u), AX.r(jj, q0 + 2 + u, q0 + 2 + u + 512)])
                self.act(Y32.ap(j, q0, q0 + 512), pb.ap(), AF.Identity, [pb.r(), self.vecs.r(0)], [Y32.r(j, q0, q0 + 512)],
                         bias=self.vcol(self.vc((l, "conv_b"), j)))
        sqi = 0
        for qb in range(NQB):
            q0 = qb * 512
            bs, bq = self.bank(self.G_STAT), self.bank(self.G_STAT)
            for j in range(CC):
                s = sqi % 4
                sqi += 1
                self.act(SQ.ap(s), Y32.ap(j, q0, q0 + 512), AF.Square, [Y32.r(j, q0, q0 + 512)], [SQ.r(s)])
                self.copy("pool", YB.ap(s), Y32.ap(j, q0, q0 + 512), [Y32.r(j, q0, q0 + 512)], [YB.r(s)])
                self.mm(bs, 0, 512, self.ones.ap(0), YB.ap(s), j == 0, j == CC - 1, [self.ones.r(0), YB.r(s)])
                self.mm(bq, 0, 512, self.ones.ap(0), SQ.ap(s), j == 0, j == CC - 1, [self.ones.r(0), SQ.r(s)])
            inv = 1.0 / c.CCH
            self.ts("dve", ST.ap(0), bs.ap(), inv, None, ALU.mult, None, [bs.r()], [ST.r(0)])
            self.tt("dve", ST.ap(3), ST.ap(0), ST.ap(0), ALU.mult, [ST.r(0)], [ST.r(3)])
            self.stt(ST.ap(1), bq.ap(), inv, ST.ap(3), ALU.mult, ALU.subtract, [bq.r(), ST.r(3)], [ST.r(1)])
            self.act(ST.ap(1), ST.ap(1), AF.Sqrt, [ST.r(1), self.epsc.r(0)], [ST.r(1)], bias=self.epsc.ap(0, 0, 1))
            self.recip(ST.ap(2), ST.ap(1), [ST.r(1)], [ST.r(2)])
            for j in range(CC):
                y = Y32.ap(j, q0, q0 + 512)
                yr = Y32.r(j, q0, q0 + 512)
                self.tt("dve", y, y, ST.ap(0), ALU.subtract, [yr, ST.r(0)], [yr])
                self.tt("dve", y, y, ST.ap(2), ALU.mult, [yr, ST.r(2)], [yr])
                self.act(CCAT.ap(j, q0, q0 + 512), y, AF.Silu, [yr, self.vecs.r(0)], [CCAT.r(j, q0, q0 + 512)],
                         scale=self.vcol(self.vc((l, "conv_ln_g"), j)), bias=self.vcol(self.vc((l, "conv_ln_b"), j)))
        off = base
        QHN = self.view(off, 2, N, BF16); off += QHN.nbytes
        QHR = self.view(off, 2, N, BF16); off += QHR.nbytes
        KN = self.view(off, 2, N, BF16); off += KN.nbytes
        KR = self.view(off, 2, N, BF16); off += KR.nbytes
        VV = self.view(off, 2, N, BF16); off += VV.nbytes
        PT = self.view(off, 4, 512, BF16); off += PT.nbytes
        OA = self.view(off, NQB, 512, F32); off += OA.nbytes
        PA = self.view(off, 2 * NQB, 512, F32); off += PA.nbytes
        ONEF = self.view(off, 1, 128, F32); off += ONEF.nbytes
        self.memset("dve", ONEF.ap(0), 1.0, [ONEF.r(0)])
        RD = self.view(off, 2, 512, F32); off += RD.nbytes
        XO = self.view(off, 2, N, F32); off += XO.nbytes
        self.ring = (off, SB_BYTES)
        self.ring_off = 0
        assert SB_BYTES - off >= 16 * 1024, (off, SB_BYTES)
        scale = 192 ** -0.5
        pti = 0
        srcs = [0, 1, 2, None]
        for h in range(NH):
            jq = h % 2
            self.dma("sp", QHN.ap(jq), self.QT[h, 0], [(f"QT:{h}", 0, N)], [QHN.r(jq)])
            self.dma("sp", QHR.ap(jq, 0, N, 0, 64), self.QT[h, 1, 0:64, :], [(f"QR:{h}", 0, N)], [QHR.r(jq)])
            for si, rk in enumerate(srcs):
                jk = (h * len(srcs) + si) % 2
                own = rk is None
                self.dma("sp", KN.ap(jk), self.gk(rk, h, 0), [(f"GIK:{h}", 0, N)] if own else [(f"GOK:{h}", 0, 1)], [KN.r(jk)])
                self.dma("sp", KR.ap(jk, 0, N, 0, 64), self.gk(rk, h, 1), [(f"GIR:{h}", 0, N)] if own else [(f"GOK:{h}", 0, 1)], [KR.r(jk)])
                self.dma("sp", VV.t[:, jk, :].rearrange("p (j d) -> p j d", d=128),
                         self.gv(rk, h).rearrange("(j p) d -> p j d", p=128),
                         [(f"GIV:{h}", 0, N)] if own else [(f"GOV:{h // 2}", 0, 1)], [VV.r(jk)])
                for qb in range(NQB):
                    q0 = qb * 512
                    bo = self.bank(self.G_B)
                    bd_box = [None]
                    items = []
                    for kb in range(NKB if not own else 4 * qb + 4):
                        c0 = 128 * (kb - 4 * qb) if (own and kb >= 4 * qb) else 0
                        items.append((kb, c0, own and kb >= 4 * qb))

                    def emit_s(kb, c0):
                        bsb = self.bank(self.G_A)
                        k0 = kb * 128
                        self.mm(bsb, c0, 512, KN.ap(jk, k0, k0 + 128), QHN.ap(jq, q0 + c0, q0 + 512), True, False,
                                [KN.r(jk, k0, k0 + 128), QHN.r(jq, q0 + c0, q0 + 512)])
                        self.mm(bsb, c0, 512, KR.ap(jk, k0, k0 + 128, 0, 64), QHR.ap(jq, q0 + c0, q0 + 512, 0, 64), False, True,
                                [KR.r(jk, k0, k0 + 128), QHR.r(jq, q0 + c0, q0 + 512)])
                        return bsb

                    nxt = emit_s(items[0][0], items[0][1])
                    for i, (kb, c0, diag) in enumerate(items):
                        bsb = nxt
                        if i + 1 < len(items):
                            nxt = emit_s(items[i + 1][0], items[i + 1][1])
                        p = pti % 4
                        pti += 1
                        if own:
                            self.act(PT.ap(p, c0, 512), bsb.ap(c0, 512), AF.Exp, [bsb.r(c0, 512)], [PT.r(p, c0, 512)], scale=scale)
                        else:
                            self.act(PT.ap(p, c0, 512), bsb.ap(c0, 512), AF.Exp, [bsb.r(c0, 512), self.vecs.r(0)], [PT.r(p, c0, 512)],
                                     scale=scale, bias=self.vcol(self.vc("visb") + rk))
                        if diag:
                            self.memset("dve", PT.ap(p, c0, c0 + 64, 64, 128), 0.0, [PT.r(p, c0, c0 + 64)])
                        first, last = i == 0, i == len(items) - 1
                        self.mm(bo, c0, 512, VV.ap(jk, kb * 128, kb * 128 + 128), PT.ap(p, c0, 512), first, last,
                                [VV.r(jk, kb * 128, kb * 128 + 128), PT.r(p, c0, 512)])
                        if DEN_MODE == "pe":
                            if first:
                                bd_box[0] = self.bank(self.G_C)
                            self.mm(bd_box[0], c0, 512, self.ones.ap(0), PT.ap(p, c0, 512), first, last, [self.ones.r(0), PT.r(p, c0, 512)])
                            if last:
                                pa = qb
                                if si == 0:
                                    self.copy("dve", PA.ap(pa), bd_box[0].ap(), [bd_box[0].r()], [PA.r(pa)])
                                else:
                                    self.tt("dve", PA.ap(pa), PA.ap(pa), bd_box[0].ap(), ALU.add, [PA.r(pa), bd_box[0].r()], [PA.r(pa)])
                            continue
                        e = i % 2
                        eng = "dve" if (e == 0 or DEN_MODE == "dve") else "pool"
                        pa = e * NQB + qb
                        if si == 0 and i < 2:
                            self.copy(eng, PA.ap(pa), PT.ap(p), [PT.r(p)], [PA.r(pa)])
                        else:
                            self.tt(eng, PA.ap(pa, c0, 512), PA.ap(pa, c0, 512), PT.ap(p, c0, 512), ALU.add,
                                    [PA.r(pa, c0, 512), PT.r(p, c0, 512)], [PA.r(pa, c0, 512)])
                    if si == 0:
                        self.act(OA.ap(qb), bo.ap(), AF.Copy, [bo.r()], [OA.r(qb)])
                    else:
                        self.tt("dve", OA.ap(qb), OA.ap(qb), bo.ap(), ALU.add, [OA.r(qb), bo.r()], [OA.r(qb)])
            for qb in range(NQB):
                q0 = qb * 512
                jr = qb % 2
                if DEN_MODE == "pe":
                    self.recip(RD.ap(jr), PA.ap(qb), [PA.r(qb)], [RD.r(jr)])
                else:
                    self.tt("dve", PA.ap(qb), PA.ap(qb), PA.ap(NQB + qb), ALU.add, [PA.r(qb), PA.r(NQB + qb)], [PA.r(qb)])
                    bd = self.bank(self.G_C)
                    self.mm(bd, 0, 512, ONEF.ap(0), PA.ap(qb), True, True, [ONEF.r(0), PA.r(qb)])
                    self.recip(RD.ap(jr), bd.ap(), [bd.r()], [RD.r(jr)])
                self.tt("dve", CCAT.ap(CC + h, q0, q0 + 512), OA.ap(qb), RD.ap(jr), ALU.mult, [OA.r(qb), RD.r(jr)],
                        [CCAT.r(CC + h, q0, q0 + 512)])
        import os
        if os.environ.get("DBG") == "ccat":
            for k in range(KC):
                self.dma("pool", self.outT.a[k], CCAT.ap(k), [CCAT.r(k)], self.outT.r(k, 0, N))
            self.dbg_done = True
            return
        self.proj_residual(self.XT, self.XT, lambda dc: self.W(l, "wout", dc), CCAT, KC, 0, N, XO, 1.0)

    def cross(self, l):
        c = self.cfg
        KC, TB, XH, XHC, NM = c.KC, c.TB, c.XH, c.XHC, c.NMEM
        NS = TB // 512
        MT = NM // 128
        off = self.work0
        KCN = self.view(off, KC, NM, BF16); off += KCN.nbytes
        VC = self.view(off, MT, c.D, BF16); off += VC.nbytes
        tmp, off = self.mktmp(off)
        base = off
        M32 = self.view(off, KC, NM, F32); off += M32.nbytes
        MN = self.view(off, KC, NM, BF16); off += MN.nbytes
        K32 = self.view(off, KC, NM, F32); off += K32.nbytes
        self.ring = (off, SB_BYTES)
        self.ring_off = 0
        self.dma("sp", M32.t, self.memT.rearrange("k p t -> p k t"), [], [M32.r(0, n=KC)])
        self.rmsnorm([(M32.ap(k), M32.r(k), 128) for k in range(KC)], c.D, [self.vc((l, "mem_norm"), k) for k in range(KC)],
                     [(MN.ap(k), MN.r(k)) for k in range(KC)], NM, tmp)
        for oc in range(KC):
            w = self.load_w(self.W(l, "wck", oc), KC, 128)
            pb = self.bank(self.G_A)
            for k in range(KC):
                self.mm(pb, 0, NM, w.ap(k), MN.ap(k), k == 0, k == KC - 1, [w.r(k), MN.r(k)])
            self.act(K32.ap(oc), pb.ap(0, NM), AF.Copy, [pb.r(0, NM)], [K32.r(oc)])
        for hh in range(XH):
            ch = [hh * XHC + i for i in range(XHC)]
            self.rmsnorm([(K32.ap(k), K32.r(k), 128) for k in ch], c.XHD, [self.vc((l, "ck_norm"), i) for i in range(XHC)],
                         [(KCN.ap(k), KCN.r(k)) for k in ch], NM, tmp)
        for cb in range(c.D // 512):
            w = self.load_w(self.W(l, "wcv", cb), KC, 512)
            for mt in range(MT):
                pb = self.bank(self.G_B)
                for k in range(KC):
                    self.mm(pb, 0, 512, MN.ap(k, mt * 128, mt * 128 + 128), w.ap(k), k == 0, k == KC - 1,
                            [w.r(k), MN.r(k, mt * 128, mt * 128 + 128)])
                self.act(VC.ap(mt, cb * 512, cb * 512 + 512), pb.ap(), AF.Copy, [pb.r()], [VC.r(mt, cb * 512, cb * 512 + 512)])
        off = base
        H = self.view(off, KC, TB, BF16); off += H.nbytes
        OC = self.view(off, KC, TB, BF16); off += OC.nbytes
        X32 = self.view(off, KC, 256, F32); off += X32.nbytes
        Q32 = self.view(off, 2 * XHC, TB, F32); off += Q32.nbytes
        QN = self.view(off, XHC, TB, BF16); off += QN.nbytes
        PT = self.view(off, 2 * MT, 512, BF16); off += PT.nbytes
        RD = self.view(off, 2, 512, F32); off += RD.nbytes
        XO = self.view(off, 2, TB, F32); off += XO.nbytes
        self.ring = (off, SB_BYTES)
        self.ring_off = 0
        assert SB_BYTES - off >= 16 * 1024, (off, SB_BYTES)
        scale = c.XHD ** -0.5
        G_S, G_O, G_D, G_P = (0, 2), (2, 2), (4, 1), (6, 2)
        pti = [0]
        for tb in range(c.NTOK // TB):
            t0 = tb * TB
            self.norm_phase(self.XT, t0, TB, H, X32, self.vc((l, "cross_norm")), tmp)

            def proj(hh):
                par = (hh % 2) * XHC
                for i in range(XHC):
                    w = self.load_w(self.W(l, "wcq", hh * XHC + i), KC, 128)
                    for sb in range(NS):
                        s0, s1 = sb * 512, sb * 512 + 512
                        pb = self.bank(G_P)
                        for k in range(KC):
                            self.mm(pb, 0, 512, w.ap(k), H.ap(k, s0, s1), k == 0, k == KC - 1, [w.r(k), H.r(k, s0, s1)])
                        self.act(Q32.ap(par + i, s0, s1), pb.ap(), AF.Copy, [pb.r()], [Q32.r(par + i, s0, s1)])

            def attn(hh):
                par = (hh % 2) * XHC
                self.G_STAT = (5, 1)
                for sb in range(NS):
                    s0, s1 = sb * 512, sb * 512 + 512
                    self.rmsnorm([(Q32.ap(par + i, s0, s1), Q32.r(par + i, s0, s1), 128) for i in range(XHC)], c.XHD,
                                 [self.vc((l, "cq_norm"), i) for i in range(XHC)],
                                 [(QN.ap(i, s0, s1), QN.r(i, s0, s1)) for i in range(XHC)], 512, tmp)
                self.G_STAT = (6, 2)
                pts = {}
                for sb in range(NS):
                    s0, s1 = sb * 512, sb * 512 + 512
                    for mt in range(MT):
                        bsb = self.bank(G_S)
                        for i in range(XHC):
                            kc = hh * XHC + i
                            self.mm(bsb, 0, 512, KCN.ap(kc, mt * 128, mt * 128 + 128), QN.ap(i, s0, s1), i == 0, i == XHC - 1,
                                    [KCN.r(kc, mt * 128, mt * 128 + 128), QN.r(i, s0, s1)])
                        p = pti[0] % (2 * MT)
                        pti[0] += 1
                        self.act(PT.ap(p), bsb.ap(), AF.Exp, [bsb.r()], [PT.r(p)], scale=scale)
                        pts[(sb, mt)] = p
                for sb in range(NS):
                    bd = self.bank(G_D)
                    for mt in range(MT):
                        self.mm(bd, 0, 512, self.ones.ap(0), PT.ap(pts[(sb, mt)]), mt == 0, mt == MT - 1,
                                [self.ones.r(0), PT.r(pts[(sb, mt)])])
                    self.recip(RD.ap(sb % 2), bd.ap(), [bd.r()], [RD.r(sb % 2)])
                for sb in range(NS):
                    s0, s1 = sb * 512, sb * 512 + 512
                    for dv in range(XHC):
                        oc = hh * XHC + dv
                        bo = self.bank(G_O)
                        for mt in range(MT):
                            self.mm(bo, 0, 512, VC.ap(mt, oc * 128, oc * 128 + 128), PT.ap(pts[(sb, mt)]), mt == 0, mt == MT - 1,
                                    [VC.r(mt, oc * 128, oc * 128 + 128), PT.r(pts[(sb, mt)])])
                        self.tt("dve", OC.ap(oc, s0, s1), bo.ap(), RD.ap(sb % 2), ALU.mult, [bo.r(), RD.r(sb % 2)], [OC.r(oc, s0, s1)])

            proj(0)
            for hh in range(XH):
                if hh + 1 < XH:
                    proj(hh + 1)
                attn(hh)
            self.proj_residual(self.XT, self.XT, lambda dc: self.W(l, "wco", dc), OC, KC, t0, TB, XO, 1.0)

    def build(self, upto=None):
        c = self.cfg
        with self.st:
            self.setup()
            stages = []
            for l in range(c.DEPTH):
                last = l == c.DEPTH - 1
                stages += [lambda l=l: self.ffn(l, "ffn1", self.xT if l == 0 else self.XT, self.XT),
                           lambda l=l: self.mixA(l, self.XT),
                           lambda l=l: self.exchange(),
                           lambda l=l: self.mixB(l),
                           lambda l=l: self.cross(l),
                           lambda l=l, last=last: self.ffn(l, "ffn2", self.XT, self.outT if last else self.XT)]
            if upto is not None:
                stages = stages[:upto]
            for s in stages:
                s()
            import os
            dbg = os.environ.get("DBG", "")
            if dbg in ("q", "k"):
                flat = self.outT.a.rearrange("k p t -> (k p) t")
                for h in range(c.NH):
                    if dbg == "q":
                        srcs = [(self.QT[h, 0], f"QT:{h}"), (self.QT[h, 1, 0:64, :], f"QR:{h}")]
                    else:
                        srcs = [(self.gk(None, h, 0), f"GIK:{h}"), (self.gk(None, h, 1), f"GIR:{h}")]
                    self.dma("pool", flat[h * 192:h * 192 + 128, :], srcs[0][0], [(srcs[0][1], 0, c.NTOK)], [("dbgout", 2 * h, 2 * h + 1)])
                    self.dma("pool", flat[h * 192 + 128:h * 192 + 192, :], srcs[1][0], [(srcs[1][1], 0, c.NTOK)], [("dbgout", 2 * h + 1, 2 * h + 2)])
                self.dbg_done = True
            if upto is not None and upto != c.DEPTH * 6 and not getattr(self, "dbg_done", False):
                for k in range(c.KC):
                    self.dma("sp", self.outT.a[k], self.XT.a[k], self.XT.r(k, 0, c.NTOK), self.outT.r(k, 0, c.NTOK))
            self.P.emit()


def prepare_inputs(c, inputs):
    x = np.asarray(inputs["x"])
    mem = np.asarray(inputs["mem"])
    pos = np.asarray(inputs["positions"])
    B, S, D = x.shape
    ncore = B * S // c.NTOK
    per_b = S // c.NTOK
    wflat = pack_weights(c, inputs)
    ident = np.eye(128, dtype=np.float32)
    maps = []
    for core in range(ncore):
        b, r = divmod(core, per_b)
        xs = x[b, r * c.NTOK:(r + 1) * c.NTOK, :]
        maps.append({
            "xT": np.ascontiguousarray(xs.T).reshape(c.KC, 128, c.NTOK),
            "memT": np.ascontiguousarray(mem[b].T).reshape(c.KC, 128, c.NMEM),
            "pos": np.ascontiguousarray(pos[b, r * c.NTOK:(r + 1) * c.NTOK]).reshape(1, c.NTOK).astype(np.int32),
            "wflat": wflat,
            "vecs": pack_vecs(c, inputs, r),
            "ident": ident,
        })
    return maps


def assemble_output(c, results, B, S):
    per_b = S // c.NTOK
    out = np.empty((B, S, c.D), np.float32)
    for core, r in enumerate(results):
        b, rk = divmod(core, per_b)
        out[b, rk * c.NTOK:(rk + 1) * c.NTOK, :] = np.asarray(r["outT"]).reshape(c.D, c.NTOK).T
    return out


_CACHE = {}


def kernel(**inputs):
    c = Cfg()
    if "nc" not in _CACHE:
        nc = bass.Bass("TRN2", target_bir_lowering=False)
        Full(nc, c).build()
        _CACHE["nc"] = nc
    nc = _CACHE["nc"]
    maps = prepare_inputs(c, inputs)
    B, S, _ = np.asarray(inputs["x"]).shape
    res = run_bass_kernel_spmd(nc, maps, core_ids=list(range(len(maps))))
    return assemble_output(c, res.results, B, S)
```
